# Optimizing a Trainium2 kernel written in Bass

```python
import math
import jax, jax.numpy as jnp
from jax import lax
import numpy as np

D_MODEL = 4096
BATCH = 4
SEQ = 4096
DEPTH = 1

CHUNK = 64
Q_BLOCK = 128
N_MEM = 256
EPS = 1e-6

H_A = 16
DH_A = 128
DV_A = 128
Q_LORA = 1536
KV_LORA = 512
H_IDX = 16
D_IDX = 128
TOPK_MAX = 256

H_B = 16
DK_B = 128
DV_B = 128
CONV_WIDTH = 4

D_MIX = H_A * DV_A + H_B * DV_B
IN_SIZES = (Q_LORA, KV_LORA, D_IDX, H_IDX,
            2 * H_B * DK_B + H_B * DV_B, H_B * DV_B,
            H_B, H_B)
N_IN = sum(IN_SIZES)

H_X = 4
DH_X = 128

D_FF = ((8 * D_MODEL + 3 * 256 - 1) // (3 * 256)) * 256

kernel_name = "hybrid_dsa_gdn_stream_block"


def rmsnorm(x, g):
    xf = x.astype(jnp.float32)
    y = xf * lax.rsqrt(jnp.mean(xf * xf, axis=-1, keepdims=True) + EPS)
    return (y * g.astype(jnp.float32)).astype(x.dtype)


def l2norm(x):
    xf = x.astype(jnp.float32)
    return xf * lax.rsqrt(jnp.sum(xf * xf, axis=-1, keepdims=True) + EPS)


def causal_depthwise_conv(x, w):
    width, ch = w.shape
    return lax.conv_general_dilated(x, w[:, None, :], window_strides=(1,), padding=[(width - 1, 0)],
                                    dimension_numbers=('NWC', 'WIO', 'NWC'), feature_group_count=ch)


def sparse_mla_attention(qa, ckv, k_idx, w_idx, g_qa, w_qb, g_kv, w_uk, w_uv, w_iq):
    B, S, _ = qa.shape
    topk = min(TOPK_MAX, S // 4)
    qa = rmsnorm(qa, g_qa)
    ckv = rmsnorm(ckv, g_kv)
    q = (qa @ w_qb).reshape(B, S, H_A, DH_A)
    q_idx = (qa @ w_iq).reshape(B, S, H_IDX, D_IDX)
    w_idx = w_idx * (H_IDX ** -0.5)
    key_chunk = jnp.arange(S) // CHUNK
    nb = S // Q_BLOCK
    gather = jax.vmap(lambda table, idx: table[idx])

    def blockify(t):
        return t.reshape(B, nb, Q_BLOCK, *t.shape[2:]).swapaxes(0, 1)

    def one_block(args):
        q_b, q_idx_b, w_b, q_pos = args
        q_chunk = q_pos // CHUNK
        admissible = key_chunk[None, :] <= q_chunk[:, None]
        rel = jax.nn.relu(jnp.einsum('bqhd,bsd->bqhs', q_idx_b, k_idx) * (D_IDX ** -0.5))
        score = jnp.einsum('bqhs,bqh->bqs', rel, w_b).astype(jnp.float32)
        score = jnp.where(admissible[None], score, -jnp.inf)
        _, idx = lax.top_k(score, topk)
        valid = key_chunk[idx] <= q_chunk[None, :, None]
        c_sel = gather(ckv, idx)
        q_lat = jnp.einsum('bqhd,hcd->bqhc', q_b, w_uk)
        logits = jnp.einsum('bqhc,bqkc->bqhk', q_lat, c_sel).astype(jnp.float32) * (DH_A ** -0.5)
        logits = jnp.where(valid[:, :, None, :], logits, -jnp.inf)
        p = jax.nn.softmax(logits, axis=-1).astype(ckv.dtype)
        o_lat = jnp.einsum('bqhk,bqkc->bqhc', p, c_sel)
        return jnp.einsum('bqhc,hce->bqhe', o_lat, w_uv)

    pos = jnp.arange(S).reshape(nb, Q_BLOCK)
    out = lax.map(one_block, (blockify(q), blockify(q_idx), blockify(w_idx), pos))
    return out.swapaxes(0, 1).reshape(B, S, H_A * DV_A)


def chunk_gated_delta_rule(q, k, v, g, beta):
    B, S, H, DK = q.shape
    DV = v.shape[-1]
    N = S // CHUNK

    def to_chunks(t):
        return jnp.moveaxis(t.reshape(B, N, CHUNK, H, *t.shape[3:]), 3, 1)

    q, k, v, g, beta = (to_chunks(t) for t in (q, k, v, g, beta))
    G = jnp.cumsum(g, axis=-1)
    incl = jnp.tril(jnp.ones((CHUNK, CHUNK), bool))
    strict = jnp.tril(jnp.ones((CHUNK, CHUNK), bool), -1)
    decay = jnp.exp(jnp.where(incl, G[..., :, None] - G[..., None, :], -jnp.inf))
    kb = k * beta[..., None]
    lower = jnp.where(strict, jnp.einsum('bhnid,bhnjd->bhnij', kb, k) * decay, 0.0)
    tmat = lower + jnp.eye(CHUNK, dtype=lower.dtype)
    u = lax.linalg.triangular_solve(tmat, v * beta[..., None], left_side=True, lower=True, unit_diagonal=True)
    w = lax.linalg.triangular_solve(tmat, kb * jnp.exp(G)[..., None], left_side=True, lower=True, unit_diagonal=True)
    intra = jnp.where(incl, jnp.einsum('bhnid,bhnjd->bhnij', q, k) * decay, 0.0)
    q_dec = q * jnp.exp(G)[..., None]
    k_dec = k * jnp.exp(G[..., -1:] - G)[..., None]
    chunk_decay = jnp.exp(G[..., -1])

    def step(state, xs):
        q_i, k_i, u_i, w_i, a_i, d_i = xs
        v_new = u_i - jnp.einsum('bhcd,bhde->bhce', w_i, state)
        o_i = jnp.einsum('bhcd,bhde->bhce', q_i, state) + jnp.einsum('bhij,bhje->bhie', a_i, v_new)
        state = state * d_i[..., None, None] + jnp.einsum('bhcd,bhce->bhde', k_i, v_new)
        return state, o_i

    xs = tuple(jnp.moveaxis(t, 2, 0) for t in (q_dec, k_dec, u, w, intra, chunk_decay))
    state0 = jnp.zeros((B, H, DK, DV), jnp.float32)
    _, o = lax.scan(step, state0, xs)
    o = jnp.moveaxis(o, 0, 2)
    return jnp.moveaxis(o, 1, 3).reshape(B, S, H, DV)


def gated_deltanet(qkv, z, b, a, conv_w, a_log, dt_bias, g_out):
    B, S, _ = qkv.shape
    qkv = jax.nn.silu(causal_depthwise_conv(qkv, conv_w))
    q, k, v = jnp.split(qkv, [H_B * DK_B, 2 * H_B * DK_B], axis=-1)
    q = l2norm(q.reshape(B, S, H_B, DK_B)) * (DK_B ** -0.5)
    k = l2norm(k.reshape(B, S, H_B, DK_B))
    v = v.reshape(B, S, H_B, DV_B).astype(jnp.float32)
    beta = jax.nn.sigmoid(b.astype(jnp.float32))
    g = -jnp.exp(a_log.astype(jnp.float32)) * jax.nn.softplus(a.astype(jnp.float32) + dt_bias.astype(jnp.float32))
    o = chunk_gated_delta_rule(q, k, v, g, beta)
    o = rmsnorm(o, g_out) * jax.nn.silu(z.reshape(B, S, H_B, DV_B).astype(jnp.float32))
    return o.reshape(B, S, H_B * DV_B).astype(qkv.dtype)


def memory_cross_attention(h, mem, w_cq, w_ckv, w_co):
    B, S, _ = h.shape
    M = mem.shape[1]
    q = (h @ w_cq).reshape(B, S, H_X, DH_X)
    k, v = jnp.split(mem @ w_ckv, 2, axis=-1)
    k = k.reshape(B, M, H_X, DH_X)
    v = v.reshape(B, M, H_X, DH_X)
    logits = jnp.einsum('bshd,bmhd->bhsm', q, k).astype(jnp.float32) * (DH_X ** -0.5)
    p = jax.nn.softmax(logits, axis=-1).astype(h.dtype)
    o = jnp.einsum('bhsm,bmhd->bshd', p, v).reshape(B, S, H_X * DH_X)
    return o @ w_co


def swiglu(h, w_in, w_out):
    gate, up = jnp.split(h @ w_in, 2, axis=-1)
    return (jax.nn.silu(gate) * up) @ w_out


def setup_inputs(seed: int = 0) -> dict:
    key = jax.random.key(seed)
    ks = iter(jax.random.split(key, 40))
    f32 = jnp.float32
    L = DEPTH

    def w(shape, fan_in):
        return jax.random.normal(next(ks), shape, f32) * (fan_in ** -0.5)

    def gain(shape):
        return 1.0 + 0.02 * jax.random.normal(next(ks), shape, f32)

    x = jax.random.normal(next(ks), (BATCH, SEQ, D_MODEL), f32)
    mem = jax.random.normal(next(ks), (BATCH, N_MEM, D_MODEL), f32)
    a_log = jnp.log(jax.random.uniform(next(ks), (L, H_B), f32, 1.0, 16.0))
    dt = jnp.exp(jax.random.uniform(next(ks), (L, H_B), f32, math.log(1e-3), math.log(1e-1)))
    dt_bias = dt + jnp.log(-jnp.expm1(-dt))
    return {
        "x": x,
        "mem": mem,
        "attn_norm_g": gain((L, D_MODEL)),
        "w_in": w((L, D_MODEL, N_IN), D_MODEL),
        "qa_norm_g": gain((L, Q_LORA)),
        "w_qb": w((L, Q_LORA, H_A * DH_A), Q_LORA),
        "kv_norm_g": gain((L, KV_LORA)),
        "w_uk": w((L, H_A, KV_LORA, DH_A), KV_LORA),
        "w_uv": w((L, H_A, KV_LORA, DV_A), KV_LORA),
        "w_iq": w((L, Q_LORA, H_IDX * D_IDX), Q_LORA),
        "conv_w": w((L, CONV_WIDTH, 2 * H_B * DK_B + H_B * DV_B), CONV_WIDTH),
        "a_log": a_log,
        "dt_bias": dt_bias,
        "delta_norm_g": gain((L, DV_B)),
        "w_o": w((L, D_MIX, D_MODEL), D_MIX),
        "cross_norm_g": gain((L, D_MODEL)),
        "mem_norm_g": gain((L, D_MODEL)),
        "w_cq": w((L, D_MODEL, H_X * DH_X), D_MODEL),
        "w_ckv": w((L, D_MODEL, 2 * H_X * DH_X), D_MODEL),
        "w_co": w((L, H_X * DH_X, D_MODEL), H_X * DH_X),
        "ffn_norm_g": gain((L, D_MODEL)),
        "w_ffn_in": w((L, D_MODEL, 2 * D_FF), D_MODEL),
        "w_ffn_out": w((L, D_FF, D_MODEL), D_FF),
        "final_norm_g": gain((D_MODEL,)),
    }


def reference(x, mem, attn_norm_g, w_in, qa_norm_g, w_qb, kv_norm_g, w_uk, w_uv, w_iq, conv_w, a_log, dt_bias,
              delta_norm_g, w_o, cross_norm_g, mem_norm_g, w_cq, w_ckv, w_co, ffn_norm_g, w_ffn_in, w_ffn_out,
              final_norm_g):
    split_points = np.cumsum(IN_SIZES)[:-1].tolist()
    h = x
    for l in range(DEPTH):
        n = rmsnorm(h, attn_norm_g[l])
        qa, ckv, k_idx, w_idx, qkv_b, z_b, b_b, a_b = jnp.split(n @ w_in[l], split_points, axis=-1)
        y_a = sparse_mla_attention(qa, ckv, k_idx, w_idx, qa_norm_g[l], w_qb[l], kv_norm_g[l], w_uk[l], w_uv[l], w_iq[l])
        y_b = gated_deltanet(qkv_b, z_b, b_b, a_b, conv_w[l], a_log[l], dt_bias[l], delta_norm_g[l])
        h = h + jnp.concatenate([y_a, y_b], axis=-1) @ w_o[l]
        h = h + memory_cross_attention(rmsnorm(h, cross_norm_g[l]), rmsnorm(mem, mem_norm_g[l]), w_cq[l], w_ckv[l], w_co[l])
        h = h + swiglu(rmsnorm(h, ffn_norm_g[l]), w_ffn_in[l], w_ffn_out[l])
    return rmsnorm(h, final_norm_g)
```

```python
import numpy as np
import concourse.bass as bass
import concourse.mybir as mybir
from concourse.bass_utils import run_bass_kernel_spmd

F32 = mybir.dt.float32
BF16 = mybir.dt.bfloat16
F32R = mybir.dt.float32r
AF = mybir.ActivationFunctionType
ALU = mybir.AluOpType
AX = mybir.AxisListType

D = 4096
S = 4096
NB = 4
CTX = 4096
OWN0 = 2048
NOWN = 2048
EPS = 1e-6
QL = 1536
KVL = 512
HA = 16
HB = 16
NIN = 10416
QA0, CKV0, KI0, WI0, QKV0, Z0, B0, A0 = 0, 1536, 2048, 2176, 2192, 8336, 10384, 10400
DFF = 11008
NMEM = 256
TOPK = 256
NEG = -30000.0


class Op:
    __slots__ = ("eng", "fn", "deps", "signal", "sem", "val", "dma", "prewait", "idx", "noattach")


class Sched:
    ENGS = ("pe", "act", "dve", "pool", "sp")
    NDMA = {"sp": 24, "pool": 12, "act": 0, "dve": 0, "pe": 0}

    def __init__(self, nc):
        self.nc = nc
        self.q = {e: [] for e in self.ENGS}
        self.last_w = {}
        self.readers = {}
        self.last_gid = {}
        self.gdeps = {}
        self.n = 0
        self.bar_id = 0
        self.bar_deps = []
        self.bar_done = {e: 0 for e in self.ENGS}

    def op(self, eng, fn, reads=(), writes=(), dma=False, gid=None, noattach=False):
        o = Op()
        o.noattach = noattach
        o.eng, o.fn, o.dma, o.signal, o.sem, o.val, o.prewait = eng, fn, dma, dma, None, 0, None
        o.idx = self.n
        self.n += 1
        deps = []
        for r in reads:
            deps.extend(self.last_w.get(r, ()))
        for w_ in writes:
            if gid is not None and self.last_gid.get(w_) == gid:
                deps.extend(self.gdeps[w_])
            else:
                g = list(self.last_w.get(w_, ())) + list(self.readers.get(w_, ()))
                deps.extend(g)
                if gid is not None:
                    self.gdeps[w_] = g
        if self.bar_done[eng] < self.bar_id:
            deps.extend(self.bar_deps)
            self.bar_done[eng] = self.bar_id
        seen = set()
        dd = []
        for d_ in deps:
            if id(d_) in seen:
                continue
            seen.add(id(d_))
            if d_.eng == "pe" and eng == "pe" and not d_.dma:
                continue
            dd.append(d_)
            d_.signal = True
        o.deps = dd
        for w_ in writes:
            if gid is not None and self.last_gid.get(w_) == gid:
                self.last_w[w_].append(o)
            else:
                self.last_w[w_] = [o]
                self.readers[w_] = []
                self.last_gid[w_] = gid
        for r in reads:
            self.readers.setdefault(r, []).append(o)
        self.q[eng].append(o)
        return o

    def barrier(self):
        deps = []
        for e in self.ENGS:
            ops = self.q[e]
            comp = [o for o in ops if not o.dma]
            if comp:
                deps.append(comp[-1])
            dm = [o for o in ops if o.dma]
            deps.extend(dm[-self.NDMA[e]:] if self.NDMA[e] else [])
        for d_ in deps:
            d_.signal = True
        self.bar_id += 1
        self.bar_deps = deps

    def emit(self, final_waits):
        nc = self.nc
        engobj = {"pe": nc.tensor, "act": nc.scalar, "dve": nc.vector, "pool": nc.gpsimd, "sp": nc.sync}
        NDMA = self.NDMA
        csem = {e: nc.alloc_semaphore("c_" + e) for e in self.ENGS}
        dsem = {e: [nc.alloc_semaphore("d_%s%d" % (e, i)) for i in range(NDMA[e])] for e in self.ENGS}
        for e in self.ENGS:
            cnt = 0
            k = 0
            for o in self.q[e]:
                if o.dma:
                    n = NDMA[e]
                    j = k % n
                    o.sem = dsem[e][j]
                    o.val = 16 * (k // n + 1)
                    if k >= n:
                        o.prewait = (dsem[e][j], 16 * (k // n))
                    k += 1
                elif o.signal:
                    cnt += 1
                    o.sem = csem[e]
                    o.val = cnt
        with nc.Block() as block:
            def run(e):
                def body(eng):
                    waited = {}
                    for o in self.q[e]:
                        ws = []
                        if o.prewait is not None:
                            ws.append(o.prewait)
                        for d_ in o.deps:
                            ws.append((d_.sem, d_.val))
                        need = []
                        for (sm, v) in ws:
                            if waited.get(sm.num, 0) >= v:
                                continue
                            waited[sm.num] = v
                            need.append((sm, v))
                        attach = None
                        if need and not o.dma and not o.noattach:
                            attach = need.pop()
                        for (sm, v) in need:
                            eng.wait_ge(sm, v)
                        ins = o.fn(eng)
                        first = last = ins
                        if isinstance(ins, tuple):
                            first, last = ins
                        if attach is not None:
                            first._wait_ge(attach[0], attach[1])
                        if o.signal:
                            last.then_inc(o.sem, 16 if o.dma else 1)
                    if e == "sp":
                        for o in final_waits:
                            if waited.get(o.sem.num, 0) < o.val:
                                waited[o.sem.num] = o.val
                                eng.wait_ge(o.sem, o.val)
                return body
            block.tensor(run("pe"))
            block.scalar(run("act"))
            block.vector(run("dve"))
            block.gpsimd(run("pool"))
            block.sync(run("sp"))


class Builder:
    def __init__(self, stop_after="all", dbg=()):
        self.nc = bass.Bass("TRN2", target_bir_lowering=False)
        self.s = Sched(self.nc)
        self.stop_after = stop_after
        self.dbg = set(dbg)
        self.uid = 0
        self.out_ops = []
        self.evac_rr = 0
        self.wpar = 0
        self.tails = []
        self.preloaded = None

    def din(self, name, shape, dt=F32):
        return self.nc.dram_tensor(name, list(shape), dt, kind="ExternalInput").ap()

    def dscr(self, name, shape, dt):
        kind = "ExternalOutput" if name in self.dbg else "Internal"
        return self.nc.dram_tensor(name, list(shape), dt, kind=kind).ap()

    def sb(self, name, shape, dt):
        return self.nc.alloc_sbuf_tensor(name, list(shape), dt)

    def dma(self, out, in_, reads, writes, q="sp", gid=None):
        o = self.s.op(q, lambda e, out=out, in_=in_: e.dma_start(out=out, in_=in_), reads, writes, dma=True, gid=gid)
        return o

    def mm(self, out, lhsT, rhs, start, stop, reads, writes):
        return self.s.op("pe", lambda e: e.matmul(out, lhsT, rhs, start=start, stop=stop), reads, writes)

    def mmgroup(self, items, reads, writes):
        def fn(e, items=items):
            ins = None
            first = None
            for (o_, l_, r_, st, sp_) in items:
                ins = e.matmul(o_, l_, r_, start=st, stop=sp_)
                if first is None:
                    first = ins
            return (first, ins)
        return self.s.op("pe", fn, reads, writes)

    def tgroup(self, items, reads, writes):
        def fn(e, items=items):
            ins = None
            first = None
            for (o_, i_, id_) in items:
                ins = e.transpose(o_, i_, id_)
                if first is None:
                    first = ins
            return (first, ins)
        return self.s.op("pe", fn, reads, writes)

    def act(self, out, in_, func, reads, writes, **kw):
        return self.s.op("act", lambda e: e.activation(out=out, in_=in_, func=func, **kw), reads, writes)

    def evac_engine(self):
        self.evac_rr += 1
        return "act" if self.evac_rr % 2 else "dve"

    def copy(self, eng, out, in_, reads, writes):
        if eng == "act":
            return self.s.op("act", lambda e: e.copy(out=out, in_=in_), reads, writes)
        if eng == "dve":
            return self.s.op("dve", lambda e: e.tensor_copy(out=out, in_=in_), reads, writes)
        return self.s.op("pool", lambda e: e.tensor_copy(out=out, in_=in_), reads, writes)

    def ts(self, eng, out, in0, s1, s2, op0, op1, reads, writes, accum_out=None):
        if op1 is None:
            op1 = ALU.bypass
        if accum_out is None:
            return self.s.op(eng, lambda e: e.tensor_scalar(out, in0, s1, s2, op0, op1), reads, writes)
        return self.s.op(eng, lambda e: e.tensor_scalar(out, in0, s1, s2, op0, op1, accum_out), reads, writes, noattach=True)

    def tt(self, eng, out, in0, in1, op, reads, writes):
        return self.s.op(eng, lambda e: e.tensor_tensor(out, in0, in1, op), reads, writes)

    def stt(self, out, in0, scalar, in1, op0, op1, reads, writes, accum_out=None):
        if accum_out is None:
            return self.s.op("dve", lambda e: e.scalar_tensor_tensor(out, in0, scalar, in1, op0, op1), reads, writes)
        return self.s.op("dve", lambda e: e.scalar_tensor_tensor(out, in0, scalar, in1, op0, op1, accum_out), reads, writes, noattach=True)

    def memset(self, eng, ap, val, writes):
        return self.s.op(eng, lambda e: e.memset(ap, val), (), writes)

    def declare(self):
        nc = self.nc
        I = {}
        I["xc"] = self.din("xc", [CTX, D])
        I["memb"] = self.din("memb", [NMEM, D])
        I["pm"] = self.din("pm", [128, 1])
        I["consts"] = self.din("consts", [128, 8 * 128])
        for nm, shp in [("w_in", [D, NIN]), ("w_qb", [QL, 2048]), ("w_iq", [QL, 2048]), ("wukT", [128, HA, KVL]), ("gcols", [128, 16]), ("cwl", [128, 192]),
                        ("w_uv", [HA, KVL, 128]), ("w_o", [D, D]), ("w_cq", [D, 512]),
                        ("w_ckv", [D, 1024]), ("w_co", [512, D]), ("w_ffn_in", [D, 2 * DFF]), ("w_ffn_out", [DFF, D]),
                        ("attn_norm_g", [1, D]), ("a_log", [1, HB]),
                        ("dt_bias", [1, HB]), ("delta_norm_g", [1, 128]), ("cross_norm_g", [1, D]),
                        ("mem_norm_g", [1, D]), ("ffn_norm_g", [1, D]), ("final_norm_g", [1, D])]:
            I[nm] = self.din(nm, shp)
        self.I = I
        Sx = {}
        Sx["s_ckvnT"] = self.dscr("s_ckvnT", [4, 128, CTX], BF16)
        Sx["s_kiT"] = self.dscr("s_kiT", [128, CTX], BF16)
        Sx["s_small"] = self.dscr("s_small", [CTX, 48], F32)
        Sx["s_zs"] = self.dscr("s_zs", [NOWN, 2048], BF16)
        Sx["s_qiT"] = self.dscr("s_qiT", [HA, 128, NOWN], BF16)
        Sx["s_yT"] = self.dscr("s_yT", [32, 128, NOWN], BF16)
        Sx["s_h2"] = self.dscr("s_h2", [NOWN, D], F32)
        self.out = nc.dram_tensor("out", [NOWN, D], F32, kind="ExternalOutput").ap()
        if self.dbg:
            Sx["s_gq"] = self.dscr("s_gq", [HB, 128, CTX], BF16)
            Sx["s_gk"] = self.dscr("s_gk", [HB, 128, CTX], BF16)
            Sx["s_gv"] = self.dscr("s_gv", [HB, 128, CTX], BF16)
            Sx["s_qlT"] = self.dscr("s_qlT", [HA, 4, 128, NOWN], BF16)
            Sx["s_h1"] = self.dscr("s_h1", [NOWN, D], F32)
            Sx["s_h3"] = self.dscr("s_h3", [NOWN, D], F32)
            Sx["s_actT"] = self.dscr("s_actT", [86, 128, NOWN], BF16)
        else:
            n1 = HB * 128 * CTX
            p1 = nc.dram_tensor("pool1", [3 * n1], BF16, kind="Internal").ap()
            Sx["s_gq"] = p1[0:n1].rearrange("(h d t) -> h d t", h=HB, d=128)
            Sx["s_gk"] = p1[n1:2 * n1].rearrange("(h d t) -> h d t", h=HB, d=128)
            Sx["s_gv"] = p1[2 * n1:3 * n1].rearrange("(h d t) -> h d t", h=HB, d=128)
            Sx["s_actT"] = p1[0:86 * 128 * NOWN].rearrange("(c p t) -> c p t", c=86, p=128)
            p2 = nc.dram_tensor("pool2", [NOWN * D], F32, kind="Internal").ap()
            Sx["s_h1"] = p2.rearrange("(r c) -> r c", c=D)
            Sx["s_qlT"] = p2.bitcast(BF16).rearrange("(h cc c t) -> h cc c t", h=HA, cc=4, c=128)
            Sx["s_h3"] = self.out
        self.S = Sx
        self.ps = [nc.alloc_psum_tensor("ps%d" % i, [128, 512], F32) for i in range(8)]
        self.ps_rr = 0
        self.cst = self.sb("cst", [128, 8 * 128], F32)
        self.dma(self.cst[:], I["consts"], [], ["cst"])
        self.identb = self.sb("identb", [128, 128], BF16)
        self.onesb = self.sb("onesb", [128, 128], BF16)
        self.copy("dve", self.identb[:], self.cst[:, 0:128], ["cst"], ["identb"])
        self.copy("dve", self.onesb[:], self.cst[:, 128:256], ["cst"], ["onesb"])
        self.pmt = self.sb("pmt", [128, 1], F32)
        self.dma(self.pmt[:], I["pm"], [], ["pmt"])

    def C(self, i):
        return self.cst[:, i * 128:(i + 1) * 128]

    def bank(self):
        b = self.ps_rr % 8
        self.ps_rr += 1
        return b

    def rstd_from_ss(self, rstd, ss, n, tag):
        self.act(rstd, ss, AF.Sqrt, [tag + "_ss"], [tag + "_rstd"], scale=1.0 / n, bias=EPS)
        self.s.op("dve", lambda e: e.reciprocal(rstd, rstd), [tag + "_rstd"], [tag + "_rstd"])

    def wstream_load(self, par, Wb, Wv, KC, c0, ncols):
        buf = Wb[par]
        view = buf[:, 0:KC * ncols].rearrange("p (c n) -> p c n", n=ncols)
        step = max(1, 2048 // ncols)
        self.uid += 1
        gid = ("wl", self.uid)
        r = ("W", buf.name if hasattr(buf, "name") else id(buf), par)
        k0 = 0
        while k0 < KC:
            k1 = min(KC, k0 + step)
            self.dma(view[:, k0:k1, :], Wv[:, k0:k1, c0:c0 + ncols], [], [r], q="pool", gid=gid)
            k0 = k1
        return view, [r]

    def norm_transpose_tile(self, src_rows, gB, XT, t, KC, bufs, tag, extra_writes=(), extra_reads=(), par=None):
        xt, xn, ss, rstd = bufs
        W = KC * 128
        ptag = tag if par is None else "%s%d" % (tag, par)
        gtag = tag + "_gB"
        self.dma(xt[:, 0:W], src_rows, list(extra_reads), [ptag + "_xt"] + list(extra_writes))
        self.stt(xn[:, 0:W], xt[:, 0:W], 1.0, xt[:, 0:W], ALU.mult, ALU.mult, [ptag + "_xt"],
                 [tag + "_xn", ptag + "_ss"] + list(extra_writes), accum_out=ss[:])
        self.act(rstd[:], ss[:], AF.Sqrt, [ptag + "_ss"], [ptag + "_rstd"], scale=1.0 / W, bias=EPS)
        self.s.op("dve", lambda e: e.reciprocal(rstd[:], rstd[:]), [ptag + "_rstd"], [ptag + "_rstd"])
        self.stt(xn[:, 0:W], xt[:, 0:W], rstd[:], gB[:, 0:W], ALU.mult, ALU.mult,
                 [ptag + "_xt", ptag + "_rstd", gtag], [tag + "_xn"])
        for k8 in range(0, KC, 8):
            n8 = min(8, KC - k8)
            b = self.bank()
            pT = self.ps[b][:].bitcast(BF16)
            items = [(pT[:, j * 128:(j + 1) * 128], xn[:, (k8 + j) * 128:(k8 + j + 1) * 128], self.identb[:]) for j in range(n8)]
            self.tgroup(items, [tag + "_xn", "identb"], [("ps", b)])
            dst = XT[:, k8:k8 + n8, t * 128:(t + 1) * 128]
            src = pT[:, 0:n8 * 128].rearrange("p (c n) -> p c n", n=128)
            self.copy(self.evac_engine(), dst, src, [("ps", b)], [(tag + "_XT", t)])

    def load_bcast(self, dst, src_row, W, tag):
        self.dma(dst[:, 0:W], src_row.partition_broadcast(128), [], [tag])

    def linear(self, KC, Wv, chunks, Wb, body, nxt_first=None):
        n = len(chunks)
        if n == 0:
            return
        c0, nc_, info = chunks[0]
        key = (id(Wb[0]), KC, c0, nc_, str(Wv.tensor.name) + str(Wv.offset))
        if self.preloaded is not None and self.preloaded[0] == key:
            cur = self.preloaded[1]
        else:
            cur = self.wstream_load(self.wpar, Wb, Wv, KC, c0, nc_)
        self.preloaded = None
        for i in range(n):
            nxt = None
            if i + 1 < n:
                c0n, ncn, infon = chunks[i + 1]
                nxt = self.wstream_load(1 - self.wpar, Wb, Wv, KC, c0n, ncn)
            elif nxt_first is not None:
                KCn, Wvn, c0n, ncn, Wbn = nxt_first
                if Wbn is Wb:
                    keyn = (id(Wb[0]), KCn, c0n, ncn, str(Wvn.tensor.name) + str(Wvn.offset))
                    self.preloaded = (keyn, self.wstream_load(1 - self.wpar, Wb, Wvn, KCn, c0n, ncn))
            view, wres = cur
            body(view, wres, chunks[i][2], chunks[i][1])
            cur = nxt
            self.wpar = 1 - self.wpar
        self.flush_tails()

    def fm_chunk(self, XT, xt_res, KC, TG, view, wres, ncols, epi, sub0):
        nh = (TG + 511) // 512
        for sub in range(ncols // 128):
            for hf in range(nh):
                t0 = hf * 512
                tw = min(512, TG - t0)
                b = self.bank()
                items = [(self.ps[b][:, 0:tw], view[:, k, sub * 128:(sub + 1) * 128], XT[:, k, t0:t0 + tw], k == 0, k == KC - 1)
                         for k in range(KC)]
                self.mmgroup(items, list(wres) + list(xt_res), [("ps", b)])
                self.flush_tails()
                if nh == 1:
                    tl = epi(sub0 + sub, b)
                else:
                    tl = epi(sub0 + sub, b, hf)
                if tl is not None:
                    self.tails.append(tl)

    def flush_tails(self):
        tl, self.tails = self.tails, []
        for f in tl:
            f()

    def tm_chunk(self, XT, xt_res, KC, TG, view, wres, ncols, epi):
        for t in range(TG // 128):
            b = self.bank()
            items = [(self.ps[b][:, 0:ncols], XT[:, k, t * 128:(t + 1) * 128], view[:, k, 0:ncols], k == 0, k == KC - 1)
                     for k in range(KC)]
            self.mmgroup(items, list(wres) + list(xt_res), [("ps", b)])
            epi(t, b)

    def phase_A(self):
        I, Sx = self.I, self.S
        sb = self.sb
        self.phase_begin()
        XT = sb("A_XT", [128, 32, 512], BF16)
        Wb = [sb("A_W0", [128, 8192], BF16), sb("A_W1", [128, 8192], BF16)]
        big = sb("A_big", [128, 6144], F32)
        xt = big[:, 0:4096]
        xn = big[:, 4096:6144].bitcast(BF16)
        ss = sb("A_ss", [128, 1], F32)
        rstd = sb("A_rstd", [128, 1], F32)
        gB = sb("A_gB", [128, D], F32)
        qaraw = big[:, 0:6144].rearrange("p (c n) -> p c n", n=512)
        qanT = sb("A_qanT", [128, 12, 512], BF16)
        ckvraw = sb("A_ckvraw", [128, 4, 512], F32)
        ckvn = sb("A_ckvn", [128, 4, 512], BF16)
        rsb = sb("A_rsb", [128, 512], F32)
        sqb = [sb("A_sq%d" % i, [128, 512], BF16) for i in range(3)]
        cbuf = [sb("A_cb%d" % i, [128, 516], F32) for i in range(3)]
        cacc = [sb("A_ca%d" % i, [128, 512], F32) for i in range(3)]
        csil = [sb("A_cs%d" % i, [128, 512], F32) for i in range(3)]
        cout = [sb("A_co%d" % i, [128, 512], BF16) for i in range(3)]
        rn = [sb("A_rn%d" % i, [128, 512], F32) for i in range(3)]
        carry = sb("A_carry", [128, 48, 4], F32)
        cw = sb("A_cw", [128, 48, 4], F32)
        Wsm = sb("A_Wsm", [128, 32, 48], BF16)
        smt = [sb("A_smt%d" % i, [128, 48], F32) for i in range(2)]
        smo = [sb("A_smo%d" % i, [128, 48], F32) for i in range(2)]
        abc = sb("A_abc", [128, 3, 16], F32)
        gqa = sb("A_gqa", [128, 12], F32)
        gkv = sb("A_gkv", [128, 4], F32)
        wukT = sb("A_wukT", [128, HA, 512], BF16)
        qTh = [sb("A_qTh%d" % i, [128, 512], BF16) for i in range(2)]
        ql = [sb("A_ql%d" % i, [128, 512], BF16) for i in range(3)]
        zso = [sb("A_zso%d" % i, [128, 256], BF16) for i in range(3)]
        kio = [sb("A_kio%d" % i, [128, 512], BF16) for i in range(2)]

        self.load_bcast(gB, I["attn_norm_g"], D, "A_gB")
        self.dma(gqa[:], I["gcols"][:, 0:12], [], ["A_gqa"])
        self.dma(gkv[:], I["gcols"][:, 12:16], [], ["A_gkv"])
        self.dma(cw[:], I["cwl"].rearrange("p (c k) -> p c k", k=4), [], ["A_cw"])
        self.memset("pool", carry[:], 0.0, ["A_carry"])
        self.dma(abc[:, 0, :], I["dt_bias"].partition_broadcast(128), [], ["A_abc0"])
        self.dma(abc[:, 2, :], I["a_log"].partition_broadcast(128), [], ["A_abc2"])
        self.act(abc[:, 1, :], abc[:, 2, :], AF.Exp, ["A_abc2"], ["A_abc1"])
        self.ts("dve", abc[:, 1, :], abc[:, 1, :], -1.0, None, ALU.mult, None, ["A_abc1"], ["A_abc1"])
        win = I["w_in"].rearrange("(c p) n -> p c n", p=128)
        self.dma(Wsm[:, :, 0:16], win[:, :, WI0:WI0 + 16], [], ["A_Wsm0"], q="pool")
        self.dma(Wsm[:, :, 16:48], win[:, :, B0:B0 + 32], [], ["A_Wsm1"], q="pool")
        for h in range(0, HA, 4):
            self.dma(wukT[:, h:h + 4, :], I["wukT"][:, h:h + 4, :], [], [("A_wukT", h + j) for j in range(4)], q="pool")
        wqb = I["w_qb"].rearrange("(c p) n -> p c n", p=128)
        wiq = I["w_iq"].rearrange("(c p) n -> p c n", p=128)

        for g in range(8):
            own = g >= 4
            og = g - 4
            T0 = g * 512
            xt_res = [("A_XT", t) for t in range(4)]
            for t in range(4):
                self.norm_transpose_tile(I["xc"][T0 + t * 128:T0 + (t + 1) * 128, :], gB, XT, t, 32,
                                         (xt, xn, ss, rstd), "A", extra_writes=[("A_qaraw", c) for c in range(12)])
            for t in range(4):
                b = self.bank()
                items = [(self.ps[b][:, 0:48], XT[:, k, t * 128:(t + 1) * 128], Wsm[:, k, :], k == 0, k == 31) for k in range(32)]
                self.mmgroup(items, ["A_Wsm0", "A_Wsm1", ("A_XT", t)], [("ps", b)])
                i2 = t % 2
                st, so = smt[i2], smo[i2]
                self.copy("dve", st[:], self.ps[b][:, 0:48], [("ps", b)], [("A_smt", i2)])
                self.ts("dve", so[:, 0:16], st[:, 0:16], 0.25, None, ALU.mult, None, [("A_smt", i2)], [("A_smo", i2, 0)])
                self.act(so[:, 16:32], st[:, 16:32], AF.Sigmoid, [("A_smt", i2)], [("A_smo", i2, 1)])
                self.tt("dve", st[:, 32:48], st[:, 32:48], abc[:, 0, :], ALU.add, [("A_smt", i2), "A_abc0"], [("A_smt", i2)])
                self.act(st[:, 32:48], st[:, 32:48], AF.Exp, [("A_smt", i2)], [("A_smt", i2)])
                self.act(st[:, 32:48], st[:, 32:48], AF.Ln, [("A_smt", i2)], [("A_smt", i2)], bias=1.0)
                self.tt("dve", so[:, 32:48], st[:, 32:48], abc[:, 1, :], ALU.mult, [("A_smt", i2), "A_abc1"], [("A_smo", i2, 2)])
                self.dma(Sx["s_small"][T0 + t * 128:T0 + (t + 1) * 128, :], so[:],
                         [("A_smo", i2, 0), ("A_smo", i2, 1), ("A_smo", i2, 2)], [("s_small", g, t)])

            chunks = []
            if own:
                for c in range(0, 12, 2):
                    chunks.append((QA0 + c * 128, 256, ("qa", c)))
            for c in range(0, 4, 2):
                chunks.append((CKV0 + c * 128, 256, ("ckv", c)))
            chunks.append((KI0, 128, ("ki", 0)))
            q_from = 0 if g >= 3 else 16
            for c in range(q_from, 48, 2):
                chunks.append((QKV0 + c * 128, 256, ("qkv", c)))

            def epi_factory(info):
                fam, cbase = info

                def epi(sub, b):
                    c = cbase + sub
                    P = self.ps[b][:, 0:512]
                    if fam == "qa":
                        self.copy("act", qaraw[:, c, :], P, [("ps", b)], [("A_qaraw", c)])
                    elif fam == "ckv":
                        self.copy("act", ckvraw[:, c, :], P, [("ps", b)], [("A_ckvraw", c)])
                    elif fam == "ki":
                        i2 = g % 2
                        self.copy("act", kio[i2][:], P, [("ps", b)], [("A_kio", i2)])
                        self.dma(Sx["s_kiT"][:, T0:T0 + 512], kio[i2][:], [("A_kio", i2)], [("s_kiT", g)])
                    else:
                        return self.qkv_epilogue(c, b, g, T0, cbuf, cacc, csil, cout, rn, sqb, carry, cw)
                    return None
                return epi

            def body(view, wres, info, ncols):
                self.fm_chunk(XT, xt_res, 32, 512, view, wres, ncols, epi_factory(info), 0)
            if own:
                nf = (32, win, Z0, 256, Wb)
            elif g + 1 < 8:
                nf = (32, win, QA0 if g + 1 >= 4 else CKV0, 256, Wb)
            else:
                nf = None
            self.linear(32, win, chunks, Wb, body, nxt_first=nf)

            self.fm_rmsnorm(ckvraw, "A_ckvraw", 4, KVL, gkv, "A_gkv", ckvn, "A_ckvn", sqb, rsb)
            for c in range(4):
                self.dma(Sx["s_ckvnT"][c, :, T0:T0 + 512], ckvn[:, c, :], [("A_ckvn", c)], [("s_ckvnT", c, g)])

            if own:
                def zbody(view, wres, info, ncols):
                    def epi(t, b):
                        self.uid += 1
                        i3 = self.uid % 3
                        self.act(zso[i3][:, 0:ncols], self.ps[b][:, 0:ncols], AF.Silu, [("ps", b)], [("A_zso", i3)])
                        r0 = og * 512 + t * 128
                        self.dma(Sx["s_zs"][r0:r0 + 128, info:info + ncols], zso[i3][:, 0:ncols], [("A_zso", i3)],
                                 [("s_zs", og, t, info)])
                    self.tm_chunk(XT, xt_res, 32, 512, view, wres, ncols, epi)
                self.linear(32, win, [(Z0 + j * 256, 256, j * 256) for j in range(8)], Wb, zbody, nxt_first=(12, wqb, 0, 512, Wb))

                self.fm_rmsnorm(qaraw, "A_qaraw", 12, QL, gqa, "A_gqa", qanT, "A_qanT", sqb, rsb)
                qres = [("A_qanT", c) for c in range(12)]

                def qbody(view, wres, info, ncols):
                    def epi(h, b):
                        i2 = h % 2
                        self.copy("act", qTh[i2][:], self.ps[b][:, 0:512], [("ps", b)], [("A_qTh", i2)])
                        for cc in range(4):
                            b2 = self.bank()
                            self.mm(self.ps[b2][:, 0:512], wukT[:, h, cc * 128:(cc + 1) * 128], qTh[i2][:], True, True,
                                    [("A_wukT", h), ("A_qTh", i2)], [("ps", b2)])
                            self.uid += 1
                            i3 = self.uid % 3
                            self.copy(self.evac_engine(), ql[i3][:], self.ps[b2][:, 0:512], [("ps", b2)], [("A_ql", i3)])
                            self.dma(Sx["s_qlT"][h, cc, :, og * 512:(og + 1) * 512], ql[i3][:], [("A_ql", i3)],
                                     [("s_qlT", h, cc, og)])
                    self.fm_chunk(qanT, qres, 12, 512, view, wres, ncols, epi, info)
                self.linear(12, wqb, [(j * 512, 512, j * 4) for j in range(4)], Wb, qbody, nxt_first=(12, wiq, 0, 512, Wb))

                def ibody(view, wres, info, ncols):
                    def epi(h, b):
                        self.uid += 1
                        i3 = self.uid % 3
                        self.copy(self.evac_engine(), ql[i3][:], self.ps[b][:, 0:512], [("ps", b)], [("A_ql", i3)])
                        self.dma(Sx["s_qiT"][h, :, og * 512:(og + 1) * 512], ql[i3][:], [("A_ql", i3)], [("s_qiT", h, og)])
                    self.fm_chunk(qanT, qres, 12, 512, view, wres, ncols, epi, info)
                self.linear(12, wiq, [(j * 512, 512, j * 4) for j in range(4)], Wb, ibody,
                            nxt_first=(32, win, QA0, 256, Wb) if g + 1 < 8 else None)
        self.phase_end()

    def fm_rmsnorm(self, raw, rawtag, KC, n, gcol, gtag, outT, outtag, sqb, rsb):
        b = self.bank()
        for c in range(KC):
            i2 = c % 2
            self.tt("dve", sqb[i2][:], raw[:, c, :], raw[:, c, :], ALU.mult, [(rawtag, c)], [("A_sq", i2)])
            self.mm(self.ps[b][:, 0:512], self.onesb[:], sqb[i2][:], c == 0, c == KC - 1, ["onesb", ("A_sq", i2)], [("ps", b)])
        self.act(rsb[:], self.ps[b][:, 0:512], AF.Sqrt, [("ps", b)], ["A_rsb"], scale=1.0 / n, bias=EPS)
        self.s.op("dve", lambda e: e.reciprocal(rsb[:], rsb[:]), ["A_rsb"], ["A_rsb"])
        for c in range(KC):
            self.stt(outT[:, c, :], raw[:, c, :], gcol[:, c:c + 1], rsb[:], ALU.mult, ALU.mult,
                     [(rawtag, c), gtag, "A_rsb"], [(outtag, c)])

    def qkv_epilogue(self, c, b, g, T0, cbuf, cacc, csil, cout, rn, sqb, carry, cw):
        Sx = self.S
        i2 = c % 3
        cb, ca, cs = cbuf[i2], cacc[i2], csil[i2]
        self.copy("pool", cb[:, 0:3], carry[:, c, 0:3], [("A_carry", c)], [("A_cb", i2, 0)])
        self.copy("act", cb[:, 3:515], self.ps[b][:, 0:512], [("ps", b)], [("A_cb", i2, 1)])
        rd = [("A_cb", i2, 0), ("A_cb", i2, 1), "A_cw"]
        self.ts("dve", ca[:], cb[:, 3:515], cw[:, c, 3:4], None, ALU.mult, None, rd, [("A_ca", i2)])
        for k in (2, 1, 0):
            self.stt(ca[:], cb[:, k:k + 512], cw[:, c, k:k + 1], ca[:], ALU.mult, ALU.add, rd + [("A_ca", i2)], [("A_ca", i2)])
        self.copy("pool", carry[:, c, 0:3], cb[:, 512:515], [("A_cb", i2, 1)], [("A_carry", c)])
        self.uid += 1
        i3 = self.uid % 3
        co = cout[i3]
        fam = c // 16
        h = c % 16
        if fam == 2:
            self.act(co[:], ca[:], AF.Silu, [("A_ca", i2)], [("A_co", i3)])
            self.dma(Sx["s_gv"][h, :, T0:T0 + 512], co[:], [("A_co", i3)], [("s_gv", h, g)])
            return
        self.act(cs[:], ca[:], AF.Silu, [("A_ca", i2)], [("A_cs", i2)])
        self.tt("dve", sqb[i2][:], cs[:], cs[:], ALU.mult, [("A_cs", i2)], [("A_sq", i2)])
        return lambda: self.qkv_tail(c, g, T0, i2, i3, co, cs, rn, sqb, fam, h)

    def qkv_tail(self, c, g, T0, i2, i3, co, cs, rn, sqb, fam, h):
        Sx = self.S
        b2 = self.bank()
        self.mm(self.ps[b2][:, 0:512], self.onesb[:], sqb[i2][:], True, True, ["onesb", ("A_sq", i2)], [("ps", b2)])
        r = rn[i2]
        self.act(r[:], self.ps[b2][:, 0:512], AF.Sqrt, [("ps", b2)], [("A_rn", i2)], scale=1.0, bias=EPS)
        self.s.op("dve", lambda e: e.reciprocal(r[:], r[:]), [("A_rn", i2)], [("A_rn", i2)])
        if fam == 0:
            self.stt(co[:], cs[:], 128.0 ** -0.5, r[:], ALU.mult, ALU.mult, [("A_cs", i2), ("A_rn", i2)], [("A_co", i3)])
            self.dma(Sx["s_gq"][h, :, T0:T0 + 512], co[:], [("A_co", i3)], [("s_gq", h, g)])
        else:
            self.tt("dve", co[:], cs[:], r[:], ALU.mult, [("A_cs", i2), ("A_rn", i2)], [("A_co", i3)])
            self.dma(Sx["s_gk"][h, :, T0:T0 + 512], co[:], [("A_co", i3)], [("s_gk", h, g)])


    def phase_begin(self):
        self._sb_mark = self.nc.sbuf_base

    def phase_end(self):
        self.s.barrier()
        self.nc.sbuf_base = self._sb_mark

    def phase_0(self):
        I = self.I
        sb = self.sb
        self.kxT = sb("P_kxT", [128, 4, NMEM], BF16)
        self.vx = sb("P_vx", [128, 2, 512], BF16)
        self.phase_begin()
        XT = sb("0_XT", [128, 32, 256], BF16)
        Wb = [sb("0_W0", [128, 8192], BF16), sb("0_W1", [128, 8192], BF16)]
        xt = sb("0_xt", [128, D], F32)
        xn = sb("0_xn", [128, D], BF16)
        ss = sb("0_ss", [128, 1], F32)
        rstd = sb("0_rstd", [128, 1], F32)
        gB = sb("0_gB", [128, D], F32)
        self.load_bcast(gB, I["mem_norm_g"], D, "0_gB")
        for t in range(2):
            self.norm_transpose_tile(I["memb"][t * 128:(t + 1) * 128, :], gB, XT, t, 32, (xt, xn, ss, rstd), "0")
        xres = [("0_XT", 0), ("0_XT", 1)]
        wckv = I["w_ckv"].rearrange("(c p) n -> p c n", p=128)

        def kbody(view, wres, info, ncols):
            def epi(h, b):
                self.copy("act", self.kxT[:, h, :], self.ps[b][:, 0:NMEM], [("ps", b)], [("P_kxT", h)])
            self.fm_chunk(XT, xres, 32, NMEM, view, wres, ncols, epi, info)
        self.linear(32, wckv, [(0, 256, 0), (256, 256, 2)], Wb, kbody)

        def vbody(view, wres, info, ncols):
            def epi(t, b):
                self.copy("act", self.vx[:, t, info:info + ncols], self.ps[b][:, 0:ncols], [("ps", b)], [("P_vx", t, info)])
            self.tm_chunk(XT, xres, 32, NMEM, view, wres, ncols, epi)
        self.linear(32, wckv, [(512, 256, 0), (768, 256, 256)], Wb, vbody)
        self.phase_end()

    def phase_B(self):
        I, Sx = self.I, self.S
        sb = self.sb
        self.phase_begin()
        S32 = sb("B_S32", [128, HB, 128], F32)
        Sb = sb("B_Sb", [128, HB, 128], BF16)
        gdnB = sb("B_gdnB", [128, 128], F32)
        kTa = [sb("B_kT%d" % i, [128, HB, 128], BF16) for i in range(2)]
        vTa = [sb("B_vT%d" % i, [128, HB, 128], BF16) for i in range(2)]
        qTa = [sb("B_qT%d" % i, [128, HB, 128], BF16) for i in range(2)]
        zsa = [sb("B_zs%d" % i, [128, 2048], BF16) for i in range(2)]
        sma = [sb("B_sm%d" % i, [128, 48], F32) for i in range(2)]
        gga = [sb("B_gg%d" % i, [128, 80], F32) for i in range(2)]
        bGa = [sb("B_bG%d" % i, [128, 16], F32) for i in range(2)]
        yb = [sb("B_yb%d" % i, [128, HB, 128], BF16) for i in range(2)]
        ybT = [sb("B_ybT%d" % i, [128, HB, 128], BF16) for i in range(2)]
        NS = 8
        Ug = [sb("B_Ug%d" % i, [128, 128], F32) for i in range(NS)]
        E = [sb("B_E%d" % i, [128, 128], F32) for i in range(NS)]
        Esn = [sb("B_Esn%d" % i, [128, 128], F32) for i in range(NS)]
        Ei = [sb("B_Ei%d" % i, [128, 128], F32) for i in range(NS)]
        kdec = [sb("B_kdec%d" % i, [128, 128], BF16) for i in range(NS)]
        X = [[sb("B_X%d_%d" % (i, j), [128, 256], F32) for j in range(2)] for i in range(NS)]
        BB = [[sb("B_BB%d_%d" % (i, j), [128, 256], F32) for j in range(2)] for i in range(NS)]
        intra = [sb("B_in%d" % i, [128, 128], BF16) for i in range(NS)]
        intraT = [sb("B_inT%d" % i, [128, 128], BF16) for i in range(NS)]
        wT = [sb("B_wT%d" % i, [128, 128], BF16) for i in range(NS)]
        vnew = [sb("B_vn%d" % i, [128, 128], BF16) for i in range(NS)]
        o1 = [sb("B_o1%d" % i, [128, 128], F32) for i in range(NS)]
        oo = [sb("B_oo%d" % i, [128, 128], F32) for i in range(NS)]
        osq = [sb("B_osq%d" % i, [128, 128], BF16) for i in range(NS)]
        oss = [sb("B_oss%d" % i, [128, 1], F32) for i in range(NS)]
        ors = [sb("B_ors%d" % i, [128, 1], F32) for i in range(NS)]
        self.memset("pool", S32[:], 0.0, [("B_S32", h) for h in range(HB)])
        self.memset("pool", Sb[:], 0.0, [("B_Sb", h) for h in range(HB)])
        self.load_bcast(gdnB, I["delta_norm_g"], 128, "B_gdnB")
        Uc, ONESf, Lst, NSTR, INCL, IDf = self.C(2), self.C(1), self.C(3), self.C(5), self.C(6), self.C(0)

        for T in range(32):
            own = T >= 16
            p2 = T % 2
            c0 = T * 128
            kT, vT, qT, zs, sm, gg, bG = kTa[p2], vTa[p2], qTa[p2], zsa[p2], sma[p2], gga[p2], bGa[p2]
            self.dma(kT[:], Sx["s_gk"][:, :, c0:c0 + 128].rearrange("h d t -> d h t"),
                     [("s_gk", h, T // 4) for h in range(HB)], [("B_kT", p2)])
            self.dma(vT[:], Sx["s_gv"][:, :, c0:c0 + 128].rearrange("h d t -> d h t"),
                     [("s_gv", h, T // 4) for h in range(HB)], [("B_vT", p2)])
            self.dma(sm[:], Sx["s_small"][c0:c0 + 128, :], [("s_small", T // 4, T % 4)], [("B_sm", p2)])
            if own:
                ot = T - 16
                self.dma(qT[:], Sx["s_gq"][:, :, c0:c0 + 128].rearrange("h d t -> d h t"),
                         [("s_gq", h, T // 4) for h in range(HB)], [("B_qT", p2)])
                self.dma(zs[:], Sx["s_zs"][ot * 128:(ot + 1) * 128, :],
                         [("s_zs", ot // 4, ot % 4, j * 256) for j in range(8)], [("B_zs", p2)])
            beta = sm[:, 16:32]
            gcol = sm[:, 32:48]
            b = self.bank()
            self.mm(self.ps[b][:, 0:16], Uc, gcol, True, True, ["cst", ("B_sm", p2)], [("ps", b)])
            self.mm(self.ps[b][:, 16:32], ONESf, gcol, True, True, ["cst", ("B_sm", p2)], [("ps", b)])
            self.copy("dve", gg[:, 0:32], self.ps[b][:, 0:32], [("ps", b)], [("B_gg", p2)])
            self.tt("dve", gg[:, 48:64], gg[:, 16:32], gg[:, 0:16], ALU.subtract, [("B_gg", p2)], [("B_gg", p2)])
            self.act(gg[:, 32:48], gg[:, 0:16], AF.Exp, [("B_gg", p2)], [("B_gg", p2)])
            self.act(gg[:, 48:64], gg[:, 48:64], AF.Exp, [("B_gg", p2)], [("B_gg", p2)])
            self.act(gg[:, 64:80], gg[:, 16:32], AF.Exp, [("B_gg", p2)], [("B_gg", p2)])
            self.tt("dve", bG[:], beta, gg[:, 32:48], ALU.mult, [("B_sm", p2), ("B_gg", p2)], [("B_bG", p2)])
            tile_res = [("B_sm", p2), ("B_gg", p2), ("B_bG", p2)]

            def head_gen(h, s_):
                kTh, vTh, qTh = kT[:, h, :], vT[:, h, :], qT[:, h, :]
                b = self.bank()
                pT = self.ps[b][:].bitcast(BF16)
                self.tgroup([(pT[:, 0:128], kTh, self.identb[:]), (pT[:, 128:256], vTh, self.identb[:])],
                            [("B_kT", p2), ("B_vT", p2), "identb"], [("ps", b)])
                self.act(kdec[s_][:], pT[:, 0:128], AF.Copy, [("ps", b)] + tile_res, [("B_kdec", s_)], scale=gg[:, 48 + h:49 + h])
                X0 = X[s_][0]
                self.act(X0[:, 128:256].bitcast(F32R), pT[:, 0:128], AF.Copy, [("ps", b)] + tile_res, [("B_X", s_, 0, 1)], scale=bG[:, h:h + 1])
                self.act(X0[:, 0:128].bitcast(F32R), pT[:, 128:256], AF.Copy, [("ps", b)] + tile_res, [("B_X", s_, 0, 0)], scale=beta[:, h:h + 1])
                yield
                self.act(Ug[s_][:], Uc, AF.Copy, ["cst"] + tile_res, [("B_Ug", s_)], scale=gcol[:, h:h + 1])
                b = self.bank()
                self.mm(self.ps[b][:, 0:128], Ug[s_][:], Lst, True, True, [("B_Ug", s_), "cst"], [("ps", b)])
                self.act(E[s_][:], self.ps[b][:, 0:128], AF.Exp, [("ps", b)], [("B_E", s_)])
                self.tt("pool", Esn[s_][:], E[s_][:], NSTR, ALU.mult, [("B_E", s_), "cst"], [("B_Esn", s_)])
                yield
                b = self.bank()
                self.mm(self.ps[b][:, 0:128], kTh, kTh, True, True, [("B_kT", p2)], [("ps", b)])
                B0 = BB[s_][0]
                self.stt(B0[:, 0:128].bitcast(F32R), self.ps[b][:, 0:128], beta[:, h:h + 1], Esn[s_][:], ALU.mult, ALU.mult,
                         [("ps", b), ("B_Esn", s_)] + tile_res, [("B_BB", s_, 0, 0)])
                if own:
                    self.tt("pool", Ei[s_][:], E[s_][:], INCL, ALU.mult, [("B_E", s_), "cst"], [("B_Ei", s_)])
                    b = self.bank()
                    self.mm(self.ps[b][:, 0:128], qTh, kTh, True, True, [("B_kT", p2), ("B_qT", p2)], [("ps", b)])
                    self.tt("dve", intra[s_][:], self.ps[b][:, 0:128], Ei[s_][:], ALU.mult, [("ps", b), ("B_Ei", s_)], [("B_in", s_)])
                yield
                b = self.bank()
                self.tgroup([(self.ps[b][:, 0:128], B0[:, 0:128], IDf)], [("B_BB", s_, 0, 0), "cst"], [("ps", b)])
                self.copy("act", B0[:, 128:256].bitcast(F32R), self.ps[b][:, 0:128], [("ps", b)], [("B_BB", s_, 0, 1)])
                if own:
                    b = self.bank()
                    pT = self.ps[b][:].bitcast(BF16)
                    self.tgroup([(pT[:, 0:128], intra[s_][:], self.identb[:])], [("B_in", s_), "identb"], [("ps", b)])
                    self.copy("act", intraT[s_][:], pT[:, 0:128], [("ps", b)], [("B_inT", s_)])
                yield
                for lv in range(7):
                    cur, nx = lv % 2, (lv + 1) % 2
                    Bc, Xc, Xn = BB[s_][cur], X[s_][cur], X[s_][nx]
                    b = self.bank()
                    self.mm(self.ps[b][:, 0:256], Bc[:, 128:256].bitcast(F32R), Xc[:].bitcast(F32R), True, True,
                            [("B_BB", s_, cur, 1), ("B_X", s_, cur, 0), ("B_X", s_, cur, 1)], [("ps", b)])
                    self.tt("dve", Xn[:].bitcast(F32R), self.ps[b][:, 0:256], Xc[:], ALU.add,
                            [("ps", b), ("B_X", s_, cur, 0), ("B_X", s_, cur, 1)], [("B_X", s_, nx, 0), ("B_X", s_, nx, 1)])
                    if lv < 6:
                        Bn = BB[s_][nx]
                        b = self.bank()
                        self.mmgroup([(self.ps[b][:, 0:128], Bc[:, 128:256].bitcast(F32R), Bc[:, 0:128].bitcast(F32R), True, True),
                                      (self.ps[b][:, 128:256], Bc[:, 0:128].bitcast(F32R), Bc[:, 128:256].bitcast(F32R), True, True)],
                                     [("B_BB", s_, cur, 0), ("B_BB", s_, cur, 1)], [("ps", b)])
                        self.copy("act" if lv % 2 == 0 else "dve", Bn[:].bitcast(F32R), self.ps[b][:, 0:256], [("ps", b)],
                                  [("B_BB", s_, nx, 0), ("B_BB", s_, nx, 1)])
                    yield
                Xf = X[s_][1]
                xres = [("B_X", s_, 1, 0), ("B_X", s_, 1, 1)]
                b = self.bank()
                self.tgroup([(self.ps[b][:, 0:128], Xf[:, 128:256], IDf)], xres + ["cst"], [("ps", b)])
                self.copy("act", wT[s_][:], self.ps[b][:, 0:128], [("ps", b)], [("B_wT", s_)])
                yield
                b = self.bank()
                self.mm(self.ps[b][:, 0:128], wT[s_][:], Sb[:, h, :], True, True, [("B_wT", s_), ("B_Sb", h)], [("ps", b)])
                self.tt("dve", vnew[s_][:], Xf[:, 0:128], self.ps[b][:, 0:128], ALU.subtract, [("ps", b)] + xres, [("B_vn", s_)])
                yield
                if own:
                    b = self.bank()
                    self.mm(self.ps[b][:, 0:128], qTh, Sb[:, h, :], True, True, [("B_qT", p2), ("B_Sb", h)], [("ps", b)])
                    self.act(o1[s_][:], self.ps[b][:, 0:128], AF.Copy, [("ps", b)] + tile_res, [("B_o1", s_)], scale=gg[:, 32 + h:33 + h])
                    b = self.bank()
                    self.mm(self.ps[b][:, 0:128], intraT[s_][:], vnew[s_][:], True, True, [("B_inT", s_), ("B_vn", s_)], [("ps", b)])
                    self.tt("dve", oo[s_][:], self.ps[b][:, 0:128], o1[s_][:], ALU.add, [("ps", b), ("B_o1", s_)], [("B_oo", s_)])
                    self.stt(osq[s_][:], oo[s_][:], 1.0, oo[s_][:], ALU.mult, ALU.mult, [("B_oo", s_)], [("B_osq", s_), ("B_oss", s_)],
                             accum_out=oss[s_][:])
                    self.act(ors[s_][:], oss[s_][:], AF.Sqrt, [("B_oss", s_)], [("B_ors", s_)], scale=1.0 / 128, bias=EPS)
                    self.s.op("dve", lambda e, r=ors[s_]: e.reciprocal(r[:], r[:]), [("B_ors", s_)], [("B_ors", s_)])
                    self.stt(oo[s_][:], oo[s_][:], ors[s_][:], gdnB[:], ALU.mult, ALU.mult, [("B_oo", s_), ("B_ors", s_), "B_gdnB"], [("B_oo", s_)])
                    self.tt("dve", yb[p2][:, h, :], oo[s_][:], zs[:, h * 128:(h + 1) * 128], ALU.mult, [("B_oo", s_), ("B_zs", p2)], [("B_yb", p2, h)])
                yield
                b = self.bank()
                self.mm(self.ps[b][:, 0:128], kdec[s_][:], vnew[s_][:], True, True, [("B_kdec", s_), ("B_vn", s_)], [("ps", b)])
                self.stt(S32[:, h, :], S32[:, h, :], gg[:, 64 + h:65 + h], self.ps[b][:, 0:128], ALU.mult, ALU.add,
                         [("ps", b), ("B_S32", h)] + tile_res, [("B_S32", h)])
                self.copy("pool", Sb[:, h, :], S32[:, h, :], [("B_S32", h)], [("B_Sb", h)])
            for g0 in range(0, HB, NS):
                gens = [head_gen(h, h - g0) for h in range(g0, g0 + NS)]
                while gens:
                    for g_ in list(gens):
                        try:
                            next(g_)
                        except StopIteration:
                            gens.remove(g_)
            if own:
                ot = T - 16
                for h0 in range(0, HB, 8):
                    b = self.bank()
                    pT = self.ps[b][:].bitcast(BF16)
                    self.tgroup([(pT[:, j * 128:(j + 1) * 128], yb[p2][:, h0 + j, :], self.identb[:]) for j in range(8)],
                                [("B_yb", p2, h0 + j) for j in range(8)] + ["identb"], [("ps", b)])
                    self.copy(self.evac_engine(), ybT[p2][:, h0:h0 + 8, :], pT[:, 0:1024].rearrange("p (c n) -> p c n", n=128),
                              [("ps", b)], [("B_ybT", p2, h0)])
                self.dma(Sx["s_yT"][16:32, :, ot * 128:(ot + 1) * 128].rearrange("h d t -> d h t"), ybT[p2][:],
                         [("B_ybT", p2, 0), ("B_ybT", p2, 8)], [("s_yT", 1, ot)])
        self.phase_end()


    def phase_C(self):
        I, Sx = self.I, self.S
        sb = self.sb
        self.phase_begin()
        ckT = sb("C_ckT", [128, 4, CTX], BF16)
        ckM = sb("C_ckM", [128, 32, 512], BF16)
        kiT = sb("C_kiT", [128, CTX], BF16)
        wuv = sb("C_wuv", [128, HA, 4, 128], BF16)
        qiT = [sb("C_qiT%d" % i, [128, HA, 128], BF16) for i in range(2)]
        qlT = [sb("C_qlT%d" % i, [128, HA, 4, 128], BF16) for i in range(2)]
        wi = [sb("C_wi%d" % i, [128, 48], F32) for i in range(2)]
        wab = [sb("C_wab%d" % i, [128, 16], F32) for i in range(2)]
        wsg = [sb("C_wsg%d" % i, [128, 16], F32) for i in range(2)]
        sc = sb("C_sc", [128, CTX], F32)
        selm = sb("C_selm", [128, CTX], BF16)
        negm = [sb("C_negm%d" % i, [128, 32, 128], BF16) for i in range(2)]
        rl = [sb("C_rl%d" % i, [128, 512], F32) for i in range(2)]
        st = sb("C_st", [128, 8], F32)
        Pm = [sb("C_P%d" % i, [128, 4, 128], BF16) for i in range(3)]
        rden = sb("C_rden", [128, 512], F32)
        olat = sb("C_olat", [128, 4, 512], BF16)
        yaT = [sb("C_yaT%d" % i, [128, HA, 128], BF16) for i in range(2)]
        for cc in range(4):
            self.dma(ckT[:, cc, :], Sx["s_ckvnT"][cc], [("s_ckvnT", cc, g) for g in range(8)], [("C_ckT", cc)])
        self.dma(kiT[:], Sx["s_kiT"], [("s_kiT", g) for g in range(8)], ["C_kiT"])
        for h0 in range(0, HA, 4):
            self.dma(wuv[:, h0:h0 + 4, :, :], I["w_uv"][h0:h0 + 4].rearrange("h (cc p) e -> p h cc e", p=128), [], [("C_wuv", h0)], q="pool")
        for kb in range(32):
            b = self.bank()
            pT = self.ps[b][:].bitcast(BF16)
            self.tgroup([(pT[:, cc * 128:(cc + 1) * 128], ckT[:, cc, kb * 128:(kb + 1) * 128], self.identb[:]) for cc in range(4)],
                        [("C_ckT", cc) for cc in range(4)] + ["identb"], [("ps", b)])
            self.copy(self.evac_engine(), ckM[:, kb, :], pT[:, 0:512], [("ps", b)], [("C_ckM", kb)])
        ckres = [("C_ckT", cc) for cc in range(4)]
        CM = self.C(4)
        SCALE = 128.0 ** -0.5
        PACC = [0, 1, 2, 3]
        PDEN = 4
        PLOG = [5, 6]
        PMISC = 7
        self._lrr = 0

        def score_gen(qt):
            T = 16 + qt
            nkb = T + 1
            nk = nkb * 128
            p2 = qt % 2
            q0 = qt * 128
            self.dma(qiT[p2][:], Sx["s_qiT"][:, :, q0:q0 + 128].rearrange("h d t -> d h t"),
                     [("s_qiT", h, qt // 4) for h in range(HA)], [("C_qiT", p2)])
            self.uid += 1
            for h0 in range(0, HA, 4):
                self.dma(qlT[p2][:, h0:h0 + 4, :, :], Sx["s_qlT"][h0:h0 + 4, :, :, q0:q0 + 128].rearrange("h cc c t -> c h cc t"),
                         [("s_qlT", h, cc, qt // 4) for h in range(h0, h0 + 4) for cc in range(4)], [("C_qlT", p2)], gid=("ql", self.uid))
            self.dma(wi[p2][:], Sx["s_small"][OWN0 + q0:OWN0 + q0 + 128, :], [("s_small", T // 4, T % 4)], [("C_wi", p2)])
            self.act(wab[p2][:], wi[p2][:, 0:16], AF.Abs, [("C_wi", p2)], [("C_wab", p2)], scale=SCALE)
            self.act(wsg[p2][:], wi[p2][:, 0:16], AF.Sign, [("C_wi", p2)], [("C_wsg", p2)])
            yield
            nch = (nk + 511) // 512
            for ch in range(nch):
                k0 = ch * 512
                kw = min(512, nk - k0)
                for h in range(HA):
                    b = PMISC
                    self.mm(self.ps[b][:, 0:kw], qiT[p2][:, h, :], kiT[:, k0:k0 + kw], True, True, [("C_qiT", p2), "C_kiT"], [("ps", b)])
                    r = rl[h % 2]
                    self.act(r[:, 0:kw], self.ps[b][:, 0:kw], AF.Relu, [("ps", b), ("C_wab", p2)], [("C_rl", h % 2)], scale=wab[p2][:, h:h + 1])
                    if h == 0:
                        self.ts("dve", sc[:, k0:k0 + kw], r[:, 0:kw], wsg[p2][:, 0:1], None, ALU.mult, None,
                                [("C_rl", 0), ("C_wsg", p2)], [("C_sc", ch)])
                    else:
                        self.stt(sc[:, k0:k0 + kw], r[:, 0:kw], wsg[p2][:, h:h + 1], sc[:, k0:k0 + kw], ALU.mult, ALU.add,
                                 [("C_rl", h % 2), ("C_wsg", p2), ("C_sc", ch)], [("C_sc", ch)])
                    yield
            scres = [("C_sc", ch) for ch in range(nch)]
            self.s.op("dve", lambda e, nk=nk: e.tensor_reduce(st[:, 0:1], sc[:, 0:nk], AX.X, ALU.max, apply_absolute_value=True),
                      scres, [("C_st", 0)])
            self.ts("dve", sc[:, 0:OWN0], sc[:, 0:OWN0], self.pmt[:, 0:1], None, ALU.add, None, scres + ["pmt"], scres)
            self.tt("dve", sc[:, T * 128:(T + 1) * 128], sc[:, T * 128:(T + 1) * 128], CM, ALU.add, scres + ["cst"], scres)
            self.ts("dve", st[:, 1:2], st[:, 0:1], -1.0, -0.01, ALU.mult, ALU.add, [("C_st", 0)], [("C_st", 1)])
            self.ts("dve", st[:, 2:3], st[:, 0:1], 1.0, 0.01, ALU.mult, ALU.add, [("C_st", 0)], [("C_st", 2)])
            yield
            nh1 = (nkb // 2) * 128
            n2 = nk - nh1
            for it in range(20):
                self.tt("dve", st[:, 3:4], st[:, 1:2], st[:, 2:3], ALU.add, [("C_st", 1), ("C_st", 2)], [("C_st", 3)])
                self.ts("dve", st[:, 6:7], st[:, 3:4], -1.0, None, ALU.mult, None, [("C_st", 3)], [("C_st", 6)])
                self.s.op("act", lambda e, nh1=nh1, nk=nk: e.activation(out=selm[:, nh1:nk], in_=sc[:, nh1:nk], func=AF.Sign,
                                                                      bias=st[:, 6:7], accum_out=st[:, 7:8]),
                          scres + [("C_st", 6)], ["C_selm_b", ("C_st", 7)], noattach=True)
                self.ts("dve", selm[:, 0:nh1], sc[:, 0:nh1], st[:, 3:4], None, ALU.is_ge, ALU.add, scres + [("C_st", 3)],
                        ["C_selm", ("C_st", 4)], accum_out=st[:, 4:5])
                self.stt(st[:, 4:5], st[:, 7:8], 0.5, st[:, 4:5], ALU.mult, ALU.add, [("C_st", 7), ("C_st", 4)], [("C_st", 4)])
                self.stt(st[:, 5:6], st[:, 4:5], TOPK - 0.5 - 0.5 * n2, st[:, 2:3], ALU.is_ge, ALU.mult, [("C_st", 4), ("C_st", 2)], [("C_st", 5)])
                self.tt("dve", st[:, 1:2], st[:, 1:2], st[:, 5:6], ALU.add, [("C_st", 1), ("C_st", 5)], [("C_st", 1)])
                self.ts("dve", st[:, 2:3], st[:, 2:3], 0.5, None, ALU.mult, None, [("C_st", 2), ("C_st", 5), ("C_st", 3)], [("C_st", 2)])
                yield
            self.ts("dve", selm[:, 0:nk], sc[:, 0:nk], st[:, 1:2], None, ALU.is_ge, None, scres + [("C_st", 1)], ["C_selm", "C_selm_b"])
            yield
            for k8 in range(0, nkb, 8):
                n8 = min(8, nkb - k8)
                b = PMISC
                pT = self.ps[b][:].bitcast(BF16)
                self.tgroup([(pT[:, j * 128:(j + 1) * 128], selm[:, (k8 + j) * 128:(k8 + j + 1) * 128], self.identb[:]) for j in range(n8)],
                            ["C_selm", "identb"], [("ps", b)])
                self.ts("dve", negm[p2][:, k8:k8 + n8, :], pT[:, 0:n8 * 128].rearrange("p (c n) -> p c n", n=128), -1.0, 30000.0,
                        ALU.add, ALU.mult, [("ps", b)], [("C_negm", p2, k8)])
                yield

        def attn_gen(qt):
            T = 16 + qt
            nkb = T + 1
            p2 = qt % 2
            q0 = qt * 128
            mres = [("C_negm", p2, k8) for k8 in range(0, nkb, 8)]

            def L(hg, kb):
                b = PLOG[kb % 2]
                pl = self.ps[b][:, 0:512].rearrange("p (h q) -> p h q", q=128)
                items = [(pl, ckT[:, cc, kb * 128:(kb + 1) * 128], qlT[p2][:, hg * 4:(hg + 1) * 4, cc, :], cc == 0, False) for cc in range(4)]
                items += [(pl[:, j, :], self.identb[:], negm[p2][:, kb, :], False, j == 3) for j in range(4)]
                self.mmgroup(items, ckres + [("C_qlT", p2), "identb"] + mres, [("ps", b)])

            for hg in range(4):
                L(hg, 0)
                for kb in range(nkb):
                    if kb + 1 < nkb:
                        L(hg, kb + 1)
                    b = PLOG[kb % 2]
                    pl = self.ps[b][:, 0:512].rearrange("p (h q) -> p h q", q=128)
                    P = Pm[kb % 3]
                    self.act(P[:], pl, AF.Exp, [("ps", b)], [("C_P", kb % 3)], scale=SCALE)
                    yield
                    Pf = P[:].rearrange("p h q -> p (h q)")
                    items = [(self.ps[PACC[cc]][:, 0:512], ckM[:, kb, cc * 128:(cc + 1) * 128], Pf, kb == 0, kb == nkb - 1) for cc in range(4)]
                    items.append((self.ps[PDEN][:, 0:512], self.onesb[:], Pf, kb == 0, kb == nkb - 1))
                    self.mmgroup(items, [("C_ckM", kb), ("C_P", kb % 3), "onesb"], [("ps", PACC[cc]) for cc in range(4)] + [("ps", PDEN)])
                    yield
                self.s.op("dve", lambda e: e.reciprocal(rden[:], self.ps[PDEN][:, 0:512]), [("ps", PDEN)], ["C_rden"])
                for cc in range(4):
                    self.tt("dve", olat[:, cc, :], self.ps[PACC[cc]][:, 0:512], rden[:], ALU.mult, [("ps", PACC[cc]), "C_rden"], [("C_olat", cc)])
                for j in range(4):
                    h = hg * 4 + j
                    b = PMISC
                    self.mmgroup([(self.ps[b][:, 0:128], wuv[:, h, cc, :], olat[:, cc, j * 128:(j + 1) * 128], cc == 0, cc == 3) for cc in range(4)],
                                 [("C_wuv", (h // 4) * 4)] + [("C_olat", cc) for cc in range(4)], [("ps", b)])
                    self.copy("act", yaT[p2][:, h, :], self.ps[b][:, 0:128], [("ps", b)], [("C_yaT", p2, h)])
                yield
            self.dma(Sx["s_yT"][0:16, :, q0:q0 + 128].rearrange("h d t -> d h t"), yaT[p2][:],
                     [("C_yaT", p2, h) for h in range(HA)], [("s_yT", 0, qt)])

        order = list(range(15, -1, -1))
        for _ in score_gen(order[0]):
            pass
        for oi, qt in enumerate(order):
            ag = attn_gen(qt)
            sg = score_gen(order[oi + 1]) if oi + 1 < 16 else None
            for _ in ag:
                if sg is not None:
                    try:
                        next(sg)
                    except StopIteration:
                        sg = None
            if sg is not None:
                for _ in sg:
                    pass
        self.phase_end()


    def phase_D(self):
        I, Sx = self.I, self.S
        sb = self.sb
        self.phase_begin()
        TG = 1024
        NT = TG // 128
        XT = sb("D_XT", [128, 32, TG], BF16)
        Wb = [sb("D_W0", [128, 8192], BF16), sb("D_W1", [128, 8192], BF16)]
        xt2 = [sb("D_xt%d" % i, [128, D], F32) for i in range(2)]
        xn = sb("D_xn", [128, D], BF16)
        ss2 = [sb("D_ss%d" % i, [128, 1], F32) for i in range(2)]
        rstd2 = [sb("D_rstd%d" % i, [128, 1], F32) for i in range(2)]
        gB = sb("D_gB", [128, D], F32)
        blk = [sb("D_blk%d" % i, [128, 512], F32) for i in range(3)]
        obl = [sb("D_obl%d" % i, [128, 512], F32) for i in range(3)]
        qx = [sb("D_qx%d" % i, [128, 512], BF16) for i in range(2)]
        Px = [sb("D_Px%d" % i, [128, 512], BF16) for i in range(2)]
        rdx = sb("D_rdx", [128, 512], F32)
        oxT = sb("D_oxT", [128, 4, TG], BF16)
        sg = [[sb("D_sg%d_%d" % (i, j), [128, 512], BF16) for j in range(2)] for i in range(2)]
        ao = [sb("D_ao%d" % i, [128, 512], BF16) for i in range(3)]
        wo = I["w_o"].rearrange("(c p) n -> p c n", p=128)
        wcq = I["w_cq"].rearrange("(c p) n -> p c n", p=128)
        wco = I["w_co"].rearrange("(c p) n -> p c n", p=128)
        wfi = I["w_ffn_in"].rearrange("(c p) n -> p c n", p=128)
        SCALE = 128.0 ** -0.5
        allxt = [("D_XT", t) for t in range(NT)]
        for og in range(NOWN // TG):
            r0 = og * TG
            for half in range(2):
                for t4 in range(0, NT, 4):
                    self.dma(XT[:, half * 16:(half + 1) * 16, t4 * 128:(t4 + 4) * 128],
                             Sx["s_yT"][half * 16:(half + 1) * 16, :, r0 + t4 * 128:r0 + (t4 + 4) * 128].rearrange("c p t -> p c t"),
                             [("s_yT", half, (r0 // 128) + t4 + t) for t in range(4)], allxt, gid=("yT", og))
            xres = allxt

            def obody(view, wres, info, ncols):
                def epi(t, b):
                    self.uid += 1
                    i3 = self.uid % 3
                    rr = r0 + t * 128
                    self.dma(blk[i3][:, 0:ncols], I["xc"][OWN0 + rr:OWN0 + rr + 128, info:info + ncols], [], [("D_blk", i3)])
                    self.tt("dve", obl[i3][:, 0:ncols], self.ps[b][:, 0:ncols], blk[i3][:, 0:ncols], ALU.add, [("ps", b), ("D_blk", i3)], [("D_obl", i3)])
                    self.dma(Sx["s_h1"][rr:rr + 128, info:info + ncols], obl[i3][:, 0:ncols], [("D_obl", i3)], [("s_h1", og, t, info)])
                self.tm_chunk(XT, xres, 32, TG, view, wres, ncols, epi)
            self.linear(32, wo, [(j * 256, 256, j * 256) for j in range(16)], Wb, obody, nxt_first=(32, wcq, 0, 256, Wb))

            self.load_bcast(gB, I["cross_norm_g"], D, "D_gB")
            for t in range(NT):
                rr = r0 + t * 128
                self.norm_transpose_tile(Sx["s_h1"][rr:rr + 128, :], gB, XT, t, 32, (xt2[t % 2], xn, ss2[t % 2], rstd2[t % 2]), "D", par=t % 2,
                                         extra_reads=[("s_h1", og, t, j * 256) for j in range(16)])

            def cqbody(view, wres, info, ncols):
                def epi(h, b, hf):
                    i2 = (h * 2 + hf) % 2
                    self.copy("act", qx[i2][:], self.ps[b][:, 0:512], [("ps", b)], [("D_qx", i2)])
                    bd = self.bank()
                    bo = self.bank()
                    for mc in range(2):
                        bl = self.bank()
                        self.mm(self.ps[bl][:, 0:512], self.kxT[:, h, mc * 128:(mc + 1) * 128], qx[i2][:], True, True,
                                [("P_kxT", h), ("D_qx", i2)], [("ps", bl)])
                        self.act(Px[mc][:], self.ps[bl][:, 0:512], AF.Exp, [("ps", bl)], [("D_Px", mc)], scale=SCALE)
                        self.mm(self.ps[bd][:, 0:512], self.onesb[:], Px[mc][:], mc == 0, mc == 1, ["onesb", ("D_Px", mc)], [("ps", bd)])
                        self.mm(self.ps[bo][:, 0:512], self.vx[:, mc, h * 128:(h + 1) * 128], Px[mc][:], mc == 0, mc == 1,
                                [("P_vx", mc, 0), ("P_vx", mc, 256), ("D_Px", mc)], [("ps", bo)])
                    self.s.op("dve", lambda e, bd=bd: e.reciprocal(rdx[:], self.ps[bd][:, 0:512]), [("ps", bd)], ["D_rdx"])
                    self.tt("dve", oxT[:, h, hf * 512:(hf + 1) * 512], self.ps[bo][:, 0:512], rdx[:], ALU.mult, [("ps", bo), "D_rdx"], [("D_oxT", h, hf)])
                self.fm_chunk(XT, xres, 32, TG, view, wres, ncols, epi, info)
            self.linear(32, wcq, [(0, 256, 0), (256, 256, 2)], Wb, cqbody, nxt_first=(4, wco, 0, 512, Wb))
            oxres = [("D_oxT", h, hf) for h in range(4) for hf in range(2)]

            def cobody(view, wres, info, ncols):
                def epi(t, b):
                    self.uid += 1
                    i3 = self.uid % 3
                    rr = r0 + t * 128
                    self.dma(blk[i3][:, 0:ncols], Sx["s_h1"][rr:rr + 128, info:info + ncols],
                             [("s_h1", og, t, info), ("s_h1", og, t, info + 256)], [("D_blk", i3)])
                    self.tt("dve", obl[i3][:, 0:ncols], self.ps[b][:, 0:ncols], blk[i3][:, 0:ncols], ALU.add, [("ps", b), ("D_blk", i3)], [("D_obl", i3)])
                    self.dma(Sx["s_h2"][rr:rr + 128, info:info + ncols], obl[i3][:, 0:ncols], [("D_obl", i3)], [("s_h2", og, t, info)])
                self.tm_chunk(oxT, oxres, 4, TG, view, wres, ncols, epi)
            self.linear(4, wco, [(j * 512, 512, j * 512) for j in range(8)], Wb, cobody, nxt_first=(32, wfi, 0, 256, Wb))

            self.load_bcast(gB, I["ffn_norm_g"], D, "D_gB")
            for t in range(NT):
                rr = r0 + t * 128
                self.norm_transpose_tile(Sx["s_h2"][rr:rr + 128, :], gB, XT, t, 32, (xt2[t % 2], xn, ss2[t % 2], rstd2[t % 2]), "D", par=t % 2,
                                         extra_reads=[("s_h2", og, t, j * 512) for j in range(8)])
            chunks = []
            for j in range(43):
                chunks.append((j * 256, 256, ("g", j)))
                chunks.append((DFF + j * 256, 256, ("u", j)))

            def fbody(view, wres, info, ncols):
                kind, j = info

                def epi(sub, b, hf):
                    if kind == "g":
                        self.act(sg[sub][hf][:], self.ps[b][:, 0:512], AF.Silu, [("ps", b)], [("D_sg", sub, hf)])
                    else:
                        self.uid += 1
                        i3 = self.uid % 3
                        self.tt("dve", ao[i3][:], self.ps[b][:, 0:512], sg[sub][hf][:], ALU.mult, [("ps", b), ("D_sg", sub, hf)], [("D_ao", i3)])
                        self.dma(Sx["s_actT"][j * 2 + sub, :, r0 + hf * 512:r0 + (hf + 1) * 512], ao[i3][:], [("D_ao", i3)],
                                 [("s_actT", j * 2 + sub, og, hf)])
                self.fm_chunk(XT, xres, 32, TG, view, wres, ncols, epi, 0)
            self.linear(32, wfi, chunks, Wb, fbody, nxt_first=(32, wo, 0, 256, Wb) if og == 0 else None)
        self.phase_end()

    def phase_E(self):
        I, Sx = self.I, self.S
        sb = self.sb
        self.phase_begin()
        TG = 1024
        KH = 43
        XT = sb("E_XT", [128, KH, TG], BF16)
        Wb = [sb("E_W0", [128, KH * 256], BF16), sb("E_W1", [128, KH * 256], BF16)]
        of = [sb("E_of%d" % i, [128, 512], F32) for i in range(2)]
        hb = [sb("E_hb%d" % i, [128, 4, 128], F32) for i in range(2)]
        h3 = [sb("E_h3%d" % i, [128, 4, 128], F32) for i in range(2)]
        IDf = self.C(0)
        for og in range(NOWN // TG):
            r0 = og * TG
            for kh in range(2):
                wfo = I["w_ffn_out"][kh * KH * 128:(kh + 1) * KH * 128, :].rearrange("(c p) n -> p c n", p=128)
                self.uid += 1
                gidx = ("EXT", self.uid)
                for k0 in range(0, KH, 8):
                    k1 = min(KH, k0 + 8)
                    self.dma(XT[:, k0:k1, :], Sx["s_actT"][kh * KH + k0:kh * KH + k1, :, r0:r0 + TG].rearrange("c p t -> p c t"),
                             [("s_actT", c, og, hf) for c in range(kh * KH + k0, kh * KH + k1) for hf in range(2)], ["E_XT"], gid=gidx)
                xres = ["E_XT"]

                def body(view, wres, info, ncols, kh=kh, r0=r0, og=og):
                    def epi(c, b, hf):
                        self.uid += 1
                        i2 = self.uid % 2
                        rr = r0 + hf * 512
                        self.copy("act", of[i2][:], self.ps[b][:, 0:512], [("ps", b)], [("E_of", i2)])
                        return lambda: etail(c, hf, i2, rr)

                    def etail(c, hf, i2, rr):
                        b2 = self.bank()
                        self.tgroup([(self.ps[b2][:, t * 128:(t + 1) * 128], of[i2][:, t * 128:(t + 1) * 128], IDf) for t in range(4)],
                                    [("E_of", i2), "cst"], [("ps", b2)])
                        if kh == 0:
                            self.dma(hb[i2][:], Sx["s_h2"][rr:rr + 512, c * 128:(c + 1) * 128].rearrange("(t p) n -> p t n", p=128),
                                     [("s_h2", og, hf * 4 + t, (c // 4) * 512) for t in range(4)], [("E_hb", i2)])
                        else:
                            self.dma(hb[i2][:], Sx["s_h3"][rr:rr + 512, c * 128:(c + 1) * 128].rearrange("(t p) n -> p t n", p=128),
                                     [("s_h3", og, hf, c)], [("E_hb", i2)])
                        self.tt("dve", h3[i2][:], self.ps[b2][:, 0:512].rearrange("p (t n) -> p t n", n=128), hb[i2][:], ALU.add,
                                [("ps", b2), ("E_hb", i2)], [("E_h3", i2)])
                        self.dma(Sx["s_h3"][rr:rr + 512, c * 128:(c + 1) * 128].rearrange("(t p) n -> p t n", p=128), h3[i2][:],
                                 [("E_h3", i2)], [("s_h3", og, hf, c)])
                    self.fm_chunk(XT, xres, KH, TG, view, wres, ncols, epi, info)
                nkh, nog = (kh + 1) % 2, og + (kh + 1) // 2
                nf = None
                if nog < NOWN // TG:
                    wfn = I["w_ffn_out"][nkh * KH * 128:(nkh + 1) * KH * 128, :].rearrange("(c p) n -> p c n", p=128)
                    nf = (KH, wfn, 0, 256, Wb)
                self.linear(KH, wfo, [(j * 256, 256, j * 2) for j in range(16)], Wb, body, nxt_first=nf)
        self.phase_end()
        self.phase_begin()
        xt = [sb("F_xt%d" % i, [128, D], F32) for i in range(2)]
        xo = [sb("F_xo%d" % i, [128, D], F32) for i in range(2)]
        jk = sb("F_jk", [128, D], BF16)
        ss = [sb("F_ss%d" % i, [128, 1], F32) for i in range(2)]
        rs = [sb("F_rs%d" % i, [128, 1], F32) for i in range(2)]
        gB = sb("F_gB", [128, D], F32)
        self.load_bcast(gB, I["final_norm_g"], D, "F_gB")
        for t in range(16):
            i2 = t % 2
            self.dma(xt[i2][:], Sx["s_h3"][t * 128:(t + 1) * 128, :], [("s_h3", t // 8, (t % 8) // 4, c) for c in range(32)], [("F_xt", i2)])
            self.stt(jk[:], xt[i2][:], 1.0, xt[i2][:], ALU.mult, ALU.mult, [("F_xt", i2)], ["F_jk", ("F_ss", i2)], accum_out=ss[i2][:])
            self.act(rs[i2][:], ss[i2][:], AF.Sqrt, [("F_ss", i2)], [("F_rs", i2)], scale=1.0 / D, bias=EPS)
            self.s.op("dve", lambda e, r=rs[i2]: e.reciprocal(r[:], r[:]), [("F_rs", i2)], [("F_rs", i2)])
            self.stt(xo[i2][:], xt[i2][:], rs[i2][:], gB[:], ALU.mult, ALU.mult, [("F_xt", i2), ("F_rs", i2), "F_gB"], [("F_xo", i2)])
            o = self.dma(self.out[t * 128:(t + 1) * 128, :], xo[i2][:], [("F_xo", i2)], [("out", t)])
            self.out_ops.append(o)
        self.phase_end()


def _consts():
    c = np.zeros((128, 8 * 128), np.float32)
    i = np.arange(128)
    c[:, 0:128] = np.eye(128)
    c[:, 128:256] = 1.0
    c[:, 256:384] = (i[:, None] <= i[None, :])
    c[:, 384:512] = (i[:, None] > i[None, :])
    c[:, 512:640] = np.where((i[:, None] < 64) & (i[None, :] >= 64), -1e30, 0.0)
    c[:, 640:768] = -1.0 * (i[:, None] > i[None, :])
    c[:, 768:896] = (i[:, None] >= i[None, :])
    return c


def build_program(stop_after="all", dbg=()):
    B = Builder(stop_after, dbg)
    B.declare()
    order = ["0", "A", "B", "C", "D", "E"]
    for ph in order:
        getattr(B, "phase_" + ph)()
        if stop_after == ph:
            break
    B.s.barrier()
    finals = list(B.s.bar_deps)
    B.s.emit(finals)
    return B


def make_in_maps(inp):
    f = lambda a: np.ascontiguousarray(np.asarray(a, dtype=np.float32))
    x = f(inp["x"])
    shared = {
        "consts": _consts(),
        "w_in": f(inp["w_in"][0]), "w_qb": f(inp["w_qb"][0]), "w_iq": f(inp["w_iq"][0]),
        "wukT": f(np.transpose(inp["w_uk"][0], (2, 0, 1))),
        "w_uv": f(inp["w_uv"][0]),
        "gcols": f(np.concatenate([np.asarray(inp["qa_norm_g"][0]).reshape(12, 128).T,
                                   np.asarray(inp["kv_norm_g"][0]).reshape(4, 128).T], axis=1)),
        "cwl": f(np.transpose(np.asarray(inp["conv_w"][0]).reshape(4, 48, 128), (2, 1, 0)).reshape(128, 192)),
        "w_o": f(inp["w_o"][0]), "w_cq": f(inp["w_cq"][0]), "w_ckv": f(inp["w_ckv"][0]), "w_co": f(inp["w_co"][0]),
        "w_ffn_in": f(inp["w_ffn_in"][0]), "w_ffn_out": f(inp["w_ffn_out"][0]),
        "attn_norm_g": f(inp["attn_norm_g"]).reshape(1, D), "a_log": f(inp["a_log"]).reshape(1, HB),
        "dt_bias": f(inp["dt_bias"]).reshape(1, HB), "delta_norm_g": f(inp["delta_norm_g"]).reshape(1, 128),
        "cross_norm_g": f(inp["cross_norm_g"]).reshape(1, D), "mem_norm_g": f(inp["mem_norm_g"]).reshape(1, D),
        "ffn_norm_g": f(inp["ffn_norm_g"]).reshape(1, D), "final_norm_g": f(inp["final_norm_g"]).reshape(1, D),
    }
    maps = []
    for c in range(8):
        b, hf = c // 2, c % 2
        xc = np.zeros((CTX, D), np.float32)
        if hf == 1:
            xc[:] = x[b]
        else:
            xc[OWN0:] = x[b, 0:NOWN]
        m = dict(shared)
        m["xc"] = xc
        m["memb"] = f(inp["mem"][b])
        m["pm"] = np.full((128, 1), 0.0 if hf == 1 else -1e30, np.float32)
        maps.append(m)
    return maps


def kernel(**inputs):
    B = build_program()
    maps = make_in_maps(inputs)
    res = run_bass_kernel_spmd(B.nc, maps, core_ids=list(range(8)))
    out = np.zeros((NB, S, D), np.float32)
    for c in range(8):
        b, hf = c // 2, c % 2
        out[b, hf * NOWN:(hf + 1) * NOWN] = res.results[c]["out"]
    return out
```

```python
import numpy as np
import concourse.bass as bass
import concourse.mybir as mybir
from concourse.bass_utils import run_bass_kernel_spmd

F32 = mybir.dt.float32
BF16 = mybir.dt.bfloat16
F32R = mybir.dt.float32r
AF = mybir.ActivationFunctionType
ALU = mybir.AluOpType
AX = mybir.AxisListType

D = 4096
S = 4096
NB = 4
CTX = 4096
OWN0 = 2048
NOWN = 2048
EPS = 1e-6
QL = 1536
KVL = 512
HA = 16
HB = 16
NIN = 10416
QA0, CKV0, KI0, WI0, QKV0, Z0, B0, A0 = 0, 1536, 2048, 2176, 2192, 8336, 10384, 10400
DFF = 11008
NMEM = 256
TOPK = 256
NEG = -30000.0


class Op:
    __slots__ = ("eng", "fn", "deps", "signal", "sem", "val", "dma", "prewait", "idx", "noattach")


class Sched:
    ENGS = ("pe", "act", "dve", "pool", "sp")
    NDMA = {"sp": 24, "pool": 12, "act": 0, "dve": 0, "pe": 0}

    def __init__(self, nc):
        self.nc = nc
        self.q = {e: [] for e in self.ENGS}
        self.last_w = {}
        self.readers = {}
        self.last_gid = {}
        self.gdeps = {}
        self.n = 0
        self.bar_id = 0
        self.bar_deps = []
        self.bar_done = {e: 0 for e in self.ENGS}

    def op(self, eng, fn, reads=(), writes=(), dma=False, gid=None, noattach=False):
        o = Op()
        o.noattach = noattach
        o.eng, o.fn, o.dma, o.signal, o.sem, o.val, o.prewait = eng, fn, dma, dma, None, 0, None
        o.idx = self.n
        self.n += 1
        deps = []
        for r in reads:
            deps.extend(self.last_w.get(r, ()))
        for w_ in writes:
            if gid is not None and self.last_gid.get(w_) == gid:
                deps.extend(self.gdeps[w_])
            else:
                g = list(self.last_w.get(w_, ())) + list(self.readers.get(w_, ()))
                deps.extend(g)
                if gid is not None:
                    self.gdeps[w_] = g
        if self.bar_done[eng] < self.bar_id:
            deps.extend(self.bar_deps)
            self.bar_done[eng] = self.bar_id
        seen = set()
        dd = []
        for d_ in deps:
            if id(d_) in seen:
                continue
            seen.add(id(d_))
            if d_.eng == "pe" and eng == "pe" and not d_.dma:
                continue
            dd.append(d_)
            d_.signal = True
        o.deps = dd
        for w_ in writes:
            if gid is not None and self.last_gid.get(w_) == gid:
                self.last_w[w_].append(o)
            else:
                self.last_w[w_] = [o]
                self.readers[w_] = []
                self.last_gid[w_] = gid
        for r in reads:
            self.readers.setdefault(r, []).append(o)
        self.q[eng].append(o)
        return o

    def barrier(self):
        deps = []
        for e in self.ENGS:
            ops = self.q[e]
            comp = [o for o in ops if not o.dma]
            if comp:
                deps.append(comp[-1])
            dm = [o for o in ops if o.dma]
            deps.extend(dm[-self.NDMA[e]:] if self.NDMA[e] else [])
        for d_ in deps:
            d_.signal = True
        self.bar_id += 1
        self.bar_deps = deps

    def emit(self, final_waits):
        nc = self.nc
        engobj = {"pe": nc.tensor, "act": nc.scalar, "dve": nc.vector, "pool": nc.gpsimd, "sp": nc.sync}
        NDMA = self.NDMA
        csem = {e: nc.alloc_semaphore("c_" + e) for e in self.ENGS}
        dsem = {e: [nc.alloc_semaphore("d_%s%d" % (e, i)) for i in range(NDMA[e])] for e in self.ENGS}
        for e in self.ENGS:
            cnt = 0
            k = 0
            for o in self.q[e]:
                if o.dma:
                    n = NDMA[e]
                    j = k % n
                    o.sem = dsem[e][j]
                    o.val = 16 * (k // n + 1)
                    if k >= n:
                        o.prewait = (dsem[e][j], 16 * (k // n))
                    k += 1
                elif o.signal:
                    cnt += 1
                    o.sem = csem[e]
                    o.val = cnt
        with nc.Block() as block:
            def run(e):
                def body(eng):
                    waited = {}
                    for o in self.q[e]:
                        ws = []
                        if o.prewait is not None:
                            ws.append(o.prewait)
                        for d_ in o.deps:
                            ws.append((d_.sem, d_.val))
                        need = []
                        for (sm, v) in ws:
                            if waited.get(sm.num, 0) >= v:
                                continue
                            waited[sm.num] = v
                            need.append((sm, v))
                        attach = None
                        if need and not o.dma and not o.noattach:
                            attach = need.pop()
                        for (sm, v) in need:
                            eng.wait_ge(sm, v)
                        ins = o.fn(eng)
                        first = last = ins
                        if isinstance(ins, tuple):
                            first, last = ins
                        if attach is not None:
                            first._wait_ge(attach[0], attach[1])
                        if o.signal:
                            last.then_inc(o.sem, 16 if o.dma else 1)
                    if e == "sp":
                        for o in final_waits:
                            if waited.get(o.sem.num, 0) < o.val:
                                waited[o.sem.num] = o.val
                                eng.wait_ge(o.sem, o.val)
                return body
            block.tensor(run("pe"))
            block.scalar(run("act"))
            block.vector(run("dve"))
            block.gpsimd(run("pool"))
            block.sync(run("sp"))


class Builder:
    def __init__(self, stop_after="all", dbg=()):
        self.nc = bass.Bass("TRN2", target_bir_lowering=False)
        self.s = Sched(self.nc)
        self.stop_after = stop_after
        self.dbg = set(dbg)
        self.uid = 0
        self.out_ops = []
        self.evac_rr = 0
        self.wpar = 0
        self.tails = []
        self.preloaded = None

    def din(self, name, shape, dt=F32):
        return self.nc.dram_tensor(name, list(shape), dt, kind="ExternalInput").ap()

    def dscr(self, name, shape, dt):
        kind = "ExternalOutput" if name in self.dbg else "Internal"
        return self.nc.dram_tensor(name, list(shape), dt, kind=kind).ap()

    def sb(self, name, shape, dt):
        return self.nc.alloc_sbuf_tensor(name, list(shape), dt)

    def dma(self, out, in_, reads, writes, q="sp", gid=None):
        o = self.s.op(q, lambda e, out=out, in_=in_: e.dma_start(out=out, in_=in_), reads, writes, dma=True, gid=gid)
        return o

    def mm(self, out, lhsT, rhs, start, stop, reads, writes):
        return self.s.op("pe", lambda e: e.matmul(out, lhsT, rhs, start=start, stop=stop), reads, writes)

    def mmgroup(self, items, reads, writes):
        def fn(e, items=items):
            ins = None
            first = None
            for (o_, l_, r_, st, sp_) in items:
                ins = e.matmul(o_, l_, r_, start=st, stop=sp_)
                if first is None:
                    first = ins
            return (first, ins)
        return self.s.op("pe", fn, reads, writes)

    def tgroup(self, items, reads, writes):
        def fn(e, items=items):
            ins = None
            first = None
            for (o_, i_, id_) in items:
                ins = e.transpose(o_, i_, id_)
                if first is None:
                    first = ins
            return (first, ins)
        return self.s.op("pe", fn, reads, writes)

    def act(self, out, in_, func, reads, writes, **kw):
        return self.s.op("act", lambda e: e.activation(out=out, in_=in_, func=func, **kw), reads, writes)

    def evac_engine(self):
        self.evac_rr += 1
        return "act" if self.evac_rr % 2 else "dve"

    def copy(self, eng, out, in_, reads, writes):
        if eng == "act":
            return self.s.op("act", lambda e: e.copy(out=out, in_=in_), reads, writes)
        if eng == "dve":
            return self.s.op("dve", lambda e: e.tensor_copy(out=out, in_=in_), reads, writes)
        return self.s.op("pool", lambda e: e.tensor_copy(out=out, in_=in_), reads, writes)

    def ts(self, eng, out, in0, s1, s2, op0, op1, reads, writes, accum_out=None):
        if op1 is None:
            op1 = ALU.bypass
        if accum_out is None:
            return self.s.op(eng, lambda e: e.tensor_scalar(out, in0, s1, s2, op0, op1), reads, writes)
        return self.s.op(eng, lambda e: e.tensor_scalar(out, in0, s1, s2, op0, op1, accum_out), reads, writes, noattach=True)

    def tt(self, eng, out, in0, in1, op, reads, writes):
        return self.s.op(eng, lambda e: e.tensor_tensor(out, in0, in1, op), reads, writes)

    def stt(self, out, in0, scalar, in1, op0, op1, reads, writes, accum_out=None):
        if accum_out is None:
            return self.s.op("dve", lambda e: e.scalar_tensor_tensor(out, in0, scalar, in1, op0, op1), reads, writes)
        return self.s.op("dve", lambda e: e.scalar_tensor_tensor(out, in0, scalar, in1, op0, op1, accum_out), reads, writes, noattach=True)

    def memset(self, eng, ap, val, writes):
        return self.s.op(eng, lambda e: e.memset(ap, val), (), writes)

    def declare(self):
        nc = self.nc
        I = {}
        I["xc"] = self.din("xc", [CTX, D])
        I["memb"] = self.din("memb", [NMEM, D])
        I["pm"] = self.din("pm", [128, 1])
        I["consts"] = self.din("consts", [128, 8 * 128])
        for nm, shp in [("w_in", [D, NIN]), ("w_qb", [QL, 2048]), ("w_iq", [QL, 2048]), ("wukT", [128, HA, KVL]), ("gcols", [128, 16]), ("cwl", [128, 192]),
                        ("w_uv", [HA, KVL, 128]), ("w_o", [D, D]), ("w_cq", [D, 512]),
                        ("w_ckv", [D, 1024]), ("w_co", [512, D]), ("w_ffn_in", [D, 2 * DFF]), ("w_ffn_out", [DFF, D]),
                        ("attn_norm_g", [1, D]), ("a_log", [1, HB]),
                        ("dt_bias", [1, HB]), ("delta_norm_g", [1, 128]), ("cross_norm_g", [1, D]),
                        ("mem_norm_g", [1, D]), ("ffn_norm_g", [1, D]), ("final_norm_g", [1, D])]:
            I[nm] = self.din(nm, shp)
        self.I = I
        Sx = {}
        Sx["s_ckvnT"] = self.dscr("s_ckvnT", [4, 128, CTX], BF16)
        Sx["s_kiT"] = self.dscr("s_kiT", [128, CTX], BF16)
        Sx["s_small"] = self.dscr("s_small", [CTX, 48], F32)
        Sx["s_zs"] = self.dscr("s_zs", [NOWN, 2048], BF16)
        Sx["s_qiT"] = self.dscr("s_qiT", [HA, 128, NOWN], BF16)
        Sx["s_yT"] = self.dscr("s_yT", [32, 128, NOWN], BF16)
        Sx["s_h2"] = self.dscr("s_h2", [NOWN, D], F32)
        self.out = nc.dram_tensor("out", [NOWN, D], F32, kind="ExternalOutput").ap()
        if self.dbg:
            Sx["s_gq"] = self.dscr("s_gq", [HB, 128, CTX], BF16)
            Sx["s_gk"] = self.dscr("s_gk", [HB, 128, CTX], BF16)
            Sx["s_gv"] = self.dscr("s_gv", [HB, 128, CTX], BF16)
            Sx["s_qlT"] = self.dscr("s_qlT", [HA, 4, 128, NOWN], BF16)
            Sx["s_h1"] = self.dscr("s_h1", [NOWN, D], F32)
            Sx["s_h3"] = self.dscr("s_h3", [NOWN, D], F32)
            Sx["s_actT"] = self.dscr("s_actT", [86, 128, NOWN], BF16)
        else:
            n1 = HB * 128 * CTX
            p1 = nc.dram_tensor("pool1", [3 * n1], BF16, kind="Internal").ap()
            Sx["s_gq"] = p1[0:n1].rearrange("(h d t) -> h d t", h=HB, d=128)
            Sx["s_gk"] = p1[n1:2 * n1].rearrange("(h d t) -> h d t", h=HB, d=128)
            Sx["s_gv"] = p1[2 * n1:3 * n1].rearrange("(h d t) -> h d t", h=HB, d=128)
            Sx["s_actT"] = p1[0:86 * 128 * NOWN].rearrange("(c p t) -> c p t", c=86, p=128)
            p2 = nc.dram_tensor("pool2", [NOWN * D], F32, kind="Internal").ap()
            Sx["s_h1"] = p2.rearrange("(r c) -> r c", c=D)
            Sx["s_qlT"] = p2.bitcast(BF16).rearrange("(h cc c t) -> h cc c t", h=HA, cc=4, c=128)
            Sx["s_h3"] = self.out
        self.S = Sx
        self.ps = [nc.alloc_psum_tensor("ps%d" % i, [128, 512], F32) for i in range(8)]
        self.ps_rr = 0
        self.cst = self.sb("cst", [128, 8 * 128], F32)
        self.dma(self.cst[:], I["consts"], [], ["cst"])
        self.identb = self.sb("identb", [128, 128], BF16)
        self.onesb = self.sb("onesb", [128, 128], BF16)
        self.copy("dve", self.identb[:], self.cst[:, 0:128], ["cst"], ["identb"])
        self.copy("dve", self.onesb[:], self.cst[:, 128:256], ["cst"], ["onesb"])
        self.pmt = self.sb("pmt", [128, 1], F32)
        self.dma(self.pmt[:], I["pm"], [], ["pmt"])

    def C(self, i):
        return self.cst[:, i * 128:(i + 1) * 128]

    def bank(self):
        b = self.ps_rr % 8
        self.ps_rr += 1
        return b

    def rstd_from_ss(self, rstd, ss, n, tag):
        self.act(rstd, ss, AF.Sqrt, [tag + "_ss"], [tag + "_rstd"], scale=1.0 / n, bias=EPS)
        self.s.op("dve", lambda e: e.reciprocal(rstd, rstd), [tag + "_rstd"], [tag + "_rstd"])

    def wstream_load(self, par, Wb, Wv, KC, c0, ncols):
        buf = Wb[par]
        view = buf[:, 0:KC * ncols].rearrange("p (c n) -> p c n", n=ncols)
        step = max(1, 2048 // ncols)
        self.uid += 1
        gid = ("wl", self.uid)
        r = ("W", buf.name if hasattr(buf, "name") else id(buf), par)
        k0 = 0
        while k0 < KC:
            k1 = min(KC, k0 + step)
            self.dma(view[:, k0:k1, :], Wv[:, k0:k1, c0:c0 + ncols], [], [r], q="pool", gid=gid)
            k0 = k1
        return view, [r]

    def norm_transpose_tile(self, src_rows, gB, XT, t, KC, bufs, tag, extra_writes=(), extra_reads=(), par=None):
        xt, xn, ss, rstd = bufs
        W = KC * 128
        ptag = tag if par is None else "%s%d" % (tag, par)
        gtag = tag + "_gB"
        self.dma(xt[:, 0:W], src_rows, list(extra_reads), [ptag + "_xt"] + list(extra_writes))
        self.stt(xn[:, 0:W], xt[:, 0:W], 1.0, xt[:, 0:W], ALU.mult, ALU.mult, [ptag + "_xt"],
                 [tag + "_xn", ptag + "_ss"] + list(extra_writes), accum_out=ss[:])
        self.act(rstd[:], ss[:], AF.Sqrt, [ptag + "_ss"], [ptag + "_rstd"], scale=1.0 / W, bias=EPS)
        self.s.op("dve", lambda e: e.reciprocal(rstd[:], rstd[:]), [ptag + "_rstd"], [ptag + "_rstd"])
        self.stt(xn[:, 0:W], xt[:, 0:W], rstd[:], gB[:, 0:W], ALU.mult, ALU.mult,
                 [ptag + "_xt", ptag + "_rstd", gtag], [tag + "_xn"])
        for k8 in range(0, KC, 8):
            n8 = min(8, KC - k8)
            b = self.bank()
            pT = self.ps[b][:].bitcast(BF16)
            items = [(pT[:, j * 128:(j + 1) * 128], xn[:, (k8 + j) * 128:(k8 + j + 1) * 128], self.identb[:]) for j in range(n8)]
            self.tgroup(items, [tag + "_xn", "identb"], [("ps", b)])
            dst = XT[:, k8:k8 + n8, t * 128:(t + 1) * 128]
            src = pT[:, 0:n8 * 128].rearrange("p (c n) -> p c n", n=128)
            self.copy(self.evac_engine(), dst, src, [("ps", b)], [(tag + "_XT", t)])

    def load_bcast(self, dst, src_row, W, tag):
        self.dma(dst[:, 0:W], src_row.partition_broadcast(128), [], [tag])

    def linear(self, KC, Wv, chunks, Wb, body, nxt_first=None):
        n = len(chunks)
        if n == 0:
            return
        c0, nc_, info = chunks[0]
        key = (id(Wb[0]), KC, c0, nc_, str(Wv.tensor.name) + str(Wv.offset))
        if self.preloaded is not None and self.preloaded[0] == key:
            cur = self.preloaded[1]
        else:
            cur = self.wstream_load(self.wpar, Wb, Wv, KC, c0, nc_)
        self.preloaded = None
        for i in range(n):
            nxt = None
            if i + 1 < n:
                c0n, ncn, infon = chunks[i + 1]
                nxt = self.wstream_load(1 - self.wpar, Wb, Wv, KC, c0n, ncn)
            elif nxt_first is not None:
                KCn, Wvn, c0n, ncn, Wbn = nxt_first
                if Wbn is Wb:
                    keyn = (id(Wb[0]), KCn, c0n, ncn, str(Wvn.tensor.name) + str(Wvn.offset))
                    self.preloaded = (keyn, self.wstream_load(1 - self.wpar, Wb, Wvn, KCn, c0n, ncn))
            view, wres = cur
            body(view, wres, chunks[i][2], chunks[i][1])
            cur = nxt
            self.wpar = 1 - self.wpar
        self.flush_tails()

    def fm_chunk(self, XT, xt_res, KC, TG, view, wres, ncols, epi, sub0):
        nh = (TG + 511) // 512
        for sub in range(ncols // 128):
            for hf in range(nh):
                t0 = hf * 512
                tw = min(512, TG - t0)
                b = self.bank()
                items = [(self.ps[b][:, 0:tw], view[:, k, sub * 128:(sub + 1) * 128], XT[:, k, t0:t0 + tw], k == 0, k == KC - 1)
                         for k in range(KC)]
                self.mmgroup(items, list(wres) + list(xt_res), [("ps", b)])
                self.flush_tails()
                if nh == 1:
                    tl = epi(sub0 + sub, b)
                else:
                    tl = epi(sub0 + sub, b, hf)
                if tl is not None:
                    self.tails.append(tl)

    def flush_tails(self):
        tl, self.tails = self.tails, []
        for f in tl:
            f()

    def tm_chunk(self, XT, xt_res, KC, TG, view, wres, ncols, epi):
        for t in range(TG // 128):
            b = self.bank()
            items = [(self.ps[b][:, 0:ncols], XT[:, k, t * 128:(t + 1) * 128], view[:, k, 0:ncols], k == 0, k == KC - 1)
                     for k in range(KC)]
            self.mmgroup(items, list(wres) + list(xt_res), [("ps", b)])
            epi(t, b)

    def phase_A(self):
        I, Sx = self.I, self.S
        sb = self.sb
        self.phase_begin()
        XT = sb("A_XT", [128, 32, 512], BF16)
        Wb = [sb("A_W0", [128, 8192], BF16), sb("A_W1", [128, 8192], BF16)]
        big = sb("A_big", [128, 6144], F32)
        xt = big[:, 0:4096]
        xn = big[:, 4096:6144].bitcast(BF16)
        ss = sb("A_ss", [128, 1], F32)
        rstd = sb("A_rstd", [128, 1], F32)
        gB = sb("A_gB", [128, D], F32)
        qaraw = big[:, 0:6144].rearrange("p (c n) -> p c n", n=512)
        qanT = sb("A_qanT", [128, 12, 512], BF16)
        ckvraw = sb("A_ckvraw", [128, 4, 512], F32)
        ckvn = sb("A_ckvn", [128, 4, 512], BF16)
        rsb = sb("A_rsb", [128, 512], F32)
        sqb = [sb("A_sq%d" % i, [128, 512], BF16) for i in range(3)]
        cbuf = [sb("A_cb%d" % i, [128, 516], F32) for i in range(3)]
        cacc = [sb("A_ca%d" % i, [128, 512], F32) for i in range(3)]
        csil = [sb("A_cs%d" % i, [128, 512], F32) for i in range(3)]
        cout = [sb("A_co%d" % i, [128, 512], BF16) for i in range(3)]
        rn = [sb("A_rn%d" % i, [128, 512], F32) for i in range(3)]
        carry = sb("A_carry", [128, 48, 4], F32)
        cw = sb("A_cw", [128, 48, 4], F32)
        Wsm = sb("A_Wsm", [128, 32, 48], BF16)
        smt = [sb("A_smt%d" % i, [128, 48], F32) for i in range(2)]
        smo = [sb("A_smo%d" % i, [128, 48], F32) for i in range(2)]
        abc = sb("A_abc", [128, 3, 16], F32)
        gqa = sb("A_gqa", [128, 12], F32)
        gkv = sb("A_gkv", [128, 4], F32)
        wukT = sb("A_wukT", [128, HA, 512], BF16)
        qTh = [sb("A_qTh%d" % i, [128, 512], BF16) for i in range(2)]
        ql = [sb("A_ql%d" % i, [128, 512], BF16) for i in range(3)]
        zso = [sb("A_zso%d" % i, [128, 256], BF16) for i in range(3)]
        kio = [sb("A_kio%d" % i, [128, 512], BF16) for i in range(2)]

        self.load_bcast(gB, I["attn_norm_g"], D, "A_gB")
        self.dma(gqa[:], I["gcols"][:, 0:12], [], ["A_gqa"])
        self.dma(gkv[:], I["gcols"][:, 12:16], [], ["A_gkv"])
        self.dma(cw[:], I["cwl"].rearrange("p (c k) -> p c k", k=4), [], ["A_cw"])
        self.memset("pool", carry[:], 0.0, ["A_carry"])
        self.dma(abc[:, 0, :], I["dt_bias"].partition_broadcast(128), [], ["A_abc0"])
        self.dma(abc[:, 2, :], I["a_log"].partition_broadcast(128), [], ["A_abc2"])
        self.act(abc[:, 1, :], abc[:, 2, :], AF.Exp, ["A_abc2"], ["A_abc1"])
        self.ts("dve", abc[:, 1, :], abc[:, 1, :], -1.0, None, ALU.mult, None, ["A_abc1"], ["A_abc1"])
        win = I["w_in"].rearrange("(c p) n -> p c n", p=128)
        self.dma(Wsm[:, :, 0:16], win[:, :, WI0:WI0 + 16], [], ["A_Wsm0"], q="pool")
        self.dma(Wsm[:, :, 16:48], win[:, :, B0:B0 + 32], [], ["A_Wsm1"], q="pool")
        for h in range(0, HA, 4):
            self.dma(wukT[:, h:h + 4, :], I["wukT"][:, h:h + 4, :], [], [("A_wukT", h + j) for j in range(4)], q="pool")
        wqb = I["w_qb"].rearrange("(c p) n -> p c n", p=128)
        wiq = I["w_iq"].rearrange("(c p) n -> p c n", p=128)

        for g in range(8):
            own = g >= 4
            og = g - 4
            T0 = g * 512
            xt_res = [("A_XT", t) for t in range(4)]
            for t in range(4):
                self.norm_transpose_tile(I["xc"][T0 + t * 128:T0 + (t + 1) * 128, :], gB, XT, t, 32,
                                         (xt, xn, ss, rstd), "A", extra_writes=[("A_qaraw", c) for c in range(12)])
            for t in range(4):
                b = self.bank()
                items = [(self.ps[b][:, 0:48], XT[:, k, t * 128:(t + 1) * 128], Wsm[:, k, :], k == 0, k == 31) for k in range(32)]
                self.mmgroup(items, ["A_Wsm0", "A_Wsm1", ("A_XT", t)], [("ps", b)])
                i2 = t % 2
                st, so = smt[i2], smo[i2]
                self.copy("dve", st[:], self.ps[b][:, 0:48], [("ps", b)], [("A_smt", i2)])
                self.ts("dve", so[:, 0:16], st[:, 0:16], 0.25, None, ALU.mult, None, [("A_smt", i2)], [("A_smo", i2, 0)])
                self.act(so[:, 16:32], st[:, 16:32], AF.Sigmoid, [("A_smt", i2)], [("A_smo", i2, 1)])
                self.tt("dve", st[:, 32:48], st[:, 32:48], abc[:, 0, :], ALU.add, [("A_smt", i2), "A_abc0"], [("A_smt", i2)])
                self.act(st[:, 32:48], st[:, 32:48], AF.Exp, [("A_smt", i2)], [("A_smt", i2)])
                self.act(st[:, 32:48], st[:, 32:48], AF.Ln, [("A_smt", i2)], [("A_smt", i2)], bias=1.0)
                self.tt("dve", so[:, 32:48], st[:, 32:48], abc[:, 1, :], ALU.mult, [("A_smt", i2), "A_abc1"], [("A_smo", i2, 2)])
                self.dma(Sx["s_small"][T0 + t * 128:T0 + (t + 1) * 128, :], so[:],
                         [("A_smo", i2, 0), ("A_smo", i2, 1), ("A_smo", i2, 2)], [("s_small", g, t)])

            chunks = []
            if own:
                for c in range(0, 12, 2):
                    chunks.append((QA0 + c * 128, 256, ("qa", c)))
            for c in range(0, 4, 2):
                chunks.append((CKV0 + c * 128, 256, ("ckv", c)))
            chunks.append((KI0, 128, ("ki", 0)))
            q_from = 0 if g >= 3 else 16
            for c in range(q_from, 48, 2):
                chunks.append((QKV0 + c * 128, 256, ("qkv", c)))

            def epi_factory(info):
                fam, cbase = info

                def epi(sub, b):
                    c = cbase + sub
                    P = self.ps[b][:, 0:512]
                    if fam == "qa":
                        self.copy("act", qaraw[:, c, :], P, [("ps", b)], [("A_qaraw", c)])
                    elif fam == "ckv":
                        self.copy("act", ckvraw[:, c, :], P, [("ps", b)], [("A_ckvraw", c)])
                    elif fam == "ki":
                        i2 = g % 2
                        self.copy("act", kio[i2][:], P, [("ps", b)], [("A_kio", i2)])
                        self.dma(Sx["s_kiT"][:, T0:T0 + 512], kio[i2][:], [("A_kio", i2)], [("s_kiT", g)])
                    else:
                        return self.qkv_epilogue(c, b, g, T0, cbuf, cacc, csil, cout, rn, sqb, carry, cw)
                    return None
                return epi

            def body(view, wres, info, ncols):
                self.fm_chunk(XT, xt_res, 32, 512, view, wres, ncols, epi_factory(info), 0)
            if own:
                nf = (32, win, Z0, 256, Wb)
            elif g + 1 < 8:
                nf = (32, win, QA0 if g + 1 >= 4 else CKV0, 256, Wb)
            else:
                nf = None
            self.linear(32, win, chunks, Wb, body, nxt_first=nf)

            self.fm_rmsnorm(ckvraw, "A_ckvraw", 4, KVL, gkv, "A_gkv", ckvn, "A_ckvn", sqb, rsb)
            for c in range(4):
                self.dma(Sx["s_ckvnT"][c, :, T0:T0 + 512], ckvn[:, c, :], [("A_ckvn", c)], [("s_ckvnT", c, g)])

            if own:
                def zbody(view, wres, info, ncols):
                    def epi(t, b):
                        self.uid += 1
                        i3 = self.uid % 3
                        self.act(zso[i3][:, 0:ncols], self.ps[b][:, 0:ncols], AF.Silu, [("ps", b)], [("A_zso", i3)])
                        r0 = og * 512 + t * 128
                        self.dma(Sx["s_zs"][r0:r0 + 128, info:info + ncols], zso[i3][:, 0:ncols], [("A_zso", i3)],
                                 [("s_zs", og, t, info)])
                    self.tm_chunk(XT, xt_res, 32, 512, view, wres, ncols, epi)
                self.linear(32, win, [(Z0 + j * 256, 256, j * 256) for j in range(8)], Wb, zbody, nxt_first=(12, wqb, 0, 512, Wb))

                self.fm_rmsnorm(qaraw, "A_qaraw", 12, QL, gqa, "A_gqa", qanT, "A_qanT", sqb, rsb)
                qres = [("A_qanT", c) for c in range(12)]

                def qbody(view, wres, info, ncols):
                    def epi(h, b):
                        i2 = h % 2
                        self.copy("act", qTh[i2][:], self.ps[b][:, 0:512], [("ps", b)], [("A_qTh", i2)])
                        for cc in range(4):
                            b2 = self.bank()
                            self.mm(self.ps[b2][:, 0:512], wukT[:, h, cc * 128:(cc + 1) * 128], qTh[i2][:], True, True,
                                    [("A_wukT", h), ("A_qTh", i2)], [("ps", b2)])
                            self.uid += 1
                            i3 = self.uid % 3
                            self.copy(self.evac_engine(), ql[i3][:], self.ps[b2][:, 0:512], [("ps", b2)], [("A_ql", i3)])
                            self.dma(Sx["s_qlT"][h, cc, :, og * 512:(og + 1) * 512], ql[i3][:], [("A_ql", i3)],
                                     [("s_qlT", h, cc, og)])
                    self.fm_chunk(qanT, qres, 12, 512, view, wres, ncols, epi, info)
                self.linear(12, wqb, [(j * 512, 512, j * 4) for j in range(4)], Wb, qbody, nxt_first=(12, wiq, 0, 512, Wb))

                def ibody(view, wres, info, ncols):
                    def epi(h, b):
                        self.uid += 1
                        i3 = self.uid % 3
                        self.copy(self.evac_engine(), ql[i3][:], self.ps[b][:, 0:512], [("ps", b)], [("A_ql", i3)])
                        self.dma(Sx["s_qiT"][h, :, og * 512:(og + 1) * 512], ql[i3][:], [("A_ql", i3)], [("s_qiT", h, og)])
                    self.fm_chunk(qanT, qres, 12, 512, view, wres, ncols, epi, info)
                self.linear(12, wiq, [(j * 512, 512, j * 4) for j in range(4)], Wb, ibody,
                            nxt_first=(32, win, QA0, 256, Wb) if g + 1 < 8 else None)
        self.phase_end()

    def fm_rmsnorm(self, raw, rawtag, KC, n, gcol, gtag, outT, outtag, sqb, rsb):
        b = self.bank()
        for c in range(KC):
            i2 = c % 2
            self.tt("dve", sqb[i2][:], raw[:, c, :], raw[:, c, :], ALU.mult, [(rawtag, c)], [("A_sq", i2)])
            self.mm(self.ps[b][:, 0:512], self.onesb[:], sqb[i2][:], c == 0, c == KC - 1, ["onesb", ("A_sq", i2)], [("ps", b)])
        self.act(rsb[:], self.ps[b][:, 0:512], AF.Sqrt, [("ps", b)], ["A_rsb"], scale=1.0 / n, bias=EPS)
        self.s.op("dve", lambda e: e.reciprocal(rsb[:], rsb[:]), ["A_rsb"], ["A_rsb"])
        for c in range(KC):
            self.stt(outT[:, c, :], raw[:, c, :], gcol[:, c:c + 1], rsb[:], ALU.mult, ALU.mult,
                     [(rawtag, c), gtag, "A_rsb"], [(outtag, c)])

    def qkv_epilogue(self, c, b, g, T0, cbuf, cacc, csil, cout, rn, sqb, carry, cw):
        Sx = self.S
        i2 = c % 3
        cb, ca, cs = cbuf[i2], cacc[i2], csil[i2]
        self.copy("pool", cb[:, 0:3], carry[:, c, 0:3], [("A_carry", c)], [("A_cb", i2, 0)])
        self.copy("act", cb[:, 3:515], self.ps[b][:, 0:512], [("ps", b)], [("A_cb", i2, 1)])
        rd = [("A_cb", i2, 0), ("A_cb", i2, 1), "A_cw"]
        self.ts("dve", ca[:], cb[:, 3:515], cw[:, c, 3:4], None, ALU.mult, None, rd, [("A_ca", i2)])
        for k in (2, 1, 0):
            self.stt(ca[:], cb[:, k:k + 512], cw[:, c, k:k + 1], ca[:], ALU.mult, ALU.add, rd + [("A_ca", i2)], [("A_ca", i2)])
        self.copy("pool", carry[:, c, 0:3], cb[:, 512:515], [("A_cb", i2, 1)], [("A_carry", c)])
        self.uid += 1
        i3 = self.uid % 3
        co = cout[i3]
        fam = c // 16
        h = c % 16
        if fam == 2:
            self.act(co[:], ca[:], AF.Silu, [("A_ca", i2)], [("A_co", i3)])
            self.dma(Sx["s_gv"][h, :, T0:T0 + 512], co[:], [("A_co", i3)], [("s_gv", h, g)])
            return
        self.act(cs[:], ca[:], AF.Silu, [("A_ca", i2)], [("A_cs", i2)])
        self.tt("dve", sqb[i2][:], cs[:], cs[:], ALU.mult, [("A_cs", i2)], [("A_sq", i2)])
        return lambda: self.qkv_tail(c, g, T0, i2, i3, co, cs, rn, sqb, fam, h)

    def qkv_tail(self, c, g, T0, i2, i3, co, cs, rn, sqb, fam, h):
        Sx = self.S
        b2 = self.bank()
        self.mm(self.ps[b2][:, 0:512], self.onesb[:], sqb[i2][:], True, True, ["onesb", ("A_sq", i2)], [("ps", b2)])
        r = rn[i2]
        self.act(r[:], self.ps[b2][:, 0:512], AF.Sqrt, [("ps", b2)], [("A_rn", i2)], scale=1.0, bias=EPS)
        self.s.op("dve", lambda e: e.reciprocal(r[:], r[:]), [("A_rn", i2)], [("A_rn", i2)])
        if fam == 0:
            self.stt(co[:], cs[:], 128.0 ** -0.5, r[:], ALU.mult, ALU.mult, [("A_cs", i2), ("A_rn", i2)], [("A_co", i3)])
            self.dma(Sx["s_gq"][h, :, T0:T0 + 512], co[:], [("A_co", i3)], [("s_gq", h, g)])
        else:
            self.tt("dve", co[:], cs[:], r[:], ALU.mult, [("A_cs", i2), ("A_rn", i2)], [("A_co", i3)])
            self.dma(Sx["s_gk"][h, :, T0:T0 + 512], co[:], [("A_co", i3)], [("s_gk", h, g)])


    def phase_begin(self):
        self._sb_mark = self.nc.sbuf_base

    def phase_end(self):
        self.s.barrier()
        self.nc.sbuf_base = self._sb_mark

    def phase_0(self):
        I = self.I
        sb = self.sb
        self.kxT = sb("P_kxT", [128, 4, NMEM], BF16)
        self.vx = sb("P_vx", [128, 2, 512], BF16)
        self.phase_begin()
        XT = sb("0_XT", [128, 32, 256], BF16)
        Wb = [sb("0_W0", [128, 8192], BF16), sb("0_W1", [128, 8192], BF16)]
        xt = sb("0_xt", [128, D], F32)
        xn = sb("0_xn", [128, D], BF16)
        ss = sb("0_ss", [128, 1], F32)
        rstd = sb("0_rstd", [128, 1], F32)
        gB = sb("0_gB", [128, D], F32)
        self.load_bcast(gB, I["mem_norm_g"], D, "0_gB")
        for t in range(2):
            self.norm_transpose_tile(I["memb"][t * 128:(t + 1) * 128, :], gB, XT, t, 32, (xt, xn, ss, rstd), "0")
        xres = [("0_XT", 0), ("0_XT", 1)]
        wckv = I["w_ckv"].rearrange("(c p) n -> p c n", p=128)

        def kbody(view, wres, info, ncols):
            def epi(h, b):
                self.copy("act", self.kxT[:, h, :], self.ps[b][:, 0:NMEM], [("ps", b)], [("P_kxT", h)])
            self.fm_chunk(XT, xres, 32, NMEM, view, wres, ncols, epi, info)
        self.linear(32, wckv, [(0, 256, 0), (256, 256, 2)], Wb, kbody)

        def vbody(view, wres, info, ncols):
            def epi(t, b):
                self.copy("act", self.vx[:, t, info:info + ncols], self.ps[b][:, 0:ncols], [("ps", b)], [("P_vx", t, info)])
            self.tm_chunk(XT, xres, 32, NMEM, view, wres, ncols, epi)
        self.linear(32, wckv, [(512, 256, 0), (768, 256, 256)], Wb, vbody)
        self.phase_end()

    def phase_B(self):
        I, Sx = self.I, self.S
        sb = self.sb
        self.phase_begin()
        S32 = sb("B_S32", [128, HB, 128], F32)
        Sb = sb("B_Sb", [128, HB, 128], BF16)
        gdnB = sb("B_gdnB", [128, 128], F32)
        kTa = [sb("B_kT%d" % i, [128, HB, 128], BF16) for i in range(2)]
        vTa = [sb("B_vT%d" % i, [128, HB, 128], BF16) for i in range(2)]
        qTa = [sb("B_qT%d" % i, [128, HB, 128], BF16) for i in range(2)]
        zsa = [sb("B_zs%d" % i, [128, 2048], BF16) for i in range(2)]
        sma = [sb("B_sm%d" % i, [128, 48], F32) for i in range(2)]
        gga = [sb("B_gg%d" % i, [128, 80], F32) for i in range(2)]
        bGa = [sb("B_bG%d" % i, [128, 16], F32) for i in range(2)]
        yb = [sb("B_yb%d" % i, [128, HB, 128], BF16) for i in range(2)]
        ybT = [sb("B_ybT%d" % i, [128, HB, 128], BF16) for i in range(2)]
        NS = 8
        Ug = [sb("B_Ug%d" % i, [128, 128], F32) for i in range(NS)]
        E = [sb("B_E%d" % i, [128, 128], F32) for i in range(NS)]
        Esn = [sb("B_Esn%d" % i, [128, 128], F32) for i in range(NS)]
        Ei = [sb("B_Ei%d" % i, [128, 128], F32) for i in range(NS)]
        kdec = [sb("B_kdec%d" % i, [128, 128], BF16) for i in range(NS)]
        X = [[sb("B_X%d_%d" % (i, j), [128, 256], F32) for j in range(2)] for i in range(NS)]
        BB = [[sb("B_BB%d_%d" % (i, j), [128, 256], F32) for j in range(2)] for i in range(NS)]
        intra = [sb("B_in%d" % i, [128, 128], BF16) for i in range(NS)]
        intraT = [sb("B_inT%d" % i, [128, 128], BF16) for i in range(NS)]
        wT = [sb("B_wT%d" % i, [128, 128], BF16) for i in range(NS)]
        vnew = [sb("B_vn%d" % i, [128, 128], BF16) for i in range(NS)]
        o1 = [sb("B_o1%d" % i, [128, 128], F32) for i in range(NS)]
        oo = [sb("B_oo%d" % i, [128, 128], F32) for i in range(NS)]
        osq = [sb("B_osq%d" % i, [128, 128], BF16) for i in range(NS)]
        oss = [sb("B_oss%d" % i, [128, 1], F32) for i in range(NS)]
        ors = [sb("B_ors%d" % i, [128, 1], F32) for i in range(NS)]
        self.memset("pool", S32[:], 0.0, [("B_S32", h) for h in range(HB)])
        self.memset("pool", Sb[:], 0.0, [("B_Sb", h) for h in range(HB)])
        self.load_bcast(gdnB, I["delta_norm_g"], 128, "B_gdnB")
        Uc, ONESf, Lst, NSTR, INCL, IDf = self.C(2), self.C(1), self.C(3), self.C(5), self.C(6), self.C(0)

        def tile_prep(T):
            own = T >= 16
            p2 = T % 2
            c0 = T * 128
            kT, vT, qT, zs, sm, gg, bG = kTa[p2], vTa[p2], qTa[p2], zsa[p2], sma[p2], gga[p2], bGa[p2]
            self.dma(kT[:], Sx["s_gk"][:, :, c0:c0 + 128].rearrange("h d t -> d h t"),
                     [("s_gk", h, T // 4) for h in range(HB)], [("B_kT", p2)])
            self.dma(vT[:], Sx["s_gv"][:, :, c0:c0 + 128].rearrange("h d t -> d h t"),
                     [("s_gv", h, T // 4) for h in range(HB)], [("B_vT", p2)])
            self.dma(sm[:], Sx["s_small"][c0:c0 + 128, :], [("s_small", T // 4, T % 4)], [("B_sm", p2)])
            if own:
                ot = T - 16
                self.dma(qT[:], Sx["s_gq"][:, :, c0:c0 + 128].rearrange("h d t -> d h t"),
                         [("s_gq", h, T // 4) for h in range(HB)], [("B_qT", p2)])
                self.dma(zs[:], Sx["s_zs"][ot * 128:(ot + 1) * 128, :],
                         [("s_zs", ot // 4, ot % 4, j * 256) for j in range(8)], [("B_zs", p2)])
            beta = sm[:, 16:32]
            gcol = sm[:, 32:48]
            b = self.bank()
            self.mm(self.ps[b][:, 0:16], Uc, gcol, True, True, ["cst", ("B_sm", p2)], [("ps", b)])
            self.mm(self.ps[b][:, 16:32], ONESf, gcol, True, True, ["cst", ("B_sm", p2)], [("ps", b)])
            self.copy("dve", gg[:, 0:32], self.ps[b][:, 0:32], [("ps", b)], [("B_gg", p2)])
            self.tt("dve", gg[:, 48:64], gg[:, 16:32], gg[:, 0:16], ALU.subtract, [("B_gg", p2)], [("B_gg", p2)])
            self.act(gg[:, 32:48], gg[:, 0:16], AF.Exp, [("B_gg", p2)], [("B_gg", p2)])
            self.act(gg[:, 48:64], gg[:, 48:64], AF.Exp, [("B_gg", p2)], [("B_gg", p2)])
            self.act(gg[:, 64:80], gg[:, 16:32], AF.Exp, [("B_gg", p2)], [("B_gg", p2)])
            self.tt("dve", bG[:], beta, gg[:, 32:48], ALU.mult, [("B_sm", p2), ("B_gg", p2)], [("B_bG", p2)])
            tile_res = [("B_sm", p2), ("B_gg", p2), ("B_bG", p2)]
            return (kT, vT, qT, zs, sm, gg, bG, beta, gcol, tile_res, own, p2, T)

        def head_gen(tv, h, s_):
            kT, vT, qT, zs, sm, gg, bG, beta, gcol, tile_res, own, p2, T = tv
            kTh, vTh, qTh = kT[:, h, :], vT[:, h, :], qT[:, h, :]
            b = self.bank()
            pT = self.ps[b][:].bitcast(BF16)
            self.tgroup([(pT[:, 0:128], kTh, self.identb[:]), (pT[:, 128:256], vTh, self.identb[:])],
                        [("B_kT", p2), ("B_vT", p2), "identb"], [("ps", b)])
            self.act(kdec[s_][:], pT[:, 0:128], AF.Copy, [("ps", b)] + tile_res, [("B_kdec", s_)], scale=gg[:, 48 + h:49 + h])
            X0 = X[s_][0]
            self.act(X0[:, 128:256].bitcast(F32R), pT[:, 0:128], AF.Copy, [("ps", b)] + tile_res, [("B_X", s_, 0, 1)], scale=bG[:, h:h + 1])
            self.act(X0[:, 0:128].bitcast(F32R), pT[:, 128:256], AF.Copy, [("ps", b)] + tile_res, [("B_X", s_, 0, 0)], scale=beta[:, h:h + 1])
            yield
            self.act(Ug[s_][:], Uc, AF.Copy, ["cst"] + tile_res, [("B_Ug", s_)], scale=gcol[:, h:h + 1])
            b = self.bank()
            self.mm(self.ps[b][:, 0:128], Ug[s_][:], Lst, True, True, [("B_Ug", s_), "cst"], [("ps", b)])
            self.act(E[s_][:], self.ps[b][:, 0:128], AF.Exp, [("ps", b)], [("B_E", s_)])
            self.tt("pool", Esn[s_][:], E[s_][:], NSTR, ALU.mult, [("B_E", s_), "cst"], [("B_Esn", s_)])
            yield
            b = self.bank()
            self.mm(self.ps[b][:, 0:128], kTh, kTh, True, True, [("B_kT", p2)], [("ps", b)])
            B0 = BB[s_][0]
            self.stt(B0[:, 0:128].bitcast(F32R), self.ps[b][:, 0:128], beta[:, h:h + 1], Esn[s_][:], ALU.mult, ALU.mult,
                     [("ps", b), ("B_Esn", s_)] + tile_res, [("B_BB", s_, 0, 0)])
            if own:
                self.tt("pool", Ei[s_][:], E[s_][:], INCL, ALU.mult, [("B_E", s_), "cst"], [("B_Ei", s_)])
                b = self.bank()
                self.mm(self.ps[b][:, 0:128], qTh, kTh, True, True, [("B_kT", p2), ("B_qT", p2)], [("ps", b)])
                self.tt("dve", intra[s_][:], self.ps[b][:, 0:128], Ei[s_][:], ALU.mult, [("ps", b), ("B_Ei", s_)], [("B_in", s_)])
            yield
            b = self.bank()
            self.tgroup([(self.ps[b][:, 0:128], B0[:, 0:128], IDf)], [("B_BB", s_, 0, 0), "cst"], [("ps", b)])
            self.copy("act", B0[:, 128:256].bitcast(F32R), self.ps[b][:, 0:128], [("ps", b)], [("B_BB", s_, 0, 1)])
            if own:
                b = self.bank()
                pT = self.ps[b][:].bitcast(BF16)
                self.tgroup([(pT[:, 0:128], intra[s_][:], self.identb[:])], [("B_in", s_), "identb"], [("ps", b)])
                self.copy("act", intraT[s_][:], pT[:, 0:128], [("ps", b)], [("B_inT", s_)])
            yield
            for lv in range(7):
                cur, nx = lv % 2, (lv + 1) % 2
                Bc, Xc, Xn = BB[s_][cur], X[s_][cur], X[s_][nx]
                b = self.bank()
                self.mm(self.ps[b][:, 0:256], Bc[:, 128:256].bitcast(F32R), Xc[:].bitcast(F32R), True, True,
                        [("B_BB", s_, cur, 1), ("B_X", s_, cur, 0), ("B_X", s_, cur, 1)], [("ps", b)])
                self.tt("dve", Xn[:].bitcast(F32R), self.ps[b][:, 0:256], Xc[:], ALU.add,
                        [("ps", b), ("B_X", s_, cur, 0), ("B_X", s_, cur, 1)], [("B_X", s_, nx, 0), ("B_X", s_, nx, 1)])
                if lv < 6:
                    Bn = BB[s_][nx]
                    b = self.bank()
                    self.mmgroup([(self.ps[b][:, 0:128], Bc[:, 128:256].bitcast(F32R), Bc[:, 0:128].bitcast(F32R), True, True),
                                  (self.ps[b][:, 128:256], Bc[:, 0:128].bitcast(F32R), Bc[:, 128:256].bitcast(F32R), True, True)],
                                 [("B_BB", s_, cur, 0), ("B_BB", s_, cur, 1)], [("ps", b)])
                    self.copy("act" if lv % 2 == 0 else "dve", Bn[:].bitcast(F32R), self.ps[b][:, 0:256], [("ps", b)],
                              [("B_BB", s_, nx, 0), ("B_BB", s_, nx, 1)])
                yield
            Xf = X[s_][1]
            xres = [("B_X", s_, 1, 0), ("B_X", s_, 1, 1)]
            b = self.bank()
            self.tgroup([(self.ps[b][:, 0:128], Xf[:, 128:256], IDf)], xres + ["cst"], [("ps", b)])
            self.copy("act", wT[s_][:], self.ps[b][:, 0:128], [("ps", b)], [("B_wT", s_)])
            yield
            b = self.bank()
            self.mm(self.ps[b][:, 0:128], wT[s_][:], Sb[:, h, :], True, True, [("B_wT", s_), ("B_Sb", h)], [("ps", b)])
            self.tt("dve", vnew[s_][:], Xf[:, 0:128], self.ps[b][:, 0:128], ALU.subtract, [("ps", b)] + xres, [("B_vn", s_)])
            yield
            if own:
                b = self.bank()
                self.mm(self.ps[b][:, 0:128], qTh, Sb[:, h, :], True, True, [("B_qT", p2), ("B_Sb", h)], [("ps", b)])
                self.act(o1[s_][:], self.ps[b][:, 0:128], AF.Copy, [("ps", b)] + tile_res, [("B_o1", s_)], scale=gg[:, 32 + h:33 + h])
                b = self.bank()
                self.mm(self.ps[b][:, 0:128], intraT[s_][:], vnew[s_][:], True, True, [("B_inT", s_), ("B_vn", s_)], [("ps", b)])
                self.tt("dve", oo[s_][:], self.ps[b][:, 0:128], o1[s_][:], ALU.add, [("ps", b), ("B_o1", s_)], [("B_oo", s_)])
                self.stt(osq[s_][:], oo[s_][:], 1.0, oo[s_][:], ALU.mult, ALU.mult, [("B_oo", s_)], [("B_osq", s_), ("B_oss", s_)],
                         accum_out=oss[s_][:])
                self.act(ors[s_][:], oss[s_][:], AF.Sqrt, [("B_oss", s_)], [("B_ors", s_)], scale=1.0 / 128, bias=EPS)
                self.s.op("dve", lambda e, r=ors[s_]: e.reciprocal(r[:], r[:]), [("B_ors", s_)], [("B_ors", s_)])
                self.stt(oo[s_][:], oo[s_][:], ors[s_][:], gdnB[:], ALU.mult, ALU.mult, [("B_oo", s_), ("B_ors", s_), "B_gdnB"], [("B_oo", s_)])
                self.tt("dve", yb[p2][:, h, :], oo[s_][:], zs[:, h * 128:(h + 1) * 128], ALU.mult, [("B_oo", s_), ("B_zs", p2)], [("B_yb", p2, h)])
            yield
            b = self.bank()
            self.mm(self.ps[b][:, 0:128], kdec[s_][:], vnew[s_][:], True, True, [("B_kdec", s_), ("B_vn", s_)], [("ps", b)])
            self.stt(S32[:, h, :], S32[:, h, :], gg[:, 64 + h:65 + h], self.ps[b][:, 0:128], ALU.mult, ALU.add,
                     [("ps", b), ("B_S32", h)] + tile_res, [("B_S32", h)])
            self.copy("pool", Sb[:, h, :], S32[:, h, :], [("B_S32", h)], [("B_Sb", h)])

        def tile_finish(tv):
            kT, vT, qT, zs, sm, gg, bG, beta, gcol, tile_res, own, p2, T = tv
            if own:
                ot = T - 16
                for h0 in range(0, HB, 8):
                    b = self.bank()
                    pT = self.ps[b][:].bitcast(BF16)
                    self.tgroup([(pT[:, j * 128:(j + 1) * 128], yb[p2][:, h0 + j, :], self.identb[:]) for j in range(8)],
                                [("B_yb", p2, h0 + j) for j in range(8)] + ["identb"], [("ps", b)])
                    self.copy(self.evac_engine(), ybT[p2][:, h0:h0 + 8, :], pT[:, 0:1024].rearrange("p (c n) -> p c n", n=128),
                              [("ps", b)], [("B_ybT", p2, h0)])
                self.dma(Sx["s_yT"][16:32, :, ot * 128:(ot + 1) * 128].rearrange("h d t -> d h t"), ybT[p2][:],
                         [("B_ybT", p2, 0), ("B_ybT", p2, 8)], [("s_yT", 1, ot)])

        work = [(T, h) for T in range(32) for h in range(HB)]
        tvs = {}
        left = {T: HB for T in range(32)}
        active = {}
        nxt = [0]

        def start(slot):
            T, h = work[nxt[0]]
            nxt[0] += 1
            if T not in tvs:
                tvs[T] = tile_prep(T)
            active[slot] = (head_gen(tvs[T], h, slot), T)

        for slot in range(NS):
            start(slot)
        while active:
            for slot in range(NS):
                if slot not in active:
                    continue
                g_, T = active[slot]
                try:
                    next(g_)
                except StopIteration:
                    del active[slot]
                    left[T] -= 1
                    if left[T] == 0:
                        tile_finish(tvs[T])
                    if nxt[0] < len(work):
                        start(slot)
        self.phase_end()


    def phase_C(self):
        I, Sx = self.I, self.S
        sb = self.sb
        self.phase_begin()
        ckT = sb("C_ckT", [128, 4, CTX], BF16)
        ckM = sb("C_ckM", [128, 32, 512], BF16)
        kiT = sb("C_kiT", [128, CTX], BF16)
        wuv = sb("C_wuv", [128, HA, 4, 128], BF16)
        qiT = [sb("C_qiT%d" % i, [128, HA, 128], BF16) for i in range(2)]
        qlT = [sb("C_qlT%d" % i, [128, HA, 4, 128], BF16) for i in range(2)]
        wi = [sb("C_wi%d" % i, [128, 48], F32) for i in range(2)]
        wab = [sb("C_wab%d" % i, [128, 16], F32) for i in range(2)]
        wsg = [sb("C_wsg%d" % i, [128, 16], F32) for i in range(2)]
        sc = sb("C_sc", [128, CTX], F32)
        selm = sb("C_selm", [128, CTX], BF16)
        negm = [sb("C_negm%d" % i, [128, 32, 128], BF16) for i in range(2)]
        rl = [sb("C_rl%d" % i, [128, 512], F32) for i in range(2)]
        st = sb("C_st", [128, 8], F32)
        Pm = [sb("C_P%d" % i, [128, 4, 128], BF16) for i in range(3)]
        rden = sb("C_rden", [128, 512], F32)
        olat = sb("C_olat", [128, 4, 512], BF16)
        yaT = [sb("C_yaT%d" % i, [128, HA, 128], BF16) for i in range(2)]
        for cc in range(4):
            self.dma(ckT[:, cc, :], Sx["s_ckvnT"][cc], [("s_ckvnT", cc, g) for g in range(8)], [("C_ckT", cc)])
        self.dma(kiT[:], Sx["s_kiT"], [("s_kiT", g) for g in range(8)], ["C_kiT"])
        for h0 in range(0, HA, 4):
            self.dma(wuv[:, h0:h0 + 4, :, :], I["w_uv"][h0:h0 + 4].rearrange("h (cc p) e -> p h cc e", p=128), [], [("C_wuv", h0)], q="pool")
        for kb in range(32):
            b = self.bank()
            pT = self.ps[b][:].bitcast(BF16)
            self.tgroup([(pT[:, cc * 128:(cc + 1) * 128], ckT[:, cc, kb * 128:(kb + 1) * 128], self.identb[:]) for cc in range(4)],
                        [("C_ckT", cc) for cc in range(4)] + ["identb"], [("ps", b)])
            self.copy(self.evac_engine(), ckM[:, kb, :], pT[:, 0:512], [("ps", b)], [("C_ckM", kb)])
        ckres = [("C_ckT", cc) for cc in range(4)]
        CM = self.C(4)
        SCALE = 128.0 ** -0.5
        PACC = [0, 1, 2, 3]
        PDEN = 4
        PLOG = [5, 6]
        PMISC = 7
        self._lrr = 0

        def score_gen(qt):
            T = 16 + qt
            nkb = T + 1
            nk = nkb * 128
            p2 = qt % 2
            q0 = qt * 128
            self.dma(qiT[p2][:], Sx["s_qiT"][:, :, q0:q0 + 128].rearrange("h d t -> d h t"),
                     [("s_qiT", h, qt // 4) for h in range(HA)], [("C_qiT", p2)])
            self.uid += 1
            for h0 in range(0, HA, 4):
                self.dma(qlT[p2][:, h0:h0 + 4, :, :], Sx["s_qlT"][h0:h0 + 4, :, :, q0:q0 + 128].rearrange("h cc c t -> c h cc t"),
                         [("s_qlT", h, cc, qt // 4) for h in range(h0, h0 + 4) for cc in range(4)], [("C_qlT", p2)], gid=("ql", self.uid))
            self.dma(wi[p2][:], Sx["s_small"][OWN0 + q0:OWN0 + q0 + 128, :], [("s_small", T // 4, T % 4)], [("C_wi", p2)])
            self.act(wab[p2][:], wi[p2][:, 0:16], AF.Abs, [("C_wi", p2)], [("C_wab", p2)], scale=SCALE)
            self.act(wsg[p2][:], wi[p2][:, 0:16], AF.Sign, [("C_wi", p2)], [("C_wsg", p2)])
            yield
            nch = (nk + 511) // 512
            for ch in range(nch):
                k0 = ch * 512
                kw = min(512, nk - k0)
                for h in range(HA):
                    b = PMISC
                    self.mm(self.ps[b][:, 0:kw], qiT[p2][:, h, :], kiT[:, k0:k0 + kw], True, True, [("C_qiT", p2), "C_kiT"], [("ps", b)])
                    r = rl[h % 2]
                    self.act(r[:, 0:kw], self.ps[b][:, 0:kw], AF.Relu, [("ps", b), ("C_wab", p2)], [("C_rl", h % 2)], scale=wab[p2][:, h:h + 1])
                    if h == 0:
                        self.ts("dve", sc[:, k0:k0 + kw], r[:, 0:kw], wsg[p2][:, 0:1], None, ALU.mult, None,
                                [("C_rl", 0), ("C_wsg", p2)], [("C_sc", ch)])
                    else:
                        self.stt(sc[:, k0:k0 + kw], r[:, 0:kw], wsg[p2][:, h:h + 1], sc[:, k0:k0 + kw], ALU.mult, ALU.add,
                                 [("C_rl", h % 2), ("C_wsg", p2), ("C_sc", ch)], [("C_sc", ch)])
                    yield
            scres = [("C_sc", ch) for ch in range(nch)]
            self.s.op("dve", lambda e, nk=nk: e.tensor_reduce(st[:, 0:1], sc[:, 0:nk], AX.X, ALU.max, apply_absolute_value=True),
                      scres, [("C_st", 0)])
            self.ts("dve", sc[:, 0:OWN0], sc[:, 0:OWN0], self.pmt[:, 0:1], None, ALU.add, None, scres + ["pmt"], scres)
            self.tt("dve", sc[:, T * 128:(T + 1) * 128], sc[:, T * 128:(T + 1) * 128], CM, ALU.add, scres + ["cst"], scres)
            self.ts("dve", st[:, 1:2], st[:, 0:1], -1.0, -0.01, ALU.mult, ALU.add, [("C_st", 0)], [("C_st", 1)])
            self.ts("dve", st[:, 2:3], st[:, 0:1], 1.0, 0.01, ALU.mult, ALU.add, [("C_st", 0)], [("C_st", 2)])
            yield
            nh1 = (nkb // 2) * 128
            n2 = nk - nh1
            for it in range(20):
                self.tt("dve", st[:, 3:4], st[:, 1:2], st[:, 2:3], ALU.add, [("C_st", 1), ("C_st", 2)], [("C_st", 3)])
                self.ts("dve", st[:, 6:7], st[:, 3:4], -1.0, None, ALU.mult, None, [("C_st", 3)], [("C_st", 6)])
                self.s.op("act", lambda e, nh1=nh1, nk=nk: e.activation(out=selm[:, nh1:nk], in_=sc[:, nh1:nk], func=AF.Sign,
                                                                      bias=st[:, 6:7], accum_out=st[:, 7:8]),
                          scres + [("C_st", 6)], ["C_selm_b", ("C_st", 7)], noattach=True)
                self.ts("dve", selm[:, 0:nh1], sc[:, 0:nh1], st[:, 3:4], None, ALU.is_ge, ALU.add, scres + [("C_st", 3)],
                        ["C_selm", ("C_st", 4)], accum_out=st[:, 4:5])
                self.stt(st[:, 4:5], st[:, 7:8], 0.5, st[:, 4:5], ALU.mult, ALU.add, [("C_st", 7), ("C_st", 4)], [("C_st", 4)])
                self.stt(st[:, 5:6], st[:, 4:5], TOPK - 0.5 - 0.5 * n2, st[:, 2:3], ALU.is_ge, ALU.mult, [("C_st", 4), ("C_st", 2)], [("C_st", 5)])
                self.tt("dve", st[:, 1:2], st[:, 1:2], st[:, 5:6], ALU.add, [("C_st", 1), ("C_st", 5)], [("C_st", 1)])
                self.ts("dve", st[:, 2:3], st[:, 2:3], 0.5, None, ALU.mult, None, [("C_st", 2), ("C_st", 5), ("C_st", 3)], [("C_st", 2)])
                yield
            self.ts("dve", selm[:, 0:nk], sc[:, 0:nk], st[:, 1:2], None, ALU.is_ge, None, scres + [("C_st", 1)], ["C_selm", "C_selm_b"])
            yield
            for k8 in range(0, nkb, 8):
                n8 = min(8, nkb - k8)
                b = PMISC
                pT = self.ps[b][:].bitcast(BF16)
                self.tgroup([(pT[:, j * 128:(j + 1) * 128], selm[:, (k8 + j) * 128:(k8 + j + 1) * 128], self.identb[:]) for j in range(n8)],
                            ["C_selm", "identb"], [("ps", b)])
                self.ts("dve", negm[p2][:, k8:k8 + n8, :], pT[:, 0:n8 * 128].rearrange("p (c n) -> p c n", n=128), -1.0, 30000.0,
                        ALU.add, ALU.mult, [("ps", b)], [("C_negm", p2, k8)])
                yield

        def attn_gen(qt):
            T = 16 + qt
            nkb = T + 1
            p2 = qt % 2
            q0 = qt * 128
            mres = [("C_negm", p2, k8) for k8 in range(0, nkb, 8)]

            def L(hg, kb):
                b = PLOG[kb % 2]
                pl = self.ps[b][:, 0:512].rearrange("p (h q) -> p h q", q=128)
                items = [(pl, ckT[:, cc, kb * 128:(kb + 1) * 128], qlT[p2][:, hg * 4:(hg + 1) * 4, cc, :], cc == 0, False) for cc in range(4)]
                items += [(pl[:, j, :], self.identb[:], negm[p2][:, kb, :], False, j == 3) for j in range(4)]
                self.mmgroup(items, ckres + [("C_qlT", p2), "identb"] + mres, [("ps", b)])

            for hg in range(4):
                L(hg, 0)
                for kb in range(nkb):
                    if kb + 1 < nkb:
                        L(hg, kb + 1)
                    b = PLOG[kb % 2]
                    pl = self.ps[b][:, 0:512].rearrange("p (h q) -> p h q", q=128)
                    P = Pm[kb % 3]
                    self.act(P[:], pl, AF.Exp, [("ps", b)], [("C_P", kb % 3)], scale=SCALE)
                    yield
                    Pf = P[:].rearrange("p h q -> p (h q)")
                    items = [(self.ps[PACC[cc]][:, 0:512], ckM[:, kb, cc * 128:(cc + 1) * 128], Pf, kb == 0, kb == nkb - 1) for cc in range(4)]
                    items.append((self.ps[PDEN][:, 0:512], self.onesb[:], Pf, kb == 0, kb == nkb - 1))
                    self.mmgroup(items, [("C_ckM", kb), ("C_P", kb % 3), "onesb"], [("ps", PACC[cc]) for cc in range(4)] + [("ps", PDEN)])
                    yield
                self.s.op("dve", lambda e: e.reciprocal(rden[:], self.ps[PDEN][:, 0:512]), [("ps", PDEN)], ["C_rden"])
                for cc in range(4):
                    self.tt("dve", olat[:, cc, :], self.ps[PACC[cc]][:, 0:512], rden[:], ALU.mult, [("ps", PACC[cc]), "C_rden"], [("C_olat", cc)])
                for j in range(4):
                    h = hg * 4 + j
                    b = PMISC
                    self.mmgroup([(self.ps[b][:, 0:128], wuv[:, h, cc, :], olat[:, cc, j * 128:(j + 1) * 128], cc == 0, cc == 3) for cc in range(4)],
                                 [("C_wuv", (h // 4) * 4)] + [("C_olat", cc) for cc in range(4)], [("ps", b)])
                    self.copy("act", yaT[p2][:, h, :], self.ps[b][:, 0:128], [("ps", b)], [("C_yaT", p2, h)])
                yield
            self.dma(Sx["s_yT"][0:16, :, q0:q0 + 128].rearrange("h d t -> d h t"), yaT[p2][:],
                     [("C_yaT", p2, h) for h in range(HA)], [("s_yT", 0, qt)])

        order = list(range(15, -1, -1))
        for _ in score_gen(order[0]):
            pass
        for oi, qt in enumerate(order):
            ag = attn_gen(qt)
            sg = score_gen(order[oi + 1]) if oi + 1 < 16 else None
            for _ in ag:
                if sg is not None:
                    try:
                        next(sg)
                    except StopIteration:
                        sg = None
            if sg is not None:
                for _ in sg:
                    pass
        self.phase_end()


    def phase_D(self):
        I, Sx = self.I, self.S
        sb = self.sb
        self.phase_begin()
        TG = 1024
        NT = TG // 128
        XT = sb("D_XT", [128, 32, TG], BF16)
        Wb = [sb("D_W0", [128, 8192], BF16), sb("D_W1", [128, 8192], BF16)]
        xt2 = [sb("D_xt%d" % i, [128, D], F32) for i in range(2)]
        xn = sb("D_xn", [128, D], BF16)
        ss2 = [sb("D_ss%d" % i, [128, 1], F32) for i in range(2)]
        rstd2 = [sb("D_rstd%d" % i, [128, 1], F32) for i in range(2)]
        gB = sb("D_gB", [128, D], F32)
        blk = [sb("D_blk%d" % i, [128, 512], F32) for i in range(3)]
        obl = [sb("D_obl%d" % i, [128, 512], F32) for i in range(3)]
        qx = [sb("D_qx%d" % i, [128, 512], BF16) for i in range(2)]
        Px = [sb("D_Px%d" % i, [128, 512], BF16) for i in range(2)]
        rdx = sb("D_rdx", [128, 512], F32)
        oxT = sb("D_oxT", [128, 4, TG], BF16)
        sg = [[sb("D_sg%d_%d" % (i, j), [128, 512], BF16) for j in range(2)] for i in range(2)]
        ao = [sb("D_ao%d" % i, [128, 512], BF16) for i in range(3)]
        wo = I["w_o"].rearrange("(c p) n -> p c n", p=128)
        wcq = I["w_cq"].rearrange("(c p) n -> p c n", p=128)
        wco = I["w_co"].rearrange("(c p) n -> p c n", p=128)
        wfi = I["w_ffn_in"].rearrange("(c p) n -> p c n", p=128)
        SCALE = 128.0 ** -0.5
        allxt = [("D_XT", t) for t in range(NT)]
        for og in range(NOWN // TG):
            r0 = og * TG
            for half in range(2):
                for t4 in range(0, NT, 4):
                    self.dma(XT[:, half * 16:(half + 1) * 16, t4 * 128:(t4 + 4) * 128],
                             Sx["s_yT"][half * 16:(half + 1) * 16, :, r0 + t4 * 128:r0 + (t4 + 4) * 128].rearrange("c p t -> p c t"),
                             [("s_yT", half, (r0 // 128) + t4 + t) for t in range(4)], allxt, gid=("yT", og))
            xres = allxt

            def obody(view, wres, info, ncols):
                def epi(t, b):
                    self.uid += 1
                    i3 = self.uid % 3
                    rr = r0 + t * 128
                    self.dma(blk[i3][:, 0:ncols], I["xc"][OWN0 + rr:OWN0 + rr + 128, info:info + ncols], [], [("D_blk", i3)])
                    self.tt("dve", obl[i3][:, 0:ncols], self.ps[b][:, 0:ncols], blk[i3][:, 0:ncols], ALU.add, [("ps", b), ("D_blk", i3)], [("D_obl", i3)])
                    self.dma(Sx["s_h1"][rr:rr + 128, info:info + ncols], obl[i3][:, 0:ncols], [("D_obl", i3)], [("s_h1", og, t, info)])
                self.tm_chunk(XT, xres, 32, TG, view, wres, ncols, epi)
            self.linear(32, wo, [(j * 256, 256, j * 256) for j in range(16)], Wb, obody, nxt_first=(32, wcq, 0, 256, Wb))

            self.load_bcast(gB, I["cross_norm_g"], D, "D_gB")
            for t in range(NT):
                rr = r0 + t * 128
                self.norm_transpose_tile(Sx["s_h1"][rr:rr + 128, :], gB, XT, t, 32, (xt2[t % 2], xn, ss2[t % 2], rstd2[t % 2]), "D", par=t % 2,
                                         extra_reads=[("s_h1", og, t, j * 256) for j in range(16)])

            def cqbody(view, wres, info, ncols):
                def epi(h, b, hf):
                    i2 = (h * 2 + hf) % 2
                    self.copy("act", qx[i2][:], self.ps[b][:, 0:512], [("ps", b)], [("D_qx", i2)])
                    bd = self.bank()
                    bo = self.bank()
                    for mc in range(2):
                        bl = self.bank()
                        self.mm(self.ps[bl][:, 0:512], self.kxT[:, h, mc * 128:(mc + 1) * 128], qx[i2][:], True, True,
                                [("P_kxT", h), ("D_qx", i2)], [("ps", bl)])
                        self.act(Px[mc][:], self.ps[bl][:, 0:512], AF.Exp, [("ps", bl)], [("D_Px", mc)], scale=SCALE)
                        self.mm(self.ps[bd][:, 0:512], self.onesb[:], Px[mc][:], mc == 0, mc == 1, ["onesb", ("D_Px", mc)], [("ps", bd)])
                        self.mm(self.ps[bo][:, 0:512], self.vx[:, mc, h * 128:(h + 1) * 128], Px[mc][:], mc == 0, mc == 1,
                                [("P_vx", mc, 0), ("P_vx", mc, 256), ("D_Px", mc)], [("ps", bo)])
                    self.s.op("dve", lambda e, bd=bd: e.reciprocal(rdx[:], self.ps[bd][:, 0:512]), [("ps", bd)], ["D_rdx"])
                    self.tt("dve", oxT[:, h, hf * 512:(hf + 1) * 512], self.ps[bo][:, 0:512], rdx[:], ALU.mult, [("ps", bo), "D_rdx"], [("D_oxT", h, hf)])
                self.fm_chunk(XT, xres, 32, TG, view, wres, ncols, epi, info)
            self.linear(32, wcq, [(0, 256, 0), (256, 256, 2)], Wb, cqbody, nxt_first=(4, wco, 0, 512, Wb))
            oxres = [("D_oxT", h, hf) for h in range(4) for hf in range(2)]

            def cobody(view, wres, info, ncols):
                def epi(t, b):
                    self.uid += 1
                    i3 = self.uid % 3
                    rr = r0 + t * 128
                    self.dma(blk[i3][:, 0:ncols], Sx["s_h1"][rr:rr + 128, info:info + ncols],
                             [("s_h1", og, t, info), ("s_h1", og, t, info + 256)], [("D_blk", i3)])
                    self.tt("dve", obl[i3][:, 0:ncols], self.ps[b][:, 0:ncols], blk[i3][:, 0:ncols], ALU.add, [("ps", b), ("D_blk", i3)], [("D_obl", i3)])
                    self.dma(Sx["s_h2"][rr:rr + 128, info:info + ncols], obl[i3][:, 0:ncols], [("D_obl", i3)], [("s_h2", og, t, info)])
                self.tm_chunk(oxT, oxres, 4, TG, view, wres, ncols, epi)
            self.linear(4, wco, [(j * 512, 512, j * 512) for j in range(8)], Wb, cobody, nxt_first=(32, wfi, 0, 256, Wb))

            self.load_bcast(gB, I["ffn_norm_g"], D, "D_gB")
            for t in range(NT):
                rr = r0 + t * 128
                self.norm_transpose_tile(Sx["s_h2"][rr:rr + 128, :], gB, XT, t, 32, (xt2[t % 2], xn, ss2[t % 2], rstd2[t % 2]), "D", par=t % 2,
                                         extra_reads=[("s_h2", og, t, j * 512) for j in range(8)])
            chunks = []
            for j in range(43):
                chunks.append((j * 256, 256, ("g", j)))
                chunks.append((DFF + j * 256, 256, ("u", j)))

            def fbody(view, wres, info, ncols):
                kind, j = info

                def epi(sub, b, hf):
                    if kind == "g":
                        self.act(sg[sub][hf][:], self.ps[b][:, 0:512], AF.Silu, [("ps", b)], [("D_sg", sub, hf)])
                    else:
                        self.uid += 1
                        i3 = self.uid % 3
                        self.tt("dve", ao[i3][:], self.ps[b][:, 0:512], sg[sub][hf][:], ALU.mult, [("ps", b), ("D_sg", sub, hf)], [("D_ao", i3)])
                        self.dma(Sx["s_actT"][j * 2 + sub, :, r0 + hf * 512:r0 + (hf + 1) * 512], ao[i3][:], [("D_ao", i3)],
                                 [("s_actT", j * 2 + sub, og, hf)])
                self.fm_chunk(XT, xres, 32, TG, view, wres, ncols, epi, 0)
            self.linear(32, wfi, chunks, Wb, fbody, nxt_first=(32, wo, 0, 256, Wb) if og == 0 else None)
        self.phase_end()

    def phase_E(self):
        I, Sx = self.I, self.S
        sb = self.sb
        self.phase_begin()
        TG = 1024
        KH = 43
        XT = sb("E_XT", [128, KH, TG], BF16)
        Wb = [sb("E_W0", [128, KH * 256], BF16), sb("E_W1", [128, KH * 256], BF16)]
        of = [sb("E_of%d" % i, [128, 512], F32) for i in range(2)]
        hb = [sb("E_hb%d" % i, [128, 4, 128], F32) for i in range(2)]
        h3 = [sb("E_h3%d" % i, [128, 4, 128], F32) for i in range(2)]
        IDf = self.C(0)
        for og in range(NOWN // TG):
            r0 = og * TG
            for kh in range(2):
                wfo = I["w_ffn_out"][kh * KH * 128:(kh + 1) * KH * 128, :].rearrange("(c p) n -> p c n", p=128)
                self.uid += 1
                gidx = ("EXT", self.uid)
                for k0 in range(0, KH, 8):
                    k1 = min(KH, k0 + 8)
                    self.dma(XT[:, k0:k1, :], Sx["s_actT"][kh * KH + k0:kh * KH + k1, :, r0:r0 + TG].rearrange("c p t -> p c t"),
                             [("s_actT", c, og, hf) for c in range(kh * KH + k0, kh * KH + k1) for hf in range(2)], ["E_XT"], gid=gidx)
                xres = ["E_XT"]

                def body(view, wres, info, ncols, kh=kh, r0=r0, og=og):
                    def epi(c, b, hf):
                        self.uid += 1
                        i2 = self.uid % 2
                        rr = r0 + hf * 512
                        self.copy("act", of[i2][:], self.ps[b][:, 0:512], [("ps", b)], [("E_of", i2)])
                        return lambda: etail(c, hf, i2, rr)

                    def etail(c, hf, i2, rr):
                        b2 = self.bank()
                        self.tgroup([(self.ps[b2][:, t * 128:(t + 1) * 128], of[i2][:, t * 128:(t + 1) * 128], IDf) for t in range(4)],
                                    [("E_of", i2), "cst"], [("ps", b2)])
                        if kh == 0:
                            self.dma(hb[i2][:], Sx["s_h2"][rr:rr + 512, c * 128:(c + 1) * 128].rearrange("(t p) n -> p t n", p=128),
                                     [("s_h2", og, hf * 4 + t, (c // 4) * 512) for t in range(4)], [("E_hb", i2)])
                        else:
                            self.dma(hb[i2][:], Sx["s_h3"][rr:rr + 512, c * 128:(c + 1) * 128].rearrange("(t p) n -> p t n", p=128),
                                     [("s_h3", og, hf, c)], [("E_hb", i2)])
                        self.tt("dve", h3[i2][:], self.ps[b2][:, 0:512].rearrange("p (t n) -> p t n", n=128), hb[i2][:], ALU.add,
                                [("ps", b2), ("E_hb", i2)], [("E_h3", i2)])
                        self.dma(Sx["s_h3"][rr:rr + 512, c * 128:(c + 1) * 128].rearrange("(t p) n -> p t n", p=128), h3[i2][:],
                                 [("E_h3", i2)], [("s_h3", og, hf, c)])
                    self.fm_chunk(XT, xres, KH, TG, view, wres, ncols, epi, info)
                nkh, nog = (kh + 1) % 2, og + (kh + 1) // 2
                nf = None
                if nog < NOWN // TG:
                    wfn = I["w_ffn_out"][nkh * KH * 128:(nkh + 1) * KH * 128, :].rearrange("(c p) n -> p c n", p=128)
                    nf = (KH, wfn, 0, 256, Wb)
                self.linear(KH, wfo, [(j * 256, 256, j * 2) for j in range(16)], Wb, body, nxt_first=nf)
        self.phase_end()
        self.phase_begin()
        xt = [sb("F_xt%d" % i, [128, D], F32) for i in range(2)]
        xo = [sb("F_xo%d" % i, [128, D], F32) for i in range(2)]
        jk = sb("F_jk", [128, D], BF16)
        ss = [sb("F_ss%d" % i, [128, 1], F32) for i in range(2)]
        rs = [sb("F_rs%d" % i, [128, 1], F32) for i in range(2)]
        gB = sb("F_gB", [128, D], F32)
        self.load_bcast(gB, I["final_norm_g"], D, "F_gB")
        for t in range(16):
            i2 = t % 2
            self.dma(xt[i2][:], Sx["s_h3"][t * 128:(t + 1) * 128, :], [("s_h3", t // 8, (t % 8) // 4, c) for c in range(32)], [("F_xt", i2)])
            self.stt(jk[:], xt[i2][:], 1.0, xt[i2][:], ALU.mult, ALU.mult, [("F_xt", i2)], ["F_jk", ("F_ss", i2)], accum_out=ss[i2][:])
            self.act(rs[i2][:], ss[i2][:], AF.Sqrt, [("F_ss", i2)], [("F_rs", i2)], scale=1.0 / D, bias=EPS)
            self.s.op("dve", lambda e, r=rs[i2]: e.reciprocal(r[:], r[:]), [("F_rs", i2)], [("F_rs", i2)])
            self.stt(xo[i2][:], xt[i2][:], rs[i2][:], gB[:], ALU.mult, ALU.mult, [("F_xt", i2), ("F_rs", i2), "F_gB"], [("F_xo", i2)])
            o = self.dma(self.out[t * 128:(t + 1) * 128, :], xo[i2][:], [("F_xo", i2)], [("out", t)])
            self.out_ops.append(o)
        self.phase_end()


def _consts():
    c = np.zeros((128, 8 * 128), np.float32)
    i = np.arange(128)
    c[:, 0:128] = np.eye(128)
    c[:, 128:256] = 1.0
    c[:, 256:384] = (i[:, None] <= i[None, :])
    c[:, 384:512] = (i[:, None] > i[None, :])
    c[:, 512:640] = np.where((i[:, None] < 64) & (i[None, :] >= 64), -1e30, 0.0)
    c[:, 640:768] = -1.0 * (i[:, None] > i[None, :])
    c[:, 768:896] = (i[:, None] >= i[None, :])
    return c


def build_program(stop_after="all", dbg=()):
    B = Builder(stop_after, dbg)
    B.declare()
    order = ["0", "A", "B", "C", "D", "E"]
    for ph in order:
        getattr(B, "phase_" + ph)()
        if stop_after == ph:
            break
    B.s.barrier()
    finals = list(B.s.bar_deps)
    B.s.emit(finals)
    return B


def make_in_maps(inp):
    f = lambda a: np.ascontiguousarray(np.asarray(a, dtype=np.float32))
    x = f(inp["x"])
    shared = {
        "consts": _consts(),
        "w_in": f(inp["w_in"][0]), "w_qb": f(inp["w_qb"][0]), "w_iq": f(inp["w_iq"][0]),
        "wukT": f(np.transpose(inp["w_uk"][0], (2, 0, 1))),
        "w_uv": f(inp["w_uv"][0]),
        "gcols": f(np.concatenate([np.asarray(inp["qa_norm_g"][0]).reshape(12, 128).T,
                                   np.asarray(inp["kv_norm_g"][0]).reshape(4, 128).T], axis=1)),
        "cwl": f(np.transpose(np.asarray(inp["conv_w"][0]).reshape(4, 48, 128), (2, 1, 0)).reshape(128, 192)),
        "w_o": f(inp["w_o"][0]), "w_cq": f(inp["w_cq"][0]), "w_ckv": f(inp["w_ckv"][0]), "w_co": f(inp["w_co"][0]),
        "w_ffn_in": f(inp["w_ffn_in"][0]), "w_ffn_out": f(inp["w_ffn_out"][0]),
        "attn_norm_g": f(inp["attn_norm_g"]).reshape(1, D), "a_log": f(inp["a_log"]).reshape(1, HB),
        "dt_bias": f(inp["dt_bias"]).reshape(1, HB), "delta_norm_g": f(inp["delta_norm_g"]).reshape(1, 128),
        "cross_norm_g": f(inp["cross_norm_g"]).reshape(1, D), "mem_norm_g": f(inp["mem_norm_g"]).reshape(1, D),
        "ffn_norm_g": f(inp["ffn_norm_g"]).reshape(1, D), "final_norm_g": f(inp["final_norm_g"]).reshape(1, D),
    }
    maps = []
    for c in range(8):
        b, hf = c // 2, c % 2
        xc = np.zeros((CTX, D), np.float32)
        if hf == 1:
            xc[:] = x[b]
        else:
            xc[OWN0:] = x[b, 0:NOWN]
        m = dict(shared)
        m["xc"] = xc
        m["memb"] = f(inp["mem"][b])
        m["pm"] = np.full((128, 1), 0.0 if hf == 1 else -1e30, np.float32)
        maps.append(m)
    return maps


def kernel(**inputs):
    B = build_program()
    maps = make_in_maps(inputs)
    res = run_bass_kernel_spmd(B.nc, maps, core_ids=list(range(8)))
    out = np.zeros((NB, S, D), np.float32)
    for c in range(8):
        b, hf = c // 2, c % 2
        out[b, hf * NOWN:(hf + 1) * NOWN] = res.results[c]["out"]
    return out
```

```python
import numpy as np
import concourse.bass as bass
import concourse.mybir as mybir
from concourse.bass_utils import run_bass_kernel_spmd

F32 = mybir.dt.float32
BF16 = mybir.dt.bfloat16
F32R = mybir.dt.float32r
AF = mybir.ActivationFunctionType
ALU = mybir.AluOpType
AX = mybir.AxisListType

D = 4096
S = 4096
NB = 4
CTX = 4096
OWN0 = 2048
NOWN = 2048
EPS = 1e-6
QL = 1536
KVL = 512
HA = 16
HB = 16
NIN = 10416
QA0, CKV0, KI0, WI0, QKV0, Z0, B0, A0 = 0, 1536, 2048, 2176, 2192, 8336, 10384, 10400
DFF = 11008
NMEM = 256
TOPK = 256
NEG = -30000.0


class Op:
    __slots__ = ("eng", "fn", "deps", "signal", "sem", "val", "dma", "prewait", "idx", "noattach")


class Sched:
    ENGS = ("pe", "act", "dve", "pool", "sp")
    NDMA = {"sp": 24, "pool": 12, "act": 0, "dve": 0, "pe": 0}

    def __init__(self, nc):
        self.nc = nc
        self.q = {e: [] for e in self.ENGS}
        self.last_w = {}
        self.readers = {}
        self.last_gid = {}
        self.gdeps = {}
        self.n = 0
        self.bar_id = 0
        self.bar_deps = []
        self.bar_done = {e: 0 for e in self.ENGS}

    def op(self, eng, fn, reads=(), writes=(), dma=False, gid=None, noattach=False):
        o = Op()
        o.noattach = noattach
        o.eng, o.fn, o.dma, o.signal, o.sem, o.val, o.prewait = eng, fn, dma, dma, None, 0, None
        o.idx = self.n
        self.n += 1
        deps = []
        for r in reads:
            deps.extend(self.last_w.get(r, ()))
        for w_ in writes:
            if gid is not None and self.last_gid.get(w_) == gid:
                deps.extend(self.gdeps[w_])
            else:
                g = list(self.last_w.get(w_, ())) + list(self.readers.get(w_, ()))
                deps.extend(g)
                if gid is not None:
                    self.gdeps[w_] = g
        if self.bar_done[eng] < self.bar_id:
            deps.extend(self.bar_deps)
            self.bar_done[eng] = self.bar_id
        seen = set()
        dd = []
        for d_ in deps:
            if id(d_) in seen:
                continue
            seen.add(id(d_))
            if d_.eng == "pe" and eng == "pe" and not d_.dma:
                continue
            dd.append(d_)
            d_.signal = True
        o.deps = dd
        for w_ in writes:
            if gid is not None and self.last_gid.get(w_) == gid:
                self.last_w[w_].append(o)
            else:
                self.last_w[w_] = [o]
                self.readers[w_] = []
                self.last_gid[w_] = gid
        for r in reads:
            self.readers.setdefault(r, []).append(o)
        self.q[eng].append(o)
        return o

    def barrier(self):
        deps = []
        for e in self.ENGS:
            ops = self.q[e]
            comp = [o for o in ops if not o.dma]
            if comp:
                deps.append(comp[-1])
            dm = [o for o in ops if o.dma]
            deps.extend(dm[-self.NDMA[e]:] if self.NDMA[e] else [])
        for d_ in deps:
            d_.signal = True
        self.bar_id += 1
        self.bar_deps = deps

    def emit(self, final_waits):
        nc = self.nc
        engobj = {"pe": nc.tensor, "act": nc.scalar, "dve": nc.vector, "pool": nc.gpsimd, "sp": nc.sync}
        NDMA = self.NDMA
        csem = {e: nc.alloc_semaphore("c_" + e) for e in self.ENGS}
        dsem = {e: [nc.alloc_semaphore("d_%s%d" % (e, i)) for i in range(NDMA[e])] for e in self.ENGS}
        for e in self.ENGS:
            cnt = 0
            k = 0
            for o in self.q[e]:
                if o.dma:
                    n = NDMA[e]
                    j = k % n
                    o.sem = dsem[e][j]
                    o.val = 16 * (k // n + 1)
                    if k >= n:
                        o.prewait = (dsem[e][j], 16 * (k // n))
                    k += 1
                elif o.signal:
                    cnt += 1
                    o.sem = csem[e]
                    o.val = cnt
        with nc.Block() as block:
            def run(e):
                def body(eng):
                    waited = {}
                    for o in self.q[e]:
                        ws = []
                        if o.prewait is not None:
                            ws.append(o.prewait)
                        for d_ in o.deps:
                            ws.append((d_.sem, d_.val))
                        need = []
                        for (sm, v) in ws:
                            if waited.get(sm.num, 0) >= v:
                                continue
                            waited[sm.num] = v
                            need.append((sm, v))
                        attach = None
                        if need and not o.dma and not o.noattach:
                            attach = need.pop()
                        for (sm, v) in need:
                            eng.wait_ge(sm, v)
                        ins = o.fn(eng)
                        first = last = ins
                        if isinstance(ins, tuple):
                            first, last = ins
                        if attach is not None:
                            first._wait_ge(attach[0], attach[1])
                        if o.signal:
                            last.then_inc(o.sem, 16 if o.dma else 1)
                    if e == "sp":
                        for o in final_waits:
                            if waited.get(o.sem.num, 0) < o.val:
                                waited[o.sem.num] = o.val
                                eng.wait_ge(o.sem, o.val)
                return body
            block.tensor(run("pe"))
            block.scalar(run("act"))
            block.vector(run("dve"))
            block.gpsimd(run("pool"))
            block.sync(run("sp"))


class Builder:
    def __init__(self, stop_after="all", dbg=()):
        self.nc = bass.Bass("TRN2", target_bir_lowering=False)
        self.s = Sched(self.nc)
        self.stop_after = stop_after
        self.dbg = set(dbg)
        self.uid = 0
        self.out_ops = []
        self.evac_rr = 0
        self.wpar = 0
        self.tails = []
        self.preloaded = None

    def din(self, name, shape, dt=F32):
        return self.nc.dram_tensor(name, list(shape), dt, kind="ExternalInput").ap()

    def dscr(self, name, shape, dt):
        kind = "ExternalOutput" if name in self.dbg else "Internal"
        return self.nc.dram_tensor(name, list(shape), dt, kind=kind).ap()

    def sb(self, name, shape, dt):
        return self.nc.alloc_sbuf_tensor(name, list(shape), dt)

    def dma(self, out, in_, reads, writes, q="sp", gid=None):
        o = self.s.op(q, lambda e, out=out, in_=in_: e.dma_start(out=out, in_=in_), reads, writes, dma=True, gid=gid)
        return o

    def mm(self, out, lhsT, rhs, start, stop, reads, writes):
        return self.s.op("pe", lambda e: e.matmul(out, lhsT, rhs, start=start, stop=stop), reads, writes)

    def mmgroup(self, items, reads, writes):
        def fn(e, items=items):
            ins = None
            first = None
            for (o_, l_, r_, st, sp_) in items:
                ins = e.matmul(o_, l_, r_, start=st, stop=sp_)
                if first is None:
                    first = ins
            return (first, ins)
        return self.s.op("pe", fn, reads, writes)

    def tgroup(self, items, reads, writes):
        def fn(e, items=items):
            ins = None
            first = None
            for (o_, i_, id_) in items:
                ins = e.transpose(o_, i_, id_)
                if first is None:
                    first = ins
            return (first, ins)
        return self.s.op("pe", fn, reads, writes)

    def act(self, out, in_, func, reads, writes, **kw):
        return self.s.op("act", lambda e: e.activation(out=out, in_=in_, func=func, **kw), reads, writes)

    def evac_engine(self):
        self.evac_rr += 1
        return "act" if self.evac_rr % 2 else "dve"

    def copy(self, eng, out, in_, reads, writes):
        if eng == "act":
            return self.s.op("act", lambda e: e.copy(out=out, in_=in_), reads, writes)
        if eng == "dve":
            return self.s.op("dve", lambda e: e.tensor_copy(out=out, in_=in_), reads, writes)
        return self.s.op("pool", lambda e: e.tensor_copy(out=out, in_=in_), reads, writes)

    def ts(self, eng, out, in0, s1, s2, op0, op1, reads, writes, accum_out=None):
        if op1 is None:
            op1 = ALU.bypass
        if accum_out is None:
            return self.s.op(eng, lambda e: e.tensor_scalar(out, in0, s1, s2, op0, op1), reads, writes)
        return self.s.op(eng, lambda e: e.tensor_scalar(out, in0, s1, s2, op0, op1, accum_out), reads, writes, noattach=True)

    def tt(self, eng, out, in0, in1, op, reads, writes):
        return self.s.op(eng, lambda e: e.tensor_tensor(out, in0, in1, op), reads, writes)

    def stt(self, out, in0, scalar, in1, op0, op1, reads, writes, accum_out=None):
        if accum_out is None:
            return self.s.op("dve", lambda e: e.scalar_tensor_tensor(out, in0, scalar, in1, op0, op1), reads, writes)
        return self.s.op("dve", lambda e: e.scalar_tensor_tensor(out, in0, scalar, in1, op0, op1, accum_out), reads, writes, noattach=True)

    def memset(self, eng, ap, val, writes):
        return self.s.op(eng, lambda e: e.memset(ap, val), (), writes)

    def declare(self):
        nc = self.nc
        I = {}
        I["xc"] = self.din("xc", [CTX, D])
        I["memb"] = self.din("memb", [NMEM, D])
        I["pm"] = self.din("pm", [128, 1])
        I["consts"] = self.din("consts", [128, 8 * 128])
        for nm, shp in [("w_in", [D, NIN]), ("w_qb", [QL, 2048]), ("w_iq", [QL, 2048]), ("wukT", [128, HA, KVL]), ("gcols", [128, 16]), ("cwl", [128, 192]),
                        ("w_uv", [HA, KVL, 128]), ("w_o", [D, D]), ("w_cq", [D, 512]),
                        ("w_ckv", [D, 1024]), ("w_co", [512, D]), ("w_ffn_in", [D, 2 * DFF]), ("w_ffn_out", [DFF, D]),
                        ("attn_norm_g", [1, D]), ("a_log", [1, HB]),
                        ("dt_bias", [1, HB]), ("delta_norm_g", [1, 128]), ("cross_norm_g", [1, D]),
                        ("mem_norm_g", [1, D]), ("ffn_norm_g", [1, D]), ("final_norm_g", [1, D])]:
            I[nm] = self.din(nm, shp)
        self.I = I
        Sx = {}
        Sx["s_ckvnT"] = self.dscr("s_ckvnT", [4, 128, CTX], BF16)
        Sx["s_kiT"] = self.dscr("s_kiT", [128, CTX], BF16)
        Sx["s_small"] = self.dscr("s_small", [CTX, 48], F32)
        Sx["s_zs"] = self.dscr("s_zs", [NOWN, 2048], BF16)
        Sx["s_qiT"] = self.dscr("s_qiT", [HA, 128, NOWN], BF16)
        Sx["s_yT"] = self.dscr("s_yT", [32, 128, NOWN], BF16)
        Sx["s_h2"] = self.dscr("s_h2", [NOWN, D], F32)
        self.out = nc.dram_tensor("out", [NOWN, D], F32, kind="ExternalOutput").ap()
        if self.dbg:
            Sx["s_gq"] = self.dscr("s_gq", [HB, 128, CTX], BF16)
            Sx["s_gk"] = self.dscr("s_gk", [HB, 128, CTX], BF16)
            Sx["s_gv"] = self.dscr("s_gv", [HB, 128, CTX], BF16)
            Sx["s_qlT"] = self.dscr("s_qlT", [HA, 4, 128, NOWN], BF16)
            Sx["s_h1"] = self.dscr("s_h1", [NOWN, D], F32)
            Sx["s_h3"] = self.dscr("s_h3", [NOWN, D], F32)
            Sx["s_actT"] = self.dscr("s_actT", [86, 128, NOWN], BF16)
        else:
            n1 = HB * 128 * CTX
            p1 = nc.dram_tensor("pool1", [3 * n1], BF16, kind="Internal").ap()
            Sx["s_gq"] = p1[0:n1].rearrange("(h d t) -> h d t", h=HB, d=128)
            Sx["s_gk"] = p1[n1:2 * n1].rearrange("(h d t) -> h d t", h=HB, d=128)
            Sx["s_gv"] = p1[2 * n1:3 * n1].rearrange("(h d t) -> h d t", h=HB, d=128)
            Sx["s_actT"] = p1[0:86 * 128 * NOWN].rearrange("(c p t) -> c p t", c=86, p=128)
            p2 = nc.dram_tensor("pool2", [NOWN * D], F32, kind="Internal").ap()
            Sx["s_h1"] = p2.rearrange("(r c) -> r c", c=D)
            Sx["s_qlT"] = p2.bitcast(BF16).rearrange("(h cc c t) -> h cc c t", h=HA, cc=4, c=128)
            Sx["s_h3"] = self.out
        self.S = Sx
        self.ps = [nc.alloc_psum_tensor("ps%d" % i, [128, 512], F32) for i in range(8)]
        self.ps_rr = 0
        self.cst = self.sb("cst", [128, 8 * 128], F32)
        self.dma(self.cst[:], I["consts"], [], ["cst"])
        self.identb = self.sb("identb", [128, 128], BF16)
        self.onesb = self.sb("onesb", [128, 128], BF16)
        self.copy("dve", self.identb[:], self.cst[:, 0:128], ["cst"], ["identb"])
        self.copy("dve", self.onesb[:], self.cst[:, 128:256], ["cst"], ["onesb"])
        self.pmt = self.sb("pmt", [128, 1], F32)
        self.dma(self.pmt[:], I["pm"], [], ["pmt"])

    def C(self, i):
        return self.cst[:, i * 128:(i + 1) * 128]

    def bank(self):
        b = self.ps_rr % 8
        self.ps_rr += 1
        return b

    def rstd_from_ss(self, rstd, ss, n, tag):
        self.act(rstd, ss, AF.Sqrt, [tag + "_ss"], [tag + "_rstd"], scale=1.0 / n, bias=EPS)
        self.s.op("dve", lambda e: e.reciprocal(rstd, rstd), [tag + "_rstd"], [tag + "_rstd"])

    def wstream_load(self, par, Wb, Wv, KC, c0, ncols):
        buf = Wb[par]
        view = buf[:, 0:KC * ncols].rearrange("p (c n) -> p c n", n=ncols)
        step = max(1, 4096 // ncols)
        self.uid += 1
        gid = ("wl", self.uid)
        r = ("W", buf.name if hasattr(buf, "name") else id(buf), par)
        k0 = 0
        while k0 < KC:
            k1 = min(KC, k0 + step)
            self.dma(view[:, k0:k1, :], Wv[:, k0:k1, c0:c0 + ncols], [], [r], q="pool", gid=gid)
            k0 = k1
        return view, [r]

    def norm_transpose_tile(self, src_rows, gB, XT, t, KC, bufs, tag, extra_writes=(), extra_reads=(), par=None):
        xt, xn, ss, rstd = bufs
        W = KC * 128
        ptag = tag if par is None else "%s%d" % (tag, par)
        gtag = tag + "_gB"
        self.dma(xt[:, 0:W], src_rows, list(extra_reads), [ptag + "_xt"] + list(extra_writes))
        self.stt(xn[:, 0:W], xt[:, 0:W], 1.0, xt[:, 0:W], ALU.mult, ALU.mult, [ptag + "_xt"],
                 [tag + "_xn", ptag + "_ss"] + list(extra_writes), accum_out=ss[:])
        self.act(rstd[:], ss[:], AF.Sqrt, [ptag + "_ss"], [ptag + "_rstd"], scale=1.0 / W, bias=EPS)
        self.s.op("dve", lambda e: e.reciprocal(rstd[:], rstd[:]), [ptag + "_rstd"], [ptag + "_rstd"])
        self.stt(xn[:, 0:W], xt[:, 0:W], rstd[:], gB[:, 0:W], ALU.mult, ALU.mult,
                 [ptag + "_xt", ptag + "_rstd", gtag], [tag + "_xn"])
        for k8 in range(0, KC, 8):
            n8 = min(8, KC - k8)
            b = self.bank()
            pT = self.ps[b][:].bitcast(BF16)
            items = [(pT[:, j * 128:(j + 1) * 128], xn[:, (k8 + j) * 128:(k8 + j + 1) * 128], self.identb[:]) for j in range(n8)]
            self.tgroup(items, [tag + "_xn", "identb"], [("ps", b)])
            dst = XT[:, k8:k8 + n8, t * 128:(t + 1) * 128]
            src = pT[:, 0:n8 * 128].rearrange("p (c n) -> p c n", n=128)
            self.copy(self.evac_engine(), dst, src, [("ps", b)], [(tag + "_XT", t)])

    def load_bcast(self, dst, src_row, W, tag):
        self.dma(dst[:, 0:W], src_row.partition_broadcast(128), [], [tag])

    def linear(self, KC, Wv, chunks, Wb, body, nxt_first=None):
        n = len(chunks)
        if n == 0:
            return
        c0, nc_, info = chunks[0]
        key = (id(Wb[0]), KC, c0, nc_, str(Wv.tensor.name) + str(Wv.offset))
        if self.preloaded is not None and self.preloaded[0] == key:
            cur = self.preloaded[1]
        else:
            cur = self.wstream_load(self.wpar, Wb, Wv, KC, c0, nc_)
        self.preloaded = None
        for i in range(n):
            nxt = None
            if i + 1 < n:
                c0n, ncn, infon = chunks[i + 1]
                nxt = self.wstream_load(1 - self.wpar, Wb, Wv, KC, c0n, ncn)
            elif nxt_first is not None:
                KCn, Wvn, c0n, ncn, Wbn = nxt_first
                if Wbn is Wb:
                    keyn = (id(Wb[0]), KCn, c0n, ncn, str(Wvn.tensor.name) + str(Wvn.offset))
                    self.preloaded = (keyn, self.wstream_load(1 - self.wpar, Wb, Wvn, KCn, c0n, ncn))
            view, wres = cur
            body(view, wres, chunks[i][2], chunks[i][1])
            cur = nxt
            self.wpar = 1 - self.wpar
        self.flush_tails()

    def fm_chunk(self, XT, xt_res, KC, TG, view, wres, ncols, epi, sub0):
        nh = (TG + 511) // 512
        for sub in range(ncols // 128):
            for hf in range(nh):
                t0 = hf * 512
                tw = min(512, TG - t0)
                b = self.bank()
                items = [(self.ps[b][:, 0:tw], view[:, k, sub * 128:(sub + 1) * 128], XT[:, k, t0:t0 + tw], k == 0, k == KC - 1)
                         for k in range(KC)]
                self.mmgroup(items, list(wres) + list(xt_res), [("ps", b)])
                self.flush_tails()
                if nh == 1:
                    tl = epi(sub0 + sub, b)
                else:
                    tl = epi(sub0 + sub, b, hf)
                if tl is not None:
                    self.tails.append(tl)

    def flush_tails(self):
        tl, self.tails = self.tails, []
        for f in tl:
            f()

    def tm_chunk(self, XT, xt_res, KC, TG, view, wres, ncols, epi):
        for t in range(TG // 128):
            b = self.bank()
            items = [(self.ps[b][:, 0:ncols], XT[:, k, t * 128:(t + 1) * 128], view[:, k, 0:ncols], k == 0, k == KC - 1)
                     for k in range(KC)]
            self.mmgroup(items, list(wres) + list(xt_res), [("ps", b)])
            epi(t, b)

    def phase_A(self):
        I, Sx = self.I, self.S
        sb = self.sb
        self.phase_begin()
        XT = sb("A_XT", [128, 32, 512], BF16)
        Wb = [sb("A_W0", [128, 8192], BF16), sb("A_W1", [128, 8192], BF16)]
        big = sb("A_big", [128, 6144], F32)
        xt = big[:, 0:4096]
        xn = big[:, 4096:6144].bitcast(BF16)
        ss = sb("A_ss", [128, 1], F32)
        rstd = sb("A_rstd", [128, 1], F32)
        gB = sb("A_gB", [128, D], F32)
        qaraw = big[:, 0:6144].rearrange("p (c n) -> p c n", n=512)
        qanT = sb("A_qanT", [128, 12, 512], BF16)
        ckvraw = sb("A_ckvraw", [128, 4, 512], F32)
        ckvn = sb("A_ckvn", [128, 4, 512], BF16)
        rsb = sb("A_rsb", [128, 512], F32)
        sqb = [sb("A_sq%d" % i, [128, 512], BF16) for i in range(3)]
        cbuf = [sb("A_cb%d" % i, [128, 516], F32) for i in range(3)]
        cacc = [sb("A_ca%d" % i, [128, 512], F32) for i in range(3)]
        csil = [sb("A_cs%d" % i, [128, 512], F32) for i in range(3)]
        cout = [sb("A_co%d" % i, [128, 512], BF16) for i in range(3)]
        rn = [sb("A_rn%d" % i, [128, 512], F32) for i in range(3)]
        carry = sb("A_carry", [128, 48, 4], F32)
        cw = sb("A_cw", [128, 48, 4], F32)
        Wsm = sb("A_Wsm", [128, 32, 48], BF16)
        smt = [sb("A_smt%d" % i, [128, 48], F32) for i in range(2)]
        smo = [sb("A_smo%d" % i, [128, 48], F32) for i in range(2)]
        abc = sb("A_abc", [128, 3, 16], F32)
        gqa = sb("A_gqa", [128, 12], F32)
        gkv = sb("A_gkv", [128, 4], F32)
        wukT = sb("A_wukT", [128, HA, 512], BF16)
        qTh = [sb("A_qTh%d" % i, [128, 512], BF16) for i in range(2)]
        ql = [sb("A_ql%d" % i, [128, 512], BF16) for i in range(3)]
        zso = [sb("A_zso%d" % i, [128, 256], BF16) for i in range(3)]
        kio = [sb("A_kio%d" % i, [128, 512], BF16) for i in range(2)]

        self.load_bcast(gB, I["attn_norm_g"], D, "A_gB")
        self.dma(gqa[:], I["gcols"][:, 0:12], [], ["A_gqa"])
        self.dma(gkv[:], I["gcols"][:, 12:16], [], ["A_gkv"])
        self.dma(cw[:], I["cwl"].rearrange("p (c k) -> p c k", k=4), [], ["A_cw"])
        self.memset("pool", carry[:], 0.0, ["A_carry"])
        self.dma(abc[:, 0, :], I["dt_bias"].partition_broadcast(128), [], ["A_abc0"])
        self.dma(abc[:, 2, :], I["a_log"].partition_broadcast(128), [], ["A_abc2"])
        self.act(abc[:, 1, :], abc[:, 2, :], AF.Exp, ["A_abc2"], ["A_abc1"])
        self.ts("dve", abc[:, 1, :], abc[:, 1, :], -1.0, None, ALU.mult, None, ["A_abc1"], ["A_abc1"])
        win = I["w_in"].rearrange("(c p) n -> p c n", p=128)
        self.dma(Wsm[:, :, 0:16], win[:, :, WI0:WI0 + 16], [], ["A_Wsm0"], q="pool")
        self.dma(Wsm[:, :, 16:48], win[:, :, B0:B0 + 32], [], ["A_Wsm1"], q="pool")
        for h in range(0, HA, 4):
            self.dma(wukT[:, h:h + 4, :], I["wukT"][:, h:h + 4, :], [], [("A_wukT", h + j) for j in range(4)], q="pool")
        wqb = I["w_qb"].rearrange("(c p) n -> p c n", p=128)
        wiq = I["w_iq"].rearrange("(c p) n -> p c n", p=128)

        for g in range(8):
            own = g >= 4
            og = g - 4
            T0 = g * 512
            xt_res = [("A_XT", t) for t in range(4)]
            for t in range(4):
                self.norm_transpose_tile(I["xc"][T0 + t * 128:T0 + (t + 1) * 128, :], gB, XT, t, 32,
                                         (xt, xn, ss, rstd), "A", extra_writes=[("A_qaraw", c) for c in range(12)])
            for t in range(4):
                b = self.bank()
                items = [(self.ps[b][:, 0:48], XT[:, k, t * 128:(t + 1) * 128], Wsm[:, k, :], k == 0, k == 31) for k in range(32)]
                self.mmgroup(items, ["A_Wsm0", "A_Wsm1", ("A_XT", t)], [("ps", b)])
                i2 = t % 2
                st, so = smt[i2], smo[i2]
                self.copy("dve", st[:], self.ps[b][:, 0:48], [("ps", b)], [("A_smt", i2)])
                self.ts("dve", so[:, 0:16], st[:, 0:16], 0.25, None, ALU.mult, None, [("A_smt", i2)], [("A_smo", i2, 0)])
                self.act(so[:, 16:32], st[:, 16:32], AF.Sigmoid, [("A_smt", i2)], [("A_smo", i2, 1)])
                self.tt("dve", st[:, 32:48], st[:, 32:48], abc[:, 0, :], ALU.add, [("A_smt", i2), "A_abc0"], [("A_smt", i2)])
                self.act(st[:, 32:48], st[:, 32:48], AF.Exp, [("A_smt", i2)], [("A_smt", i2)])
                self.act(st[:, 32:48], st[:, 32:48], AF.Ln, [("A_smt", i2)], [("A_smt", i2)], bias=1.0)
                self.tt("dve", so[:, 32:48], st[:, 32:48], abc[:, 1, :], ALU.mult, [("A_smt", i2), "A_abc1"], [("A_smo", i2, 2)])
                self.dma(Sx["s_small"][T0 + t * 128:T0 + (t + 1) * 128, :], so[:],
                         [("A_smo", i2, 0), ("A_smo", i2, 1), ("A_smo", i2, 2)], [("s_small", g, t)])

            chunks = []
            if own:
                for c in range(0, 12, 2):
                    chunks.append((QA0 + c * 128, 256, ("qa", c)))
            for c in range(0, 4, 2):
                chunks.append((CKV0 + c * 128, 256, ("ckv", c)))
            chunks.append((KI0, 128, ("ki", 0)))
            q_from = 0 if g >= 3 else 16
            for c in range(q_from, 48, 2):
                chunks.append((QKV0 + c * 128, 256, ("qkv", c)))

            def epi_factory(info):
                fam, cbase = info

                def epi(sub, b):
                    c = cbase + sub
                    P = self.ps[b][:, 0:512]
                    if fam == "qa":
                        self.copy("act", qaraw[:, c, :], P, [("ps", b)], [("A_qaraw", c)])
                    elif fam == "ckv":
                        self.copy("act", ckvraw[:, c, :], P, [("ps", b)], [("A_ckvraw", c)])
                    elif fam == "ki":
                        i2 = g % 2
                        self.copy("act", kio[i2][:], P, [("ps", b)], [("A_kio", i2)])
                        self.dma(Sx["s_kiT"][:, T0:T0 + 512], kio[i2][:], [("A_kio", i2)], [("s_kiT", g)])
                    else:
                        return self.qkv_epilogue(c, b, g, T0, cbuf, cacc, csil, cout, rn, sqb, carry, cw)
                    return None
                return epi

            def body(view, wres, info, ncols):
                self.fm_chunk(XT, xt_res, 32, 512, view, wres, ncols, epi_factory(info), 0)
            if own:
                nf = (32, win, Z0, 256, Wb)
            elif g + 1 < 8:
                nf = (32, win, QA0 if g + 1 >= 4 else CKV0, 256, Wb)
            else:
                nf = None
            self.linear(32, win, chunks, Wb, body, nxt_first=nf)

            self.fm_rmsnorm(ckvraw, "A_ckvraw", 4, KVL, gkv, "A_gkv", ckvn, "A_ckvn", sqb, rsb)
            for c in range(4):
                self.dma(Sx["s_ckvnT"][c, :, T0:T0 + 512], ckvn[:, c, :], [("A_ckvn", c)], [("s_ckvnT", c, g)])

            if own:
                def zbody(view, wres, info, ncols):
                    def epi(t, b):
                        self.uid += 1
                        i3 = self.uid % 3
                        self.act(zso[i3][:, 0:ncols], self.ps[b][:, 0:ncols], AF.Silu, [("ps", b)], [("A_zso", i3)])
                        r0 = og * 512 + t * 128
                        self.dma(Sx["s_zs"][r0:r0 + 128, info:info + ncols], zso[i3][:, 0:ncols], [("A_zso", i3)],
                                 [("s_zs", og, t, info)])
                    self.tm_chunk(XT, xt_res, 32, 512, view, wres, ncols, epi)
                self.linear(32, win, [(Z0 + j * 256, 256, j * 256) for j in range(8)], Wb, zbody, nxt_first=(12, wqb, 0, 512, Wb))

                self.fm_rmsnorm(qaraw, "A_qaraw", 12, QL, gqa, "A_gqa", qanT, "A_qanT", sqb, rsb)
                qres = [("A_qanT", c) for c in range(12)]

                def qbody(view, wres, info, ncols):
                    def epi(h, b):
                        i2 = h % 2
                        self.copy("act", qTh[i2][:], self.ps[b][:, 0:512], [("ps", b)], [("A_qTh", i2)])
                        for cc in range(4):
                            b2 = self.bank()
                            self.mm(self.ps[b2][:, 0:512], wukT[:, h, cc * 128:(cc + 1) * 128], qTh[i2][:], True, True,
                                    [("A_wukT", h), ("A_qTh", i2)], [("ps", b2)])
                            self.uid += 1
                            i3 = self.uid % 3
                            self.copy(self.evac_engine(), ql[i3][:], self.ps[b2][:, 0:512], [("ps", b2)], [("A_ql", i3)])
                            self.dma(Sx["s_qlT"][h, cc, :, og * 512:(og + 1) * 512], ql[i3][:], [("A_ql", i3)],
                                     [("s_qlT", h, cc, og)])
                    self.fm_chunk(qanT, qres, 12, 512, view, wres, ncols, epi, info)
                self.linear(12, wqb, [(j * 512, 512, j * 4) for j in range(4)], Wb, qbody, nxt_first=(12, wiq, 0, 512, Wb))

                def ibody(view, wres, info, ncols):
                    def epi(h, b):
                        self.uid += 1
                        i3 = self.uid % 3
                        self.copy(self.evac_engine(), ql[i3][:], self.ps[b][:, 0:512], [("ps", b)], [("A_ql", i3)])
                        self.dma(Sx["s_qiT"][h, :, og * 512:(og + 1) * 512], ql[i3][:], [("A_ql", i3)], [("s_qiT", h, og)])
                    self.fm_chunk(qanT, qres, 12, 512, view, wres, ncols, epi, info)
                self.linear(12, wiq, [(j * 512, 512, j * 4) for j in range(4)], Wb, ibody,
                            nxt_first=(32, win, QA0, 256, Wb) if g + 1 < 8 else None)
        self.phase_end()

    def fm_rmsnorm(self, raw, rawtag, KC, n, gcol, gtag, outT, outtag, sqb, rsb):
        b = self.bank()
        for c in range(KC):
            i2 = c % 2
            self.tt("dve", sqb[i2][:], raw[:, c, :], raw[:, c, :], ALU.mult, [(rawtag, c)], [("A_sq", i2)])
            self.mm(self.ps[b][:, 0:512], self.onesb[:], sqb[i2][:], c == 0, c == KC - 1, ["onesb", ("A_sq", i2)], [("ps", b)])
        self.act(rsb[:], self.ps[b][:, 0:512], AF.Sqrt, [("ps", b)], ["A_rsb"], scale=1.0 / n, bias=EPS)
        self.s.op("dve", lambda e: e.reciprocal(rsb[:], rsb[:]), ["A_rsb"], ["A_rsb"])
        for c in range(KC):
            self.stt(outT[:, c, :], raw[:, c, :], gcol[:, c:c + 1], rsb[:], ALU.mult, ALU.mult,
                     [(rawtag, c), gtag, "A_rsb"], [(outtag, c)])

    def qkv_epilogue(self, c, b, g, T0, cbuf, cacc, csil, cout, rn, sqb, carry, cw):
        Sx = self.S
        i2 = c % 3
        cb, ca, cs = cbuf[i2], cacc[i2], csil[i2]
        self.copy("pool", cb[:, 0:3], carry[:, c, 0:3], [("A_carry", c)], [("A_cb", i2, 0)])
        self.copy("act", cb[:, 3:515], self.ps[b][:, 0:512], [("ps", b)], [("A_cb", i2, 1)])
        rd = [("A_cb", i2, 0), ("A_cb", i2, 1), "A_cw"]
        self.ts("dve", ca[:], cb[:, 3:515], cw[:, c, 3:4], None, ALU.mult, None, rd, [("A_ca", i2)])
        for k in (2, 1, 0):
            self.stt(ca[:], cb[:, k:k + 512], cw[:, c, k:k + 1], ca[:], ALU.mult, ALU.add, rd + [("A_ca", i2)], [("A_ca", i2)])
        self.copy("pool", carry[:, c, 0:3], cb[:, 512:515], [("A_cb", i2, 1)], [("A_carry", c)])
        self.uid += 1
        i3 = self.uid % 3
        co = cout[i3]
        fam = c // 16
        h = c % 16
        if fam == 2:
            self.act(co[:], ca[:], AF.Silu, [("A_ca", i2)], [("A_co", i3)])
            self.dma(Sx["s_gv"][h, :, T0:T0 + 512], co[:], [("A_co", i3)], [("s_gv", h, g)])
            return
        self.act(cs[:], ca[:], AF.Silu, [("A_ca", i2)], [("A_cs", i2)])
        self.tt("dve", sqb[i2][:], cs[:], cs[:], ALU.mult, [("A_cs", i2)], [("A_sq", i2)])
        return lambda: self.qkv_tail(c, g, T0, i2, i3, co, cs, rn, sqb, fam, h)

    def qkv_tail(self, c, g, T0, i2, i3, co, cs, rn, sqb, fam, h):
        Sx = self.S
        b2 = self.bank()
        self.mm(self.ps[b2][:, 0:512], self.onesb[:], sqb[i2][:], True, True, ["onesb", ("A_sq", i2)], [("ps", b2)])
        r = rn[i2]
        self.act(r[:], self.ps[b2][:, 0:512], AF.Sqrt, [("ps", b2)], [("A_rn", i2)], scale=1.0, bias=EPS)
        self.s.op("dve", lambda e: e.reciprocal(r[:], r[:]), [("A_rn", i2)], [("A_rn", i2)])
        if fam == 0:
            self.stt(co[:], cs[:], 128.0 ** -0.5, r[:], ALU.mult, ALU.mult, [("A_cs", i2), ("A_rn", i2)], [("A_co", i3)])
            self.dma(Sx["s_gq"][h, :, T0:T0 + 512], co[:], [("A_co", i3)], [("s_gq", h, g)])
        else:
            self.tt("dve", co[:], cs[:], r[:], ALU.mult, [("A_cs", i2), ("A_rn", i2)], [("A_co", i3)])
            self.dma(Sx["s_gk"][h, :, T0:T0 + 512], co[:], [("A_co", i3)], [("s_gk", h, g)])


    def phase_begin(self):
        self._sb_mark = self.nc.sbuf_base

    def phase_end(self):
        self.s.barrier()
        self.nc.sbuf_base = self._sb_mark

    def phase_0(self):
        I = self.I
        sb = self.sb
        self.kxT = sb("P_kxT", [128, 4, NMEM], BF16)
        self.vx = sb("P_vx", [128, 2, 512], BF16)
        self.phase_begin()
        XT = sb("0_XT", [128, 32, 256], BF16)
        Wb = [sb("0_W0", [128, 8192], BF16), sb("0_W1", [128, 8192], BF16)]
        xt = sb("0_xt", [128, D], F32)
        xn = sb("0_xn", [128, D], BF16)
        ss = sb("0_ss", [128, 1], F32)
        rstd = sb("0_rstd", [128, 1], F32)
        gB = sb("0_gB", [128, D], F32)
        self.load_bcast(gB, I["mem_norm_g"], D, "0_gB")
        for t in range(2):
            self.norm_transpose_tile(I["memb"][t * 128:(t + 1) * 128, :], gB, XT, t, 32, (xt, xn, ss, rstd), "0")
        xres = [("0_XT", 0), ("0_XT", 1)]
        wckv = I["w_ckv"].rearrange("(c p) n -> p c n", p=128)

        def kbody(view, wres, info, ncols):
            def epi(h, b):
                self.copy("act", self.kxT[:, h, :], self.ps[b][:, 0:NMEM], [("ps", b)], [("P_kxT", h)])
            self.fm_chunk(XT, xres, 32, NMEM, view, wres, ncols, epi, info)
        self.linear(32, wckv, [(0, 256, 0), (256, 256, 2)], Wb, kbody)

        def vbody(view, wres, info, ncols):
            def epi(t, b):
                self.copy("act", self.vx[:, t, info:info + ncols], self.ps[b][:, 0:ncols], [("ps", b)], [("P_vx", t, info)])
            self.tm_chunk(XT, xres, 32, NMEM, view, wres, ncols, epi)
        self.linear(32, wckv, [(512, 256, 0), (768, 256, 256)], Wb, vbody)
        self.phase_end()

    def phase_B(self):
        I, Sx = self.I, self.S
        sb = self.sb
        self.phase_begin()
        S32 = sb("B_S32", [128, HB, 128], F32)
        Sb = sb("B_Sb", [128, HB, 128], BF16)
        gdnB = sb("B_gdnB", [128, 128], F32)
        kTa = [sb("B_kT%d" % i, [128, HB, 128], BF16) for i in range(2)]
        vTa = [sb("B_vT%d" % i, [128, HB, 128], BF16) for i in range(2)]
        qTa = [sb("B_qT%d" % i, [128, HB, 128], BF16) for i in range(2)]
        zsa = [sb("B_zs%d" % i, [128, 2048], BF16) for i in range(2)]
        sma = [sb("B_sm%d" % i, [128, 48], F32) for i in range(2)]
        gga = [sb("B_gg%d" % i, [128, 80], F32) for i in range(2)]
        bGa = [sb("B_bG%d" % i, [128, 16], F32) for i in range(2)]
        yb = [sb("B_yb%d" % i, [128, HB, 128], BF16) for i in range(2)]
        ybT = [sb("B_ybT%d" % i, [128, HB, 128], BF16) for i in range(2)]
        NS = 8
        Ug = [sb("B_Ug%d" % i, [128, 128], F32) for i in range(NS)]
        E = [sb("B_E%d" % i, [128, 128], F32) for i in range(NS)]
        Esn = [sb("B_Esn%d" % i, [128, 128], F32) for i in range(NS)]
        Ei = [sb("B_Ei%d" % i, [128, 128], F32) for i in range(NS)]
        kdec = [sb("B_kdec%d" % i, [128, 128], BF16) for i in range(NS)]
        X = [[sb("B_X%d_%d" % (i, j), [128, 256], F32) for j in range(2)] for i in range(NS)]
        BB = [[sb("B_BB%d_%d" % (i, j), [128, 256], F32) for j in range(2)] for i in range(NS)]
        intra = [sb("B_in%d" % i, [128, 128], BF16) for i in range(NS)]
        intraT = [sb("B_inT%d" % i, [128, 128], BF16) for i in range(NS)]
        wT = [sb("B_wT%d" % i, [128, 128], BF16) for i in range(NS)]
        vnew = [sb("B_vn%d" % i, [128, 128], BF16) for i in range(NS)]
        o1 = [sb("B_o1%d" % i, [128, 128], F32) for i in range(NS)]
        oo = [sb("B_oo%d" % i, [128, 128], F32) for i in range(NS)]
        osq = [sb("B_osq%d" % i, [128, 128], BF16) for i in range(NS)]
        oss = [sb("B_oss%d" % i, [128, 1], F32) for i in range(NS)]
        ors = [sb("B_ors%d" % i, [128, 1], F32) for i in range(NS)]
        self.memset("pool", S32[:], 0.0, [("B_S32", h) for h in range(HB)])
        self.memset("pool", Sb[:], 0.0, [("B_Sb", h) for h in range(HB)])
        self.load_bcast(gdnB, I["delta_norm_g"], 128, "B_gdnB")
        Uc, ONESf, Lst, NSTR, INCL, IDf = self.C(2), self.C(1), self.C(3), self.C(5), self.C(6), self.C(0)

        for T in range(32):
            own = T >= 16
            p2 = T % 2
            c0 = T * 128
            kT, vT, qT, zs, sm, gg, bG = kTa[p2], vTa[p2], qTa[p2], zsa[p2], sma[p2], gga[p2], bGa[p2]
            self.dma(kT[:], Sx["s_gk"][:, :, c0:c0 + 128].rearrange("h d t -> d h t"),
                     [("s_gk", h, T // 4) for h in range(HB)], [("B_kT", p2)])
            self.dma(vT[:], Sx["s_gv"][:, :, c0:c0 + 128].rearrange("h d t -> d h t"),
                     [("s_gv", h, T // 4) for h in range(HB)], [("B_vT", p2)])
            self.dma(sm[:], Sx["s_small"][c0:c0 + 128, :], [("s_small", T // 4, T % 4)], [("B_sm", p2)])
            if own:
                ot = T - 16
                self.dma(qT[:], Sx["s_gq"][:, :, c0:c0 + 128].rearrange("h d t -> d h t"),
                         [("s_gq", h, T // 4) for h in range(HB)], [("B_qT", p2)])
                self.dma(zs[:], Sx["s_zs"][ot * 128:(ot + 1) * 128, :],
                         [("s_zs", ot // 4, ot % 4, j * 256) for j in range(8)], [("B_zs", p2)])
            beta = sm[:, 16:32]
            gcol = sm[:, 32:48]
            b = self.bank()
            self.mm(self.ps[b][:, 0:16], Uc, gcol, True, True, ["cst", ("B_sm", p2)], [("ps", b)])
            self.mm(self.ps[b][:, 16:32], ONESf, gcol, True, True, ["cst", ("B_sm", p2)], [("ps", b)])
            self.copy("dve", gg[:, 0:32], self.ps[b][:, 0:32], [("ps", b)], [("B_gg", p2)])
            self.tt("dve", gg[:, 48:64], gg[:, 16:32], gg[:, 0:16], ALU.subtract, [("B_gg", p2)], [("B_gg", p2)])
            self.act(gg[:, 32:48], gg[:, 0:16], AF.Exp, [("B_gg", p2)], [("B_gg", p2)])
            self.act(gg[:, 48:64], gg[:, 48:64], AF.Exp, [("B_gg", p2)], [("B_gg", p2)])
            self.act(gg[:, 64:80], gg[:, 16:32], AF.Exp, [("B_gg", p2)], [("B_gg", p2)])
            self.tt("dve", bG[:], beta, gg[:, 32:48], ALU.mult, [("B_sm", p2), ("B_gg", p2)], [("B_bG", p2)])
            tile_res = [("B_sm", p2), ("B_gg", p2), ("B_bG", p2)]

            def head_gen(h, s_):
                kTh, vTh, qTh = kT[:, h, :], vT[:, h, :], qT[:, h, :]
                b = self.bank()
                pT = self.ps[b][:].bitcast(BF16)
                self.tgroup([(pT[:, 0:128], kTh, self.identb[:]), (pT[:, 128:256], vTh, self.identb[:])],
                            [("B_kT", p2), ("B_vT", p2), "identb"], [("ps", b)])
                self.act(kdec[s_][:], pT[:, 0:128], AF.Copy, [("ps", b)] + tile_res, [("B_kdec", s_)], scale=gg[:, 48 + h:49 + h])
                X0 = X[s_][0]
                self.act(X0[:, 128:256].bitcast(F32R), pT[:, 0:128], AF.Copy, [("ps", b)] + tile_res, [("B_X", s_, 0, 1)], scale=bG[:, h:h + 1])
                self.act(X0[:, 0:128].bitcast(F32R), pT[:, 128:256], AF.Copy, [("ps", b)] + tile_res, [("B_X", s_, 0, 0)], scale=beta[:, h:h + 1])
                yield
                self.act(Ug[s_][:], Uc, AF.Copy, ["cst"] + tile_res, [("B_Ug", s_)], scale=gcol[:, h:h + 1])
                b = self.bank()
                self.mm(self.ps[b][:, 0:128], Ug[s_][:], Lst, True, True, [("B_Ug", s_), "cst"], [("ps", b)])
                self.act(E[s_][:], self.ps[b][:, 0:128], AF.Exp, [("ps", b)], [("B_E", s_)])
                self.tt("pool", Esn[s_][:], E[s_][:], NSTR, ALU.mult, [("B_E", s_), "cst"], [("B_Esn", s_)])
                yield
                b = self.bank()
                self.mm(self.ps[b][:, 0:128], kTh, kTh, True, True, [("B_kT", p2)], [("ps", b)])
                B0 = BB[s_][0]
                self.stt(B0[:, 0:128].bitcast(F32R), self.ps[b][:, 0:128], beta[:, h:h + 1], Esn[s_][:], ALU.mult, ALU.mult,
                         [("ps", b), ("B_Esn", s_)] + tile_res, [("B_BB", s_, 0, 0)])
                if own:
                    self.tt("pool", Ei[s_][:], E[s_][:], INCL, ALU.mult, [("B_E", s_), "cst"], [("B_Ei", s_)])
                    b = self.bank()
                    self.mm(self.ps[b][:, 0:128], qTh, kTh, True, True, [("B_kT", p2), ("B_qT", p2)], [("ps", b)])
                    self.tt("dve", intra[s_][:], self.ps[b][:, 0:128], Ei[s_][:], ALU.mult, [("ps", b), ("B_Ei", s_)], [("B_in", s_)])
                yield
                b = self.bank()
                self.tgroup([(self.ps[b][:, 0:128], B0[:, 0:128], IDf)], [("B_BB", s_, 0, 0), "cst"], [("ps", b)])
                self.copy("act", B0[:, 128:256].bitcast(F32R), self.ps[b][:, 0:128], [("ps", b)], [("B_BB", s_, 0, 1)])
                if own:
                    b = self.bank()
                    pT = self.ps[b][:].bitcast(BF16)
                    self.tgroup([(pT[:, 0:128], intra[s_][:], self.identb[:])], [("B_in", s_), "identb"], [("ps", b)])
                    self.copy("act", intraT[s_][:], pT[:, 0:128], [("ps", b)], [("B_inT", s_)])
                yield
                for lv in range(7):
                    cur, nx = lv % 2, (lv + 1) % 2
                    Bc, Xc, Xn = BB[s_][cur], X[s_][cur], X[s_][nx]
                    b = self.bank()
                    self.mm(self.ps[b][:, 0:256], Bc[:, 128:256].bitcast(F32R), Xc[:].bitcast(F32R), True, True,
                            [("B_BB", s_, cur, 1), ("B_X", s_, cur, 0), ("B_X", s_, cur, 1)], [("ps", b)])
                    self.tt("dve", Xn[:].bitcast(F32R), self.ps[b][:, 0:256], Xc[:], ALU.add,
                            [("ps", b), ("B_X", s_, cur, 0), ("B_X", s_, cur, 1)], [("B_X", s_, nx, 0), ("B_X", s_, nx, 1)])
                    if lv < 6:
                        Bn = BB[s_][nx]
                        b = self.bank()
                        self.mmgroup([(self.ps[b][:, 0:128], Bc[:, 128:256].bitcast(F32R), Bc[:, 0:128].bitcast(F32R), True, True),
                                      (self.ps[b][:, 128:256], Bc[:, 0:128].bitcast(F32R), Bc[:, 128:256].bitcast(F32R), True, True)],
                                     [("B_BB", s_, cur, 0), ("B_BB", s_, cur, 1)], [("ps", b)])
                        self.copy("act" if lv % 2 == 0 else "dve", Bn[:].bitcast(F32R), self.ps[b][:, 0:256], [("ps", b)],
                                  [("B_BB", s_, nx, 0), ("B_BB", s_, nx, 1)])
                    yield
                Xf = X[s_][1]
                xres = [("B_X", s_, 1, 0), ("B_X", s_, 1, 1)]
                b = self.bank()
                self.tgroup([(self.ps[b][:, 0:128], Xf[:, 128:256], IDf)], xres + ["cst"], [("ps", b)])
                self.copy("act", wT[s_][:], self.ps[b][:, 0:128], [("ps", b)], [("B_wT", s_)])
                yield
                b = self.bank()
                self.mm(self.ps[b][:, 0:128], wT[s_][:], Sb[:, h, :], True, True, [("B_wT", s_), ("B_Sb", h)], [("ps", b)])
                self.tt("dve", vnew[s_][:], Xf[:, 0:128], self.ps[b][:, 0:128], ALU.subtract, [("ps", b)] + xres, [("B_vn", s_)])
                yield
                if own:
                    b = self.bank()
                    self.mm(self.ps[b][:, 0:128], qTh, Sb[:, h, :], True, True, [("B_qT", p2), ("B_Sb", h)], [("ps", b)])
                    self.act(o1[s_][:], self.ps[b][:, 0:128], AF.Copy, [("ps", b)] + tile_res, [("B_o1", s_)], scale=gg[:, 32 + h:33 + h])
                    b = self.bank()
                    self.mm(self.ps[b][:, 0:128], intraT[s_][:], vnew[s_][:], True, True, [("B_inT", s_), ("B_vn", s_)], [("ps", b)])
                    self.tt("dve", oo[s_][:], self.ps[b][:, 0:128], o1[s_][:], ALU.add, [("ps", b), ("B_o1", s_)], [("B_oo", s_)])
                    self.stt(osq[s_][:], oo[s_][:], 1.0, oo[s_][:], ALU.mult, ALU.mult, [("B_oo", s_)], [("B_osq", s_), ("B_oss", s_)],
                             accum_out=oss[s_][:])
                    self.act(ors[s_][:], oss[s_][:], AF.Sqrt, [("B_oss", s_)], [("B_ors", s_)], scale=1.0 / 128, bias=EPS)
                    self.s.op("dve", lambda e, r=ors[s_]: e.reciprocal(r[:], r[:]), [("B_ors", s_)], [("B_ors", s_)])
                    self.stt(oo[s_][:], oo[s_][:], ors[s_][:], gdnB[:], ALU.mult, ALU.mult, [("B_oo", s_), ("B_ors", s_), "B_gdnB"], [("B_oo", s_)])
                    self.tt("dve", yb[p2][:, h, :], oo[s_][:], zs[:, h * 128:(h + 1) * 128], ALU.mult, [("B_oo", s_), ("B_zs", p2)], [("B_yb", p2, h)])
                yield
                b = self.bank()
                self.mm(self.ps[b][:, 0:128], kdec[s_][:], vnew[s_][:], True, True, [("B_kdec", s_), ("B_vn", s_)], [("ps", b)])
                self.stt(S32[:, h, :], S32[:, h, :], gg[:, 64 + h:65 + h], self.ps[b][:, 0:128], ALU.mult, ALU.add,
                         [("ps", b), ("B_S32", h)] + tile_res, [("B_S32", h)])
                self.copy("pool", Sb[:, h, :], S32[:, h, :], [("B_S32", h)], [("B_Sb", h)])
            for g0 in range(0, HB, NS):
                gens = [head_gen(h, h - g0) for h in range(g0, g0 + NS)]
                while gens:
                    for g_ in list(gens):
                        try:
                            next(g_)
                        except StopIteration:
                            gens.remove(g_)
            if own:
                ot = T - 16
                for h0 in range(0, HB, 8):
                    b = self.bank()
                    pT = self.ps[b][:].bitcast(BF16)
                    self.tgroup([(pT[:, j * 128:(j + 1) * 128], yb[p2][:, h0 + j, :], self.identb[:]) for j in range(8)],
                                [("B_yb", p2, h0 + j) for j in range(8)] + ["identb"], [("ps", b)])
                    self.copy(self.evac_engine(), ybT[p2][:, h0:h0 + 8, :], pT[:, 0:1024].rearrange("p (c n) -> p c n", n=128),
                              [("ps", b)], [("B_ybT", p2, h0)])
                self.dma(Sx["s_yT"][16:32, :, ot * 128:(ot + 1) * 128].rearrange("h d t -> d h t"), ybT[p2][:],
                         [("B_ybT", p2, 0), ("B_ybT", p2, 8)], [("s_yT", 1, ot)])
        self.phase_end()


    def phase_C(self):
        I, Sx = self.I, self.S
        sb = self.sb
        self.phase_begin()
        ckT = sb("C_ckT", [128, 4, CTX], BF16)
        ckM = sb("C_ckM", [128, 32, 512], BF16)
        kiT = sb("C_kiT", [128, CTX], BF16)
        wuv = sb("C_wuv", [128, HA, 4, 128], BF16)
        qiT = [sb("C_qiT%d" % i, [128, HA, 128], BF16) for i in range(2)]
        qlT = [sb("C_qlT%d" % i, [128, HA, 4, 128], BF16) for i in range(2)]
        wi = [sb("C_wi%d" % i, [128, 48], F32) for i in range(2)]
        wab = [sb("C_wab%d" % i, [128, 16], F32) for i in range(2)]
        wsg = [sb("C_wsg%d" % i, [128, 16], F32) for i in range(2)]
        sc = sb("C_sc", [128, CTX], F32)
        selm = sb("C_selm", [128, CTX], BF16)
        negm = [sb("C_negm%d" % i, [128, 32, 128], BF16) for i in range(2)]
        rl = [sb("C_rl%d" % i, [128, 512], F32) for i in range(2)]
        st = sb("C_st", [128, 8], F32)
        Pm = [sb("C_P%d" % i, [128, 4, 128], BF16) for i in range(3)]
        rden = sb("C_rden", [128, 512], F32)
        olat = sb("C_olat", [128, 4, 512], BF16)
        yaT = [sb("C_yaT%d" % i, [128, HA, 128], BF16) for i in range(2)]
        for cc in range(4):
            self.dma(ckT[:, cc, :], Sx["s_ckvnT"][cc], [("s_ckvnT", cc, g) for g in range(8)], [("C_ckT", cc)])
        self.dma(kiT[:], Sx["s_kiT"], [("s_kiT", g) for g in range(8)], ["C_kiT"])
        for h0 in range(0, HA, 4):
            self.dma(wuv[:, h0:h0 + 4, :, :], I["w_uv"][h0:h0 + 4].rearrange("h (cc p) e -> p h cc e", p=128), [], [("C_wuv", h0)], q="pool")
        for kb in range(32):
            b = self.bank()
            pT = self.ps[b][:].bitcast(BF16)
            self.tgroup([(pT[:, cc * 128:(cc + 1) * 128], ckT[:, cc, kb * 128:(kb + 1) * 128], self.identb[:]) for cc in range(4)],
                        [("C_ckT", cc) for cc in range(4)] + ["identb"], [("ps", b)])
            self.copy(self.evac_engine(), ckM[:, kb, :], pT[:, 0:512], [("ps", b)], [("C_ckM", kb)])
        ckres = [("C_ckT", cc) for cc in range(4)]
        CM = self.C(4)
        SCALE = 128.0 ** -0.5
        PACC = [0, 1, 2, 3]
        PDEN = 4
        PLOG = [5, 6]
        PMISC = 7
        self._lrr = 0

        def score_gen(qt):
            T = 16 + qt
            nkb = T + 1
            nk = nkb * 128
            p2 = qt % 2
            q0 = qt * 128
            self.dma(qiT[p2][:], Sx["s_qiT"][:, :, q0:q0 + 128].rearrange("h d t -> d h t"),
                     [("s_qiT", h, qt // 4) for h in range(HA)], [("C_qiT", p2)])
            self.uid += 1
            for h0 in range(0, HA, 4):
                self.dma(qlT[p2][:, h0:h0 + 4, :, :], Sx["s_qlT"][h0:h0 + 4, :, :, q0:q0 + 128].rearrange("h cc c t -> c h cc t"),
                         [("s_qlT", h, cc, qt // 4) for h in range(h0, h0 + 4) for cc in range(4)], [("C_qlT", p2)], gid=("ql", self.uid))
            self.dma(wi[p2][:], Sx["s_small"][OWN0 + q0:OWN0 + q0 + 128, :], [("s_small", T // 4, T % 4)], [("C_wi", p2)])
            self.act(wab[p2][:], wi[p2][:, 0:16], AF.Abs, [("C_wi", p2)], [("C_wab", p2)], scale=SCALE)
            self.act(wsg[p2][:], wi[p2][:, 0:16], AF.Sign, [("C_wi", p2)], [("C_wsg", p2)])
            yield
            nch = (nk + 511) // 512
            for ch in range(nch):
                k0 = ch * 512
                kw = min(512, nk - k0)
                for h in range(HA):
                    b = PMISC
                    self.mm(self.ps[b][:, 0:kw], qiT[p2][:, h, :], kiT[:, k0:k0 + kw], True, True, [("C_qiT", p2), "C_kiT"], [("ps", b)])
                    r = rl[h % 2]
                    self.act(r[:, 0:kw], self.ps[b][:, 0:kw], AF.Relu, [("ps", b), ("C_wab", p2)], [("C_rl", h % 2)], scale=wab[p2][:, h:h + 1])
                    if h == 0:
                        self.ts("dve", sc[:, k0:k0 + kw], r[:, 0:kw], wsg[p2][:, 0:1], None, ALU.mult, None,
                                [("C_rl", 0), ("C_wsg", p2)], [("C_sc", ch)])
                    else:
                        self.stt(sc[:, k0:k0 + kw], r[:, 0:kw], wsg[p2][:, h:h + 1], sc[:, k0:k0 + kw], ALU.mult, ALU.add,
                                 [("C_rl", h % 2), ("C_wsg", p2), ("C_sc", ch)], [("C_sc", ch)])
                    yield
            scres = [("C_sc", ch) for ch in range(nch)]
            self.s.op("dve", lambda e, nk=nk: e.tensor_reduce(st[:, 0:1], sc[:, 0:nk], AX.X, ALU.max, apply_absolute_value=True),
                      scres, [("C_st", 0)])
            self.ts("dve", sc[:, 0:OWN0], sc[:, 0:OWN0], self.pmt[:, 0:1], None, ALU.add, None, scres + ["pmt"], scres)
            self.tt("dve", sc[:, T * 128:(T + 1) * 128], sc[:, T * 128:(T + 1) * 128], CM, ALU.add, scres + ["cst"], scres)
            self.ts("dve", st[:, 1:2], st[:, 0:1], -1.0, -0.01, ALU.mult, ALU.add, [("C_st", 0)], [("C_st", 1)])
            self.ts("dve", st[:, 2:3], st[:, 0:1], 1.0, 0.01, ALU.mult, ALU.add, [("C_st", 0)], [("C_st", 2)])
            yield
            nh1 = (nkb // 2) * 128
            n2 = nk - nh1
            for it in range(20):
                self.tt("dve", st[:, 3:4], st[:, 1:2], st[:, 2:3], ALU.add, [("C_st", 1), ("C_st", 2)], [("C_st", 3)])
                self.ts("dve", st[:, 6:7], st[:, 3:4], -1.0, None, ALU.mult, None, [("C_st", 3)], [("C_st", 6)])
                self.s.op("act", lambda e, nh1=nh1, nk=nk: e.activation(out=selm[:, nh1:nk], in_=sc[:, nh1:nk], func=AF.Sign,
                                                                      bias=st[:, 6:7], accum_out=st[:, 7:8]),
                          scres + [("C_st", 6)], ["C_selm_b", ("C_st", 7)], noattach=True)
                self.ts("dve", selm[:, 0:nh1], sc[:, 0:nh1], st[:, 3:4], None, ALU.is_ge, ALU.add, scres + [("C_st", 3)],
                        ["C_selm", ("C_st", 4)], accum_out=st[:, 4:5])
                self.stt(st[:, 4:5], st[:, 7:8], 0.5, st[:, 4:5], ALU.mult, ALU.add, [("C_st", 7), ("C_st", 4)], [("C_st", 4)])
                self.stt(st[:, 5:6], st[:, 4:5], TOPK - 0.5 - 0.5 * n2, st[:, 2:3], ALU.is_ge, ALU.mult, [("C_st", 4), ("C_st", 2)], [("C_st", 5)])
                self.tt("dve", st[:, 1:2], st[:, 1:2], st[:, 5:6], ALU.add, [("C_st", 1), ("C_st", 5)], [("C_st", 1)])
                self.ts("dve", st[:, 2:3], st[:, 2:3], 0.5, None, ALU.mult, None, [("C_st", 2), ("C_st", 5), ("C_st", 3)], [("C_st", 2)])
                yield
            self.ts("dve", selm[:, 0:nk], sc[:, 0:nk], st[:, 1:2], None, ALU.is_ge, None, scres + [("C_st", 1)], ["C_selm", "C_selm_b"])
            yield
            for k8 in range(0, nkb, 8):
                n8 = min(8, nkb - k8)
                b = PMISC
                pT = self.ps[b][:].bitcast(BF16)
                self.tgroup([(pT[:, j * 128:(j + 1) * 128], selm[:, (k8 + j) * 128:(k8 + j + 1) * 128], self.identb[:]) for j in range(n8)],
                            ["C_selm", "identb"], [("ps", b)])
                self.ts("dve", negm[p2][:, k8:k8 + n8, :], pT[:, 0:n8 * 128].rearrange("p (c n) -> p c n", n=128), -1.0, 30000.0,
                        ALU.add, ALU.mult, [("ps", b)], [("C_negm", p2, k8)])
                yield

        def attn_gen(qt):
            T = 16 + qt
            nkb = T + 1
            p2 = qt % 2
            q0 = qt * 128
            mres = [("C_negm", p2, k8) for k8 in range(0, nkb, 8)]

            def L(hg, kb):
                b = PLOG[kb % 2]
                pl = self.ps[b][:, 0:512].rearrange("p (h q) -> p h q", q=128)
                items = [(pl, ckT[:, cc, kb * 128:(kb + 1) * 128], qlT[p2][:, hg * 4:(hg + 1) * 4, cc, :], cc == 0, False) for cc in range(4)]
                items += [(pl[:, j, :], self.identb[:], negm[p2][:, kb, :], False, j == 3) for j in range(4)]
                self.mmgroup(items, ckres + [("C_qlT", p2), "identb"] + mres, [("ps", b)])

            for hg in range(4):
                L(hg, 0)
                for kb in range(nkb):
                    if kb + 1 < nkb:
                        L(hg, kb + 1)
                    b = PLOG[kb % 2]
                    pl = self.ps[b][:, 0:512].rearrange("p (h q) -> p h q", q=128)
                    P = Pm[kb % 3]
                    self.act(P[:], pl, AF.Exp, [("ps", b)], [("C_P", kb % 3)], scale=SCALE)
                    yield
                    Pf = P[:].rearrange("p h q -> p (h q)")
                    items = [(self.ps[PACC[cc]][:, 0:512], ckM[:, kb, cc * 128:(cc + 1) * 128], Pf, kb == 0, kb == nkb - 1) for cc in range(4)]
                    items.append((self.ps[PDEN][:, 0:512], self.onesb[:], Pf, kb == 0, kb == nkb - 1))
                    self.mmgroup(items, [("C_ckM", kb), ("C_P", kb % 3), "onesb"], [("ps", PACC[cc]) for cc in range(4)] + [("ps", PDEN)])
                    yield
                self.s.op("dve", lambda e: e.reciprocal(rden[:], self.ps[PDEN][:, 0:512]), [("ps", PDEN)], ["C_rden"])
                for cc in range(4):
                    self.tt("dve", olat[:, cc, :], self.ps[PACC[cc]][:, 0:512], rden[:], ALU.mult, [("ps", PACC[cc]), "C_rden"], [("C_olat", cc)])
                for j in range(4):
                    h = hg * 4 + j
                    b = PMISC
                    self.mmgroup([(self.ps[b][:, 0:128], wuv[:, h, cc, :], olat[:, cc, j * 128:(j + 1) * 128], cc == 0, cc == 3) for cc in range(4)],
                                 [("C_wuv", (h // 4) * 4)] + [("C_olat", cc) for cc in range(4)], [("ps", b)])
                    self.copy("act", yaT[p2][:, h, :], self.ps[b][:, 0:128], [("ps", b)], [("C_yaT", p2, h)])
                yield
            self.dma(Sx["s_yT"][0:16, :, q0:q0 + 128].rearrange("h d t -> d h t"), yaT[p2][:],
                     [("C_yaT", p2, h) for h in range(HA)], [("s_yT", 0, qt)])

        order = list(range(15, -1, -1))
        for _ in score_gen(order[0]):
            pass
        for oi, qt in enumerate(order):
            ag = attn_gen(qt)
            sg = score_gen(order[oi + 1]) if oi + 1 < 16 else None
            for _ in ag:
                if sg is not None:
                    try:
                        next(sg)
                    except StopIteration:
                        sg = None
            if sg is not None:
                for _ in sg:
                    pass
        self.phase_end()


    def phase_D(self):
        I, Sx = self.I, self.S
        sb = self.sb
        self.phase_begin()
        TG = 1024
        NT = TG // 128
        XT = sb("D_XT", [128, 32, TG], BF16)
        Wb = [sb("D_W0", [128, 8192], BF16), sb("D_W1", [128, 8192], BF16)]
        xt2 = [sb("D_xt%d" % i, [128, D], F32) for i in range(2)]
        xn = sb("D_xn", [128, D], BF16)
        ss2 = [sb("D_ss%d" % i, [128, 1], F32) for i in range(2)]
        rstd2 = [sb("D_rstd%d" % i, [128, 1], F32) for i in range(2)]
        gB = sb("D_gB", [128, D], F32)
        blk = [sb("D_blk%d" % i, [128, 512], F32) for i in range(3)]
        obl = [sb("D_obl%d" % i, [128, 512], F32) for i in range(3)]
        qx = [sb("D_qx%d" % i, [128, 512], BF16) for i in range(2)]
        Px = [sb("D_Px%d" % i, [128, 512], BF16) for i in range(2)]
        rdx = sb("D_rdx", [128, 512], F32)
        oxT = sb("D_oxT", [128, 4, TG], BF16)
        sg = [[sb("D_sg%d_%d" % (i, j), [128, 512], BF16) for j in range(2)] for i in range(2)]
        ao = [sb("D_ao%d" % i, [128, 512], BF16) for i in range(3)]
        wo = I["w_o"].rearrange("(c p) n -> p c n", p=128)
        wcq = I["w_cq"].rearrange("(c p) n -> p c n", p=128)
        wco = I["w_co"].rearrange("(c p) n -> p c n", p=128)
        wfi = I["w_ffn_in"].rearrange("(c p) n -> p c n", p=128)
        SCALE = 128.0 ** -0.5
        allxt = [("D_XT", t) for t in range(NT)]
        for og in range(NOWN // TG):
            r0 = og * TG
            for half in range(2):
                for t4 in range(0, NT, 4):
                    self.dma(XT[:, half * 16:(half + 1) * 16, t4 * 128:(t4 + 4) * 128],
                             Sx["s_yT"][half * 16:(half + 1) * 16, :, r0 + t4 * 128:r0 + (t4 + 4) * 128].rearrange("c p t -> p c t"),
                             [("s_yT", half, (r0 // 128) + t4 + t) for t in range(4)], allxt, gid=("yT", og))
            xres = allxt

            def obody(view, wres, info, ncols):
                def epi(t, b):
                    self.uid += 1
                    i3 = self.uid % 3
                    rr = r0 + t * 128
                    self.dma(blk[i3][:, 0:ncols], I["xc"][OWN0 + rr:OWN0 + rr + 128, info:info + ncols], [], [("D_blk", i3)])
                    self.tt("dve", obl[i3][:, 0:ncols], self.ps[b][:, 0:ncols], blk[i3][:, 0:ncols], ALU.add, [("ps", b), ("D_blk", i3)], [("D_obl", i3)])
                    self.dma(Sx["s_h1"][rr:rr + 128, info:info + ncols], obl[i3][:, 0:ncols], [("D_obl", i3)], [("s_h1", og, t, info)])
                self.tm_chunk(XT, xres, 32, TG, view, wres, ncols, epi)
            self.linear(32, wo, [(j * 256, 256, j * 256) for j in range(16)], Wb, obody, nxt_first=(32, wcq, 0, 256, Wb))

            self.load_bcast(gB, I["cross_norm_g"], D, "D_gB")
            for t in range(NT):
                rr = r0 + t * 128
                self.norm_transpose_tile(Sx["s_h1"][rr:rr + 128, :], gB, XT, t, 32, (xt2[t % 2], xn, ss2[t % 2], rstd2[t % 2]), "D", par=t % 2,
                                         extra_reads=[("s_h1", og, t, j * 256) for j in range(16)])

            def cqbody(view, wres, info, ncols):
                def epi(h, b, hf):
                    i2 = (h * 2 + hf) % 2
                    self.copy("act", qx[i2][:], self.ps[b][:, 0:512], [("ps", b)], [("D_qx", i2)])
                    bd = self.bank()
                    bo = self.bank()
                    for mc in range(2):
                        bl = self.bank()
                        self.mm(self.ps[bl][:, 0:512], self.kxT[:, h, mc * 128:(mc + 1) * 128], qx[i2][:], True, True,
                                [("P_kxT", h), ("D_qx", i2)], [("ps", bl)])
                        self.act(Px[mc][:], self.ps[bl][:, 0:512], AF.Exp, [("ps", bl)], [("D_Px", mc)], scale=SCALE)
                        self.mm(self.ps[bd][:, 0:512], self.onesb[:], Px[mc][:], mc == 0, mc == 1, ["onesb", ("D_Px", mc)], [("ps", bd)])
                        self.mm(self.ps[bo][:, 0:512], self.vx[:, mc, h * 128:(h + 1) * 128], Px[mc][:], mc == 0, mc == 1,
                                [("P_vx", mc, 0), ("P_vx", mc, 256), ("D_Px", mc)], [("ps", bo)])
                    self.s.op("dve", lambda e, bd=bd: e.reciprocal(rdx[:], self.ps[bd][:, 0:512]), [("ps", bd)], ["D_rdx"])
                    self.tt("dve", oxT[:, h, hf * 512:(hf + 1) * 512], self.ps[bo][:, 0:512], rdx[:], ALU.mult, [("ps", bo), "D_rdx"], [("D_oxT", h, hf)])
                self.fm_chunk(XT, xres, 32, TG, view, wres, ncols, epi, info)
            self.linear(32, wcq, [(0, 256, 0), (256, 256, 2)], Wb, cqbody, nxt_first=(4, wco, 0, 512, Wb))
            oxres = [("D_oxT", h, hf) for h in range(4) for hf in range(2)]

            def cobody(view, wres, info, ncols):
                def epi(t, b):
                    self.uid += 1
                    i3 = self.uid % 3
                    rr = r0 + t * 128
                    self.dma(blk[i3][:, 0:ncols], Sx["s_h1"][rr:rr + 128, info:info + ncols],
                             [("s_h1", og, t, info), ("s_h1", og, t, info + 256)], [("D_blk", i3)])
                    self.tt("dve", obl[i3][:, 0:ncols], self.ps[b][:, 0:ncols], blk[i3][:, 0:ncols], ALU.add, [("ps", b), ("D_blk", i3)], [("D_obl", i3)])
                    self.dma(Sx["s_h2"][rr:rr + 128, info:info + ncols], obl[i3][:, 0:ncols], [("D_obl", i3)], [("s_h2", og, t, info)])
                self.tm_chunk(oxT, oxres, 4, TG, view, wres, ncols, epi)
            self.linear(4, wco, [(j * 512, 512, j * 512) for j in range(8)], Wb, cobody, nxt_first=(32, wfi, 0, 256, Wb))

            self.load_bcast(gB, I["ffn_norm_g"], D, "D_gB")
            for t in range(NT):
                rr = r0 + t * 128
                self.norm_transpose_tile(Sx["s_h2"][rr:rr + 128, :], gB, XT, t, 32, (xt2[t % 2], xn, ss2[t % 2], rstd2[t % 2]), "D", par=t % 2,
                                         extra_reads=[("s_h2", og, t, j * 512) for j in range(8)])
            chunks = []
            for j in range(43):
                chunks.append((j * 256, 256, ("g", j)))
                chunks.append((DFF + j * 256, 256, ("u", j)))

            def fbody(view, wres, info, ncols):
                kind, j = info

                def epi(sub, b, hf):
                    if kind == "g":
                        self.act(sg[sub][hf][:], self.ps[b][:, 0:512], AF.Silu, [("ps", b)], [("D_sg", sub, hf)])
                    else:
                        self.uid += 1
                        i3 = self.uid % 3
                        self.tt("dve", ao[i3][:], self.ps[b][:, 0:512], sg[sub][hf][:], ALU.mult, [("ps", b), ("D_sg", sub, hf)], [("D_ao", i3)])
                        self.dma(Sx["s_actT"][j * 2 + sub, :, r0 + hf * 512:r0 + (hf + 1) * 512], ao[i3][:], [("D_ao", i3)],
                                 [("s_actT", j * 2 + sub, og, hf)])
                self.fm_chunk(XT, xres, 32, TG, view, wres, ncols, epi, 0)
            self.linear(32, wfi, chunks, Wb, fbody, nxt_first=(32, wo, 0, 256, Wb) if og == 0 else None)
        self.phase_end()

    def phase_E(self):
        I, Sx = self.I, self.S
        sb = self.sb
        self.phase_begin()
        TG = 1024
        KH = 43
        XT = sb("E_XT", [128, KH, TG], BF16)
        Wb = [sb("E_W0", [128, KH * 256], BF16), sb("E_W1", [128, KH * 256], BF16)]
        of = [sb("E_of%d" % i, [128, 512], F32) for i in range(2)]
        hb = [sb("E_hb%d" % i, [128, 4, 128], F32) for i in range(2)]
        h3 = [sb("E_h3%d" % i, [128, 4, 128], F32) for i in range(2)]
        IDf = self.C(0)
        for og in range(NOWN // TG):
            r0 = og * TG
            for kh in range(2):
                wfo = I["w_ffn_out"][kh * KH * 128:(kh + 1) * KH * 128, :].rearrange("(c p) n -> p c n", p=128)
                self.uid += 1
                gidx = ("EXT", self.uid)
                for k0 in range(0, KH, 8):
                    k1 = min(KH, k0 + 8)
                    self.dma(XT[:, k0:k1, :], Sx["s_actT"][kh * KH + k0:kh * KH + k1, :, r0:r0 + TG].rearrange("c p t -> p c t"),
                             [("s_actT", c, og, hf) for c in range(kh * KH + k0, kh * KH + k1) for hf in range(2)], ["E_XT"], gid=gidx)
                xres = ["E_XT"]

                def body(view, wres, info, ncols, kh=kh, r0=r0, og=og):
                    def epi(c, b, hf):
                        self.uid += 1
                        i2 = self.uid % 2
                        rr = r0 + hf * 512
                        self.copy("act", of[i2][:], self.ps[b][:, 0:512], [("ps", b)], [("E_of", i2)])
                        return lambda: etail(c, hf, i2, rr)

                    def etail(c, hf, i2, rr):
                        b2 = self.bank()
                        self.tgroup([(self.ps[b2][:, t * 128:(t + 1) * 128], of[i2][:, t * 128:(t + 1) * 128], IDf) for t in range(4)],
                                    [("E_of", i2), "cst"], [("ps", b2)])
                        if kh == 0:
                            self.dma(hb[i2][:], Sx["s_h2"][rr:rr + 512, c * 128:(c + 1) * 128].rearrange("(t p) n -> p t n", p=128),
                                     [("s_h2", og, hf * 4 + t, (c // 4) * 512) for t in range(4)], [("E_hb", i2)])
                        else:
                            self.dma(hb[i2][:], Sx["s_h3"][rr:rr + 512, c * 128:(c + 1) * 128].rearrange("(t p) n -> p t n", p=128),
                                     [("s_h3", og, hf, c)], [("E_hb", i2)])
                        self.tt("dve", h3[i2][:], self.ps[b2][:, 0:512].rearrange("p (t n) -> p t n", n=128), hb[i2][:], ALU.add,
                                [("ps", b2), ("E_hb", i2)], [("E_h3", i2)])
                        self.dma(Sx["s_h3"][rr:rr + 512, c * 128:(c + 1) * 128].rearrange("(t p) n -> p t n", p=128), h3[i2][:],
                                 [("E_h3", i2)], [("s_h3", og, hf, c)])
                    self.fm_chunk(XT, xres, KH, TG, view, wres, ncols, epi, info)
                nkh, nog = (kh + 1) % 2, og + (kh + 1) // 2
                nf = None
                if nog < NOWN // TG:
                    wfn = I["w_ffn_out"][nkh * KH * 128:(nkh + 1) * KH * 128, :].rearrange("(c p) n -> p c n", p=128)
                    nf = (KH, wfn, 0, 256, Wb)
                self.linear(KH, wfo, [(j * 256, 256, j * 2) for j in range(16)], Wb, body, nxt_first=nf)
        self.phase_end()
        self.phase_begin()
        xt = [sb("F_xt%d" % i, [128, D], F32) for i in range(2)]
        xo = [sb("F_xo%d" % i, [128, D], F32) for i in range(2)]
        jk = sb("F_jk", [128, D], BF16)
        ss = [sb("F_ss%d" % i, [128, 1], F32) for i in range(2)]
        rs = [sb("F_rs%d" % i, [128, 1], F32) for i in range(2)]
        gB = sb("F_gB", [128, D], F32)
        self.load_bcast(gB, I["final_norm_g"], D, "F_gB")
        for t in range(16):
            i2 = t % 2
            self.dma(xt[i2][:], Sx["s_h3"][t * 128:(t + 1) * 128, :], [("s_h3", t // 8, (t % 8) // 4, c) for c in range(32)], [("F_xt", i2)])
            self.stt(jk[:], xt[i2][:], 1.0, xt[i2][:], ALU.mult, ALU.mult, [("F_xt", i2)], ["F_jk", ("F_ss", i2)], accum_out=ss[i2][:])
            self.act(rs[i2][:], ss[i2][:], AF.Sqrt, [("F_ss", i2)], [("F_rs", i2)], scale=1.0 / D, bias=EPS)
            self.s.op("dve", lambda e, r=rs[i2]: e.reciprocal(r[:], r[:]), [("F_rs", i2)], [("F_rs", i2)])
            self.stt(xo[i2][:], xt[i2][:], rs[i2][:], gB[:], ALU.mult, ALU.mult, [("F_xt", i2), ("F_rs", i2), "F_gB"], [("F_xo", i2)])
            o = self.dma(self.out[t * 128:(t + 1) * 128, :], xo[i2][:], [("F_xo", i2)], [("out", t)])
            self.out_ops.append(o)
        self.phase_end()


def _consts():
    c = np.zeros((128, 8 * 128), np.float32)
    i = np.arange(128)
    c[:, 0:128] = np.eye(128)
    c[:, 128:256] = 1.0
    c[:, 256:384] = (i[:, None] <= i[None, :])
    c[:, 384:512] = (i[:, None] > i[None, :])
    c[:, 512:640] = np.where((i[:, None] < 64) & (i[None, :] >= 64), -1e30, 0.0)
    c[:, 640:768] = -1.0 * (i[:, None] > i[None, :])
    c[:, 768:896] = (i[:, None] >= i[None, :])
    return c


def build_program(stop_after="all", dbg=()):
    B = Builder(stop_after, dbg)
    B.declare()
    order = ["0", "A", "B", "C", "D", "E"]
    for ph in order:
        getattr(B, "phase_" + ph)()
        if stop_after == ph:
            break
    B.s.barrier()
    finals = list(B.s.bar_deps)
    B.s.emit(finals)
    return B


def make_in_maps(inp):
    f = lambda a: np.ascontiguousarray(np.asarray(a, dtype=np.float32))
    x = f(inp["x"])
    shared = {
        "consts": _consts(),
        "w_in": f(inp["w_in"][0]), "w_qb": f(inp["w_qb"][0]), "w_iq": f(inp["w_iq"][0]),
        "wukT": f(np.transpose(inp["w_uk"][0], (2, 0, 1))),
        "w_uv": f(inp["w_uv"][0]),
        "gcols": f(np.concatenate([np.asarray(inp["qa_norm_g"][0]).reshape(12, 128).T,
                                   np.asarray(inp["kv_norm_g"][0]).reshape(4, 128).T], axis=1)),
        "cwl": f(np.transpose(np.asarray(inp["conv_w"][0]).reshape(4, 48, 128), (2, 1, 0)).reshape(128, 192)),
        "w_o": f(inp["w_o"][0]), "w_cq": f(inp["w_cq"][0]), "w_ckv": f(inp["w_ckv"][0]), "w_co": f(inp["w_co"][0]),
        "w_ffn_in": f(inp["w_ffn_in"][0]), "w_ffn_out": f(inp["w_ffn_out"][0]),
        "attn_norm_g": f(inp["attn_norm_g"]).reshape(1, D), "a_log": f(inp["a_log"]).reshape(1, HB),
        "dt_bias": f(inp["dt_bias"]).reshape(1, HB), "delta_norm_g": f(inp["delta_norm_g"]).reshape(1, 128),
        "cross_norm_g": f(inp["cross_norm_g"]).reshape(1, D), "mem_norm_g": f(inp["mem_norm_g"]).reshape(1, D),
        "ffn_norm_g": f(inp["ffn_norm_g"]).reshape(1, D), "final_norm_g": f(inp["final_norm_g"]).reshape(1, D),
    }
    maps = []
    for c in range(8):
        b, hf = c // 2, c % 2
        xc = np.zeros((CTX, D), np.float32)
        if hf == 1:
            xc[:] = x[b]
        else:
            xc[OWN0:] = x[b, 0:NOWN]
        m = dict(shared)
        m["xc"] = xc
        m["memb"] = f(inp["mem"][b])
        m["pm"] = np.full((128, 1), 0.0 if hf == 1 else -1e30, np.float32)
        maps.append(m)
    return maps


def kernel(**inputs):
    B = build_program()
    maps = make_in_maps(inputs)
    res = run_bass_kernel_spmd(B.nc, maps, core_ids=list(range(8)))
    out = np.zeros((NB, S, D), np.float32)
    for c in range(8):
        b, hf = c // 2, c % 2
        out[b, hf * NOWN:(hf + 1) * NOWN] = res.results[c]["out"]
    return out
```
